# Optimizing a Trainium2 kernel written in Bass

```python
import math
import jax
import jax.numpy as jnp
from jax import lax
import numpy as np

D_MODEL = 1024
BATCH = 8
SEQ = 2048
DEPTH = 4
DEC_BATCH = 128
DEC_SEQ = 1
PAST_LEN = 16384
PAGE_SIZE = 128

N_META = 16
NORM_EPS = 1e-6
GLA_HEADS = 4
GLA_DK = 64
GLA_DV = 128
GLA_QK = GLA_HEADS * GLA_DK
GLA_V = GLA_HEADS * GLA_DV
GLA_GATE_RANK = 16
GLA_TAU = 16.0
GLA_CHUNK = 64
RWKV_HEADS = 8
RWKV_N = 64
RWKV_W = RWKV_HEADS * RWKV_N
RWKV_W_RANK = 64
RWKV_A_RANK = 64
RWKV_G_RANK = 128
RWKV_COLS = 3 * RWKV_W + RWKV_W_RANK + RWKV_A_RANK + RWKV_G_RANK
RWKV_GN_EPS = 64e-5
S5_GROUP = 16
S5_GROUPS = 32
S5_WIDTH = S5_GROUPS * S5_GROUP
S5_STATE = 64
IN_SIZES = (GLA_QK, GLA_QK, GLA_V, GLA_V, GLA_GATE_RANK, RWKV_COLS, S5_WIDTH, D_MODEL, D_MODEL, D_MODEL)
N_IN = sum(IN_SIZES)
RWKV_SIZES = (RWKV_W, RWKV_W, RWKV_W, RWKV_W_RANK, RWKV_A_RANK, RWKV_G_RANK)
D_FF = ((8 * D_MODEL // 3 + 255) // 256) * 256

kernel_name = 'gla_rwkv7_s5_hybrid_step'


def _split(t, sizes):
    idx = [int(i) for i in np.cumsum(sizes)[:-1]]
    return jnp.split(t, idx, axis=-1)


def _rmsnorm(x, g):
    xf = x.astype(jnp.float32)
    xf = xf * lax.rsqrt(jnp.mean(xf * xf, axis=-1, keepdims=True) + NORM_EPS)
    return (xf * g.astype(jnp.float32)).astype(x.dtype)


def _gla_chunks(q, k, v, log_a, s0, chunk):
    B, T, H, DK = q.shape
    n = T // chunk

    def to_chunks(t):
        return jnp.moveaxis(t.reshape(B, n, chunk, H, t.shape[-1]), 1, 0)

    causal = jnp.tril(jnp.ones((chunk, chunk), dtype=bool))[None, :, :, None, None]

    def step(s, inp):
        qc, kc, vc, lc = inp
        b = jnp.cumsum(lc, axis=1)
        o_inter = jnp.einsum('bihd,bhde->bihe', qc * jnp.exp(b), s)
        rel = jnp.where(causal, b[:, :, None] - b[:, None, :], -jnp.inf)
        scores = jnp.sum(qc[:, :, None] * kc[:, None] * jnp.exp(rel), axis=-1)
        o_intra = jnp.einsum('bijh,bjhe->bihe', scores, vc)
        b_last = b[:, -1]
        s_new = jnp.exp(b_last)[..., None] * s + jnp.einsum('bjhd,bjhe->bhde', kc * jnp.exp(b_last[:, None] - b), vc)
        return s_new, o_inter + o_intra

    s_fin, o = lax.scan(step, s0, (to_chunks(q), to_chunks(k), to_chunks(v), to_chunks(log_a)))
    return jnp.moveaxis(o, 0, 1).reshape(B, T, H, v.shape[-1]), s_fin


def _gla_segments(q, k, v, log_a, s0, segments):
    outs = []
    s = s0
    start = 0
    for length, chunk in segments:
        sl = slice(start, start + length)
        o, s = _gla_chunks(q[:, sl], k[:, sl], v[:, sl], log_a[:, sl], s, chunk)
        outs.append(o)
        start += length
    return jnp.concatenate(outs, axis=1), s


def _rwkv7(p, prev, s0, lp):
    B, T, _ = p.shape
    p = p.astype(jnp.float32)
    shifted = jnp.concatenate([prev.astype(jnp.float32)[:, None], p[:, :-1]], axis=1)
    xs = p + (shifted - p) * lp['rwkv_mu']
    r, k, v, zw, za, zg = _split(xs, RWKV_SIZES)
    w_log = -jax.nn.softplus(-(lp['rwkv_w0'] + jnp.tanh(zw) @ lp['rwkv_w2'])) - 0.5
    decay = jnp.exp(-jnp.exp(w_log))
    a = jax.nn.sigmoid(lp['rwkv_a0'] + za @ lp['rwkv_a2'])
    g = jax.nn.sigmoid(zg) @ lp['rwkv_g2']

    def hd(t):
        return t.reshape(B, T, RWKV_HEADS, RWKV_N)

    kk = hd(k * lp['rwkv_k_k'])
    kk = kk * lax.rsqrt(jnp.sum(kk * kk, axis=-1, keepdims=True) + 1e-12)
    k = k * (1.0 + (a - 1.0) * lp['rwkv_k_a'])
    r_h, w_h, k_h, v_h, a_h = hd(r), hd(decay), hd(k), hd(v), hd(a)

    def step(S, inp):
        rt, wt, kt, vt, kkt, at = inp
        S = (S * wt[:, :, None, :]
             - jnp.einsum('bhvk,bhk->bhv', S, kkt)[..., None] * (kkt * at)[:, :, None, :]
             + vt[..., None] * kt[:, :, None, :])
        return S, jnp.einsum('bhvk,bhk->bhv', S, rt)

    def tm(t):
        return jnp.swapaxes(t, 0, 1)

    S, y = lax.scan(step, s0.astype(jnp.float32), (tm(r_h), tm(w_h), tm(k_h), tm(v_h), tm(kk), tm(a_h)))
    y = tm(y)
    mu = jnp.mean(y, axis=-1, keepdims=True)
    var = jnp.mean((y - mu) ** 2, axis=-1, keepdims=True)
    y = ((y - mu) * lax.rsqrt(var + RWKV_GN_EPS)).reshape(B, T, RWKV_W) * lp['rwkv_ln_w'] + lp['rwkv_ln_b']
    bonus = jnp.sum(r_h * k_h * lp['rwkv_r_k'], axis=-1, keepdims=True) * v_h
    y = (y + bonus.reshape(B, T, RWKV_W)) * g
    return y, S, p[:, -1]


def _s5(u, x0_re, x0_im, lp):
    B, T, _ = u.shape
    u = u.astype(jnp.float32)
    ug = u.reshape(B, T, S5_GROUPS, S5_GROUP)
    lam_re, lam_im = lp['s5_a_re'], lp['s5_a_im']
    dt = jnp.exp(lp['s5_log_dt'])[:, None]
    mag = jnp.exp(lam_re * dt)
    abar_re = mag * jnp.cos(lam_im * dt)
    abar_im = mag * jnp.sin(lam_im * dt)
    den = lam_re * lam_re + lam_im * lam_im
    nr = abar_re - 1.0
    coef_re = (nr * lam_re + abar_im * lam_im) / den
    coef_im = (abar_im * lam_re - nr * lam_im) / den
    bbar_re = coef_re[..., None] * lp['s5_b_re'] - coef_im[..., None] * lp['s5_b_im']
    bbar_im = coef_re[..., None] * lp['s5_b_im'] + coef_im[..., None] * lp['s5_b_re']
    bu_re = jnp.einsum('btgh,gph->btgp', ug, bbar_re)
    bu_im = jnp.einsum('btgh,gph->btgp', ug, bbar_im)
    x0_re = x0_re.astype(jnp.float32)
    x0_im = x0_im.astype(jnp.float32)
    bu_re = bu_re.at[:, 0].add(abar_re * x0_re - abar_im * x0_im)
    bu_im = bu_im.at[:, 0].add(abar_re * x0_im + abar_im * x0_re)
    a_re = jnp.broadcast_to(abar_re, bu_re.shape)
    a_im = jnp.broadcast_to(abar_im, bu_im.shape)

    def combine(e1, e2):
        a1r, a1i, b1r, b1i = e1
        a2r, a2i, b2r, b2i = e2
        return (a1r * a2r - a1i * a2i, a1r * a2i + a1i * a2r,
                a2r * b1r - a2i * b1i + b2r, a2r * b1i + a2i * b1r + b2i)

    _, _, xr, xi = lax.associative_scan(combine, (a_re, a_im, bu_re, bu_im), axis=1)
    y = jnp.einsum('btgp,ghp->btgh', xr, lp['s5_c_re']) - jnp.einsum('btgp,ghp->btgh', xi, lp['s5_c_im'])
    y = y.reshape(B, T, S5_WIDTH) + lp['s5_d'] * u
    y = jax.nn.gelu(y)
    y = y * jax.nn.sigmoid(y @ lp['s5_glu_w'] + lp['s5_glu_b'])
    return y, xr[:, -1], xi[:, -1]


def _layer(x, st, lp, segments):
    gla_s0, rwkv_s0, shift0, s5r0, s5i0 = st
    B, T, _ = x.shape
    dt = x.dtype
    f32 = jnp.float32
    h = _rmsnorm(x, lp['norm_mix'])
    proj = h @ lp['w_in']
    q, k, v, g_gla, z_gate, p_rwkv, u_s5, gate_a, gate_b, gate_c = _split(proj, IN_SIZES)
    log_a = jax.nn.log_sigmoid((z_gate @ lp['gla_gate_w2'] + lp['gla_gate_b']).astype(f32)) / GLA_TAU
    qh = q.astype(f32).reshape(B, T, GLA_HEADS, GLA_DK) * (GLA_DK ** -0.5)
    kh = k.astype(f32).reshape(B, T, GLA_HEADS, GLA_DK)
    vh = v.astype(f32).reshape(B, T, GLA_HEADS, GLA_DV)
    o, gla_s = _gla_segments(qh, kh, vh, log_a.reshape(B, T, GLA_HEADS, GLA_DK), gla_s0.astype(f32), segments)
    o = o * lax.rsqrt(jnp.mean(o * o, axis=-1, keepdims=True) + NORM_EPS)
    o_gla = o.reshape(B, T, GLA_V) * lp['gla_norm'] * jax.nn.silu(g_gla.astype(f32))
    o_rwkv, rwkv_s, shift = _rwkv7(p_rwkv, shift0, rwkv_s0, lp)
    o_s5, s5r, s5i = _s5(u_s5, s5r0, s5i0, lp)
    m = (jax.nn.sigmoid(gate_a) * (o_gla.astype(dt) @ lp['w_br_gla'])
         + jax.nn.sigmoid(gate_b) * (o_rwkv.astype(dt) @ lp['w_br_rwkv'])
         + jax.nn.sigmoid(gate_c) * (o_s5.astype(dt) @ lp['w_br_s5']))
    x = x + m @ lp['w_out']
    h2 = _rmsnorm(x, lp['norm_ffn'])
    x = x + (jax.nn.silu(h2 @ lp['ffn_w1']) * (h2 @ lp['ffn_w3'])) @ lp['ffn_w2']
    return x, (gla_s, rwkv_s, shift, s5r, s5i)


def setup_inputs(seed: int = 0) -> dict:
    key = jax.random.key(seed)
    ks = iter(jax.random.split(key, 64))
    f32 = jnp.float32

    def nrm(shape, scale):
        return jax.random.normal(next(ks), shape, f32) * scale

    def gain(shape):
        return 1.0 + nrm(shape, 0.02)

    L = DEPTH
    a_im0 = math.pi * jnp.arange(S5_STATE, dtype=f32)
    return {
        'x_prompt': nrm((BATCH, SEQ, D_MODEL), 1.0),
        'x_sample': nrm((DEC_BATCH, DEC_SEQ, D_MODEL), 1.0),
        'state_gla': nrm((L, DEC_BATCH, GLA_HEADS, GLA_DK, GLA_DV), 0.5),
        'state_rwkv': nrm((L, DEC_BATCH, RWKV_HEADS, RWKV_N, RWKV_N), 0.3),
        'state_rwkv_shift': nrm((L, DEC_BATCH, RWKV_COLS), 1.0),
        'state_s5_re': nrm((L, DEC_BATCH, S5_GROUPS, S5_STATE), 0.5),
        'state_s5_im': nrm((L, DEC_BATCH, S5_GROUPS, S5_STATE), 0.5),
        'meta_tokens': nrm((N_META, D_MODEL), 1.0),
        'norm_mix': gain((L, D_MODEL)),
        'norm_ffn': gain((L, D_MODEL)),
        'w_in': nrm((L, D_MODEL, N_IN), D_MODEL ** -0.5),
        'gla_gate_w2': nrm((L, GLA_GATE_RANK, GLA_QK), GLA_GATE_RANK ** -0.5),
        'gla_gate_b': nrm((L, GLA_QK), 0.1),
        'gla_norm': gain((L, GLA_V)),
        'rwkv_mu': jax.random.uniform(next(ks), (L, RWKV_COLS), f32),
        'rwkv_w0': -2.0 + nrm((L, RWKV_W), 0.5),
        'rwkv_w2': nrm((L, RWKV_W_RANK, RWKV_W), 0.1 * RWKV_W_RANK ** -0.5),
        'rwkv_a0': nrm((L, RWKV_W), 0.1),
        'rwkv_a2': nrm((L, RWKV_A_RANK, RWKV_W), 0.1 * RWKV_A_RANK ** -0.5),
        'rwkv_g2': nrm((L, RWKV_G_RANK, RWKV_W), RWKV_G_RANK ** -0.5),
        'rwkv_k_k': 0.85 + nrm((L, RWKV_W), 0.02),
        'rwkv_k_a': gain((L, RWKV_W)),
        'rwkv_r_k': nrm((L, RWKV_HEADS, RWKV_N), 0.1),
        'rwkv_ln_w': gain((L, RWKV_W)),
        'rwkv_ln_b': nrm((L, RWKV_W), 0.02),
        's5_a_re': -0.5 * jnp.exp(nrm((L, S5_GROUPS, S5_STATE), 0.05)),
        's5_a_im': a_im0 + nrm((L, S5_GROUPS, S5_STATE), 0.01),
        's5_log_dt': jax.random.uniform(next(ks), (L, S5_GROUPS), f32, math.log(1e-3), math.log(1e-1)),
        's5_b_re': nrm((L, S5_GROUPS, S5_STATE, S5_GROUP), (2 * S5_GROUP) ** -0.5),
        's5_b_im': nrm((L, S5_GROUPS, S5_STATE, S5_GROUP), (2 * S5_GROUP) ** -0.5),
        's5_c_re': nrm((L, S5_GROUPS, S5_GROUP, S5_STATE), (2 * S5_STATE) ** -0.5),
        's5_c_im': nrm((L, S5_GROUPS, S5_GROUP, S5_STATE), (2 * S5_STATE) ** -0.5),
        's5_d': nrm((L, S5_WIDTH), 0.5),
        's5_glu_w': nrm((L, S5_WIDTH, S5_WIDTH), S5_WIDTH ** -0.5),
        's5_glu_b': nrm((L, S5_WIDTH), 0.02),
        'w_br_gla': nrm((L, GLA_V, D_MODEL), GLA_V ** -0.5),
        'w_br_rwkv': nrm((L, RWKV_W, D_MODEL), RWKV_W ** -0.5),
        'w_br_s5': nrm((L, S5_WIDTH, D_MODEL), S5_WIDTH ** -0.5),
        'w_out': nrm((L, D_MODEL, D_MODEL), D_MODEL ** -0.5),
        'ffn_w1': nrm((L, D_MODEL, D_FF), D_MODEL ** -0.5),
        'ffn_w3': nrm((L, D_MODEL, D_FF), D_MODEL ** -0.5),
        'ffn_w2': nrm((L, D_FF, D_MODEL), D_FF ** -0.5),
        'final_norm': gain((D_MODEL,)),
    }


def reference(x_prompt, x_sample, state_gla, state_rwkv, state_rwkv_shift, state_s5_re, state_s5_im,
              meta_tokens, norm_mix, norm_ffn, w_in, gla_gate_w2, gla_gate_b, gla_norm,
              rwkv_mu, rwkv_w0, rwkv_w2, rwkv_a0, rwkv_a2, rwkv_g2, rwkv_k_k, rwkv_k_a, rwkv_r_k,
              rwkv_ln_w, rwkv_ln_b, s5_a_re, s5_a_im, s5_log_dt, s5_b_re, s5_b_im, s5_c_re, s5_c_im,
              s5_d, s5_glu_w, s5_glu_b, w_br_gla, w_br_rwkv, w_br_s5, w_out, ffn_w1, ffn_w3, ffn_w2,
              final_norm):
    f32 = jnp.float32
    Bp, Tp, _ = x_prompt.shape
    Ts = x_sample.shape[1]
    meta = jnp.broadcast_to(meta_tokens.astype(x_prompt.dtype)[None], (Bp, N_META, D_MODEL))
    xp = jnp.concatenate([meta, x_prompt], axis=1)
    xs = x_sample
    seg_p = ((N_META, N_META), (Tp, GLA_CHUNK))
    seg_s = ((Ts, Ts),)
    zero_p = (jnp.zeros((Bp, GLA_HEADS, GLA_DK, GLA_DV), f32),
              jnp.zeros((Bp, RWKV_HEADS, RWKV_N, RWKV_N), f32),
              jnp.zeros((Bp, RWKV_COLS), f32),
              jnp.zeros((Bp, S5_GROUPS, S5_STATE), f32),
              jnp.zeros((Bp, S5_GROUPS, S5_STATE), f32))
    new_p = [[], [], [], [], []]
    new_s = [[], [], [], [], []]
    for l in range(DEPTH):
        lp = {
            'norm_mix': norm_mix[l], 'norm_ffn': norm_ffn[l], 'w_in': w_in[l],
            'gla_gate_w2': gla_gate_w2[l], 'gla_gate_b': gla_gate_b[l], 'gla_norm': gla_norm[l],
            'rwkv_mu': rwkv_mu[l], 'rwkv_w0': rwkv_w0[l], 'rwkv_w2': rwkv_w2[l], 'rwkv_a0': rwkv_a0[l],
            'rwkv_a2': rwkv_a2[l], 'rwkv_g2': rwkv_g2[l], 'rwkv_k_k': rwkv_k_k[l], 'rwkv_k_a': rwkv_k_a[l],
            'rwkv_r_k': rwkv_r_k[l], 'rwkv_ln_w': rwkv_ln_w[l], 'rwkv_ln_b': rwkv_ln_b[l],
            's5_a_re': s5_a_re[l], 's5_a_im': s5_a_im[l], 's5_log_dt': s5_log_dt[l],
            's5_b_re': s5_b_re[l], 's5_b_im': s5_b_im[l], 's5_c_re': s5_c_re[l], 's5_c_im': s5_c_im[l],
            's5_d': s5_d[l], 's5_glu_w': s5_glu_w[l], 's5_glu_b': s5_glu_b[l],
            'w_br_gla': w_br_gla[l], 'w_br_rwkv': w_br_rwkv[l], 'w_br_s5': w_br_s5[l], 'w_out': w_out[l],
            'ffn_w1': ffn_w1[l], 'ffn_w3': ffn_w3[l], 'ffn_w2': ffn_w2[l],
        }
        xp, stp = _layer(xp, zero_p, lp, seg_p)
        st_in = (state_gla[l], state_rwkv[l], state_rwkv_shift[l], state_s5_re[l], state_s5_im[l])
        xs, sts = _layer(xs, st_in, lp, seg_s)
        for i in range(5):
            new_p[i].append(stp[i])
            new_s[i].append(sts[i])
    y_prompt = _rmsnorm(xp, final_norm)[:, N_META:]
    y_sample = _rmsnorm(xs, final_norm)
    gla_p, rwkv_p, shift_p, s5re_p, s5im_p = [jnp.stack(t, axis=0) for t in new_p]
    gla_s, rwkv_s, shift_s, s5re_s, s5im_s = [jnp.stack(t, axis=0) for t in new_s]
    return (y_prompt, y_sample, gla_p, rwkv_p, shift_p, s5re_p, s5im_p, gla_s, rwkv_s, shift_s, s5re_s, s5im_s)
```

```python
import contextlib
import numpy as np
import concourse.bass as bass
import concourse.mybir as mybir
from concourse.bass_utils import run_bass_kernel_spmd
PH = set('RSC')
MARK = True
FP32R = False

F32 = mybir.dt.float32
BF16 = mybir.dt.bfloat16
AF = mybir.ActivationFunctionType
ALU = mybir.AluOpType

EPOCH = 30000
DEPOCH = 1500


class Sched:
    CE = ('pe', 'act', 'dve', 'pool')

    def __init__(self, nc, stack, n_dma_slots=12, n_bg=10, n_pf=4):
        self.nc = nc
        self.stack = stack
        self.eng = {'pe': nc.tensor, 'act': nc.scalar, 'dve': nc.vector,
                    'pool': nc.gpsimd, 'sp': nc.sync}
        self.q = {e: [] for e in self.eng}
        self.cnt = {e: 0 for e in self.CE}
        self.sems = {}
        self.seen = {e: {} for e in self.eng}
        self.last_w = {}
        self.readers = {}
        self.nslots = n_dma_slots
        self.slot_sems = {}
        self.slot_cnt = [0] * (n_dma_slots + n_bg)
        self.bgslots = list(range(n_dma_slots, n_dma_slots + n_bg))
        self.bgnext = 0
        self.slot_cnt += [0] * n_pf
        self.pfslots = list(range(n_dma_slots + n_bg, n_dma_slots + n_bg + n_pf))
        self.pfnext = 0
        self.qslots = {'sp': list(range(0, 6)), 'act': list(range(6, 8)), 'pool': list(range(8, n_dma_slots))}
        self.qnext = {'sp': 0, 'act': 0, 'pool': 0}
        self.out_dmas = []
        self.n_instr = 0

    def _sem(self, e, epoch):
        k = (e, epoch)
        if k not in self.sems:
            self.sems[k] = self.stack.enter_context(self.nc.semaphore(f"p_{e}_{epoch}"))
        return self.sems[k]

    def _slot_sem(self, s, ep):
        k = (s, ep)
        if k not in self.slot_sems:
            self.slot_sems[k] = self.stack.enter_context(self.nc.semaphore(f"dslot{s}_{ep}"))
        return self.slot_sems[k]

    def _wait(self, eng, src, c):
        if self.seen[eng].get(src, 0) >= c:
            return
        self.seen[eng][src] = c
        e = self.eng[eng]
        if isinstance(src, tuple):
            sem = self._slot_sem(src[1], (c - 1) // DEPOCH)
            val = ((c - 1) % DEPOCH + 1) * 16
        else:
            idx = c - 1
            sem = self._sem(src, idx // EPOCH)
            val = (idx % EPOCH) + 1
        self.q[eng].append(lambda e=e, sem=sem, val=val: e.wait_ge(sem, val))
        self.n_instr += 1

    def _deps(self, reads, writes):
        deps = set()
        for k in reads:
            w = self.last_w.get(k)
            if w:
                deps.add(w)
        for k in writes:
            w = self.last_w.get(k)
            if w:
                deps.add(w)
            for r in self.readers.get(k, ()):
                deps.add(r)
        return deps

    def _commit(self, me, reads, writes):
        for k in writes:
            self.last_w[k] = me
            self.readers[k] = []
        for k in reads:
            self.readers.setdefault(k, []).append(me)

    @staticmethod
    def _is_psum(k):
        return k in ('pn', 'ptb', 'py', 'pst') or (isinstance(k, tuple) and k[0] in ('pp', 'pc'))

    def op(self, eng, fn, reads=(), writes=(), skip_same=False):
        pr = [k for k in reads if self._is_psum(k)]
        if pr:
            writes = list(writes) + pr
            reads = [k for k in reads if not self._is_psum(k)]
        for (src, c) in sorted(self._deps(reads, writes), key=str):
            if skip_same and src == eng:
                continue
            self._wait(eng, src, c)
        self.cnt[eng] += 1
        c = self.cnt[eng]
        idx = c - 1
        sem = self._sem(eng, idx // EPOCH)
        self.q[eng].append(lambda fn=fn, sem=sem: fn().then_inc(sem, 1))
        self.n_instr += 1
        self._commit((eng, c), reads, writes)

    def dma(self, qeng, out, in_, reads=(), writes=(), is_output=False, bg=False, **kw):
        if bg == 'p':
            s = self.pfslots[self.pfnext % len(self.pfslots)]
            self.pfnext += 1
        elif bg:
            s = self.bgslots[self.bgnext % len(self.bgslots)]
            self.bgnext += 1
        else:
            sl = self.qslots[qeng]
            s = sl[self.qnext[qeng] % len(sl)]
            self.qnext[qeng] += 1
        if self.slot_cnt[s] > 0:
            self._wait(qeng, ('d', s), self.slot_cnt[s])
        for (src, c) in sorted(self._deps(reads, writes), key=str):
            self._wait(qeng, src, c)
        self.slot_cnt[s] += 1
        c = self.slot_cnt[s]
        e = self.eng[qeng]
        sem = self._slot_sem(s, (c - 1) // DEPOCH)
        self.q[qeng].append(lambda e=e, out=out, in_=in_, sem=sem, kw=kw: e.dma_start(out=out, in_=in_, **kw).then_inc(sem, 16))
        self.n_instr += 1
        me = (('d', s), c)
        self._commit(me, reads, writes)
        if is_output:
            self.out_dmas.append(me)

    def finish(self, block):
        for (src, c) in self.out_dmas:
            self.seen['sp'].pop(src, None) if self.seen['sp'].get(src, 0) < c else None
            self._wait('sp', src, c)
        q = self.q

        if q['sp']:
            @block.sync
            def _(e):
                for f in q['sp']:
                    f()
        if q['pe']:
            @block.tensor
            def _(e):
                for f in q['pe']:
                    f()
        if q['act']:
            @block.scalar
            def _(e):
                for f in q['act']:
                    f()
        if q['dve']:
            @block.vector
            def _(e):
                for f in q['dve']:
                    f()
        if q['pool']:
            @block.gpsimd
            def _(e):
                for f in q['pool']:
                    f()


def _barrier(self):
    for e in self.eng:
        for src in self.CE:
            if self.cnt[src] > 0:
                self._wait(e, src, self.cnt[src])
        for qn, slots in self.qslots.items():
            for s in slots:
                if self.slot_cnt[s] > 0:
                    self._wait(e, ('d', s), self.slot_cnt[s])


Sched.barrier = _barrier


NT = 2080
NPR = 2064
NS = 16
BLK = 256
NTOKMAX = 288
D = 1024
NIN = 6928
DFF = 2816

PVO = {}
_o = 0
for _n, _w in [('norm_mix', 8), ('norm_ffn', 8), ('gla_gate_b', 2), ('gla_norm', 4), ('rwkv_mu', 14),
               ('rwkv_w0', 4), ('rwkv_a0', 4), ('rwkv_k_k', 4), ('rwkv_k_a', 4), ('rwkv_r_k', 4),
               ('rwkv_ln_w', 4), ('rwkv_ln_b', 4), ('s5_d', 4), ('s5_glu_b', 4), ('s5_are', 16),
               ('s5_aim', 16), ('s5_ldt', 16), ('final_norm', 8)]:
    PVO[_n] = _o
    _o += _w
NV = _o
CO = {n: i * 128 for i, n in enumerate(['MU', 'MUI', 'MLn', 'MUn', 'ident', 'ones', 'bones'])}
NCONST = 7 * 128

BIGW = [('w_in', D, NIN), ('w_br_gla', 512, D), ('w_br_rwkv', 512, D), ('w_br_s5', 512, D), ('w_out', D, D),
        ('ffn_w1', D, DFF), ('ffn_w3', D, DFF), ('ffn_w2', DFF, D)]


def build_nc(depth=4, nblk=None):
    nc = bass.Bass("TRN2", target_bir_lowering=False)
    din = {}

    def I(name, shape, dt=F32):
        din[name] = nc.dram_tensor(name, shape, dt, kind="ExternalInput").ap()
        return din[name]

    def O(name, shape):
        return nc.dram_tensor(name, shape, F32, kind="ExternalOutput").ap()

    def SC(name, shape, dt):
        return nc.dram_tensor(name, shape, dt, kind="Internal").ap()

    xin = I("xin", [NT, D])
    sgla = I("sgla", [4, NS, 4, 64, 128])
    srw = I("srw", [4, NS, 8, 64, 64])
    ssh = I("ssh", [4, NS, 1792])
    s5r = I("s5r", [4, NS, 2048])
    s5i = I("s5i", [4, NS, 2048])
    pv = I("pv", [4, 128, NV])
    consts = I("consts", [128, NCONST])
    gw2 = I("gla_gate_w2", [4, 16, 256])
    rw2 = I("rwkv_w2", [4, 64, 512])
    ra2 = I("rwkv_a2", [4, 64, 512])
    rg2 = I("rwkv_g2", [4, 128, 512])
    gluw = I("s5_glu_w", [4, 512, 512])
    bpr = I("bpad_re", [4, 128, 16, 128])
    bpi = I("bpad_im", [4, 128, 16, 128])
    cpr = I("cpad_re", [4, 128, 16, 128])
    cpi = I("cpad_im", [4, 128, 16, 128])
    wf = {n: I(n, [4, r, c]) for n, r, c in BIGW}
    wb = {n: SC(n + "_b", [4, r, c], BF16) for n, r, c in BIGW}
    xres = SC("xres", [128, 8, NT], F32)
    tabd = SC("tabd", [128, 4, 16, 128], F32)

    yp = O("yp", [2048, D])
    ys = O("ys", [NS, D])
    glap = O("glap", [4, 4, 64, 128])
    rwp = O("rwp", [4, 8, 64, 64])
    shp = O("shp", [4, 14, 128])
    s5rp = O("s5rp", [4, 16, 128])
    s5ip = O("s5ip", [4, 16, 128])
    glas = O("glas", [4, NS, 4, 64, 128])
    rws = O("rws", [4, NS, 8, 64, 64])
    shs = O("shs", [4, NS, 1792])
    s5rs = O("s5rs", [4, NS, 2048])
    s5is = O("s5is", [4, NS, 2048])

    st = contextlib.ExitStack()
    with st:
        S = Sched(nc, st)
        T = lambda name, shape, dt=F32: st.enter_context(nc.sbuf_tensor(name, shape, dt))
        PS = lambda name, shape, dt=F32: st.enter_context(nc.psum_tensor(name, shape, dt))

        def ACT(out, in_, func, R, W, bias=None, scale=None):
            kw = {}
            if bias is not None:
                kw['bias'] = bias
            if scale is not None:
                kw['scale'] = scale
            S.op('act', lambda: nc.scalar.activation(out=out, in_=in_, func=func, **kw), R, W)

        def EN(eng):
            return nc.vector if eng == 'dve' else nc.gpsimd

        def TT(eng, out, a, b, op, R, W):
            S.op(eng, lambda: EN(eng).tensor_tensor(out=out, in0=a, in1=b, op=op), R, W)

        def TS(eng, out, a, s1, op0, R, W, s2=None, op1=None):
            if op1 is None:
                S.op(eng, lambda: EN(eng).tensor_scalar(out=out, in0=a, scalar1=s1, scalar2=None, op0=op0), R, W)
            else:
                S.op(eng, lambda: EN(eng).tensor_scalar(out=out, in0=a, scalar1=s1, scalar2=s2, op0=op0, op1=op1), R, W)

        def STT(out, a, s, b, op0, op1, R, W):
            S.op('dve', lambda: nc.vector.scalar_tensor_tensor(out=out, in0=a, scalar=s, in1=b, op0=op0, op1=op1), R, W)

        def CP(eng, out, in_, R, W):
            if eng == 'act':
                S.op('act', lambda: nc.scalar.copy(out=out, in_=in_), R, W)
            else:
                S.op(eng, lambda: EN(eng).tensor_copy(out=out, in_=in_), R, W)

        def MM(out, lhsT, rhs, R, W, start=True, stop=True, chain=False):
            S.op('pe', lambda: nc.tensor.matmul(out, lhsT=lhsT, rhs=rhs, start=start, stop=stop), R, W, skip_same=chain)

        def TR(out, in_, ident, R, W):
            S.op('pe', lambda: nc.tensor.transpose(out=out, in_=in_, identity=ident), R, W)

        def SCAN(out, d0, d1, init, R, W):
            S.op('dve', lambda: nc.vector.tensor_tensor_scan(out=out, data0=d0, data1=d1, initial=init, op0=ALU.mult, op1=ALU.add), R, W)

        def RECIP(out, in_, R, W):
            S.op('dve', lambda: nc.vector.reciprocal(out=out, in_=in_), R, W)

        def MSET(eng, ap, v, W):
            S.op(eng, lambda: EN(eng).memset(ap, v), (), W)

        M, A, SUB = ALU.mult, ALU.add, ALU.subtract

        def RR(ap):
            return ap.bitcast(mybir.dt.float32r) if FP32R else ap

        cst = T("cst", [128, NCONST])
        cstb = T("cstb", [128, 128], BF16)
        pvt = T("pvt", [128, NV])
        pvd = T("pvd", [128, 8])
        gw2b = T("gw2b", [128, 256], BF16)
        w2b = T("w2b", [128, 512], BF16)
        a2b = T("a2b", [128, 512], BF16)
        g2b = T("g2b", [128, 512], BF16)
        glub = T("glub", [128, 4, 512], BF16)
        xb = T("xb", [128, 8, NTOKMAX])
        hb = T("hb", [128, 8, NTOKMAX], BF16)
        og = T("og", [128, 3, 4, NTOKMAX], BF16)
        Sg_fs = [T(f"Sg_f{i}", [128, 2, 128]) for i in range(3)]
        Sg_bs = [T(f"Sg_b{i}", [128, 2, 128], BF16) for i in range(3)]
        Sr_fs = [T(f"Sr_f{i}", [128, 4, 64]) for i in range(3)]
        Sr_bs = [T(f"Sr_b{i}", [128, 4, 64], BF16) for i in range(3)]
        x5 = T("x5", [128, 16, 2])
        shprev = T("shprev", [128, 14, 1])
        NWB = 4
        wbuf = [T(f"wbuf{i}", [128, 8, 512], BF16) for i in range(NWB)]
        RF = T("RF", [128, 26000])
        RB = T("RB", [128, 14200], BF16)
        pp = [PS(f"pp{i}", [128, 512]) for i in range(2)]
        pn = PS("pn", [128, 512])
        ptb = PS("ptb", [128, 1024], BF16)
        pc = [PS(f"pc{i}", [128, 512]) for i in range(2)]
        py = PS("py", [128, 512])
        pst = PS("pst", [128, 512])
        block = st.enter_context(nc.Block())

        MU = cst[:, CO['MU']:CO['MU'] + 128]
        MUI = cst[:, CO['MUI']:CO['MUI'] + 128]
        MLn = cst[:, CO['MLn']:CO['MLn'] + 128]
        MUn = cst[:, CO['MUn']:CO['MUn'] + 128]
        identf = cst[:, CO['ident']:CO['ident'] + 128]
        onesf = cst[:, CO['ones']:CO['ones'] + 128]
        bonesf = cst[:, CO['bones']:CO['bones'] + 128]
        CK = ['cst']

        al = {'f': 0, 'b': 0, 'n': 0}

        mark = T("mark", [128, 2])

        def phase(soft=False):
            if not soft:
                S.barrier()
                al['f'] = 0
                al['b'] = 0
            if MARK:
                S.op('pool', lambda: nc.gpsimd.memset(mark[:, 0:1], 1.0), (), ['mark'])

        def FA(c, t):
            o = al['f']
            al['f'] += c * t
            assert al['f'] <= 25900, al['f']
            al['n'] += 1
            return RF[:, o:o + c * t].rearrange("p (c t) -> p c t", c=c), f"F{al['n']}"

        def BA(c, t):
            o = al['b']
            al['b'] += c * t
            assert al['b'] <= 14200, al['b']
            al['n'] += 1
            return RB[:, o:o + c * t].rearrange("p (c t) -> p c t", c=c), f"B{al['n']}"

        wt = {'i': 0, 'j': 0, 'p': 0, 'c': 0}

        WROWS = {n: r for n, r, c in BIGW}

        def WK(name, l):
            return [('wb', name, l, r0) for r0 in range(0, WROWS[name], 128)]

        pref = {}

        def prefetch_w(name, l, c0, nk, ncols):
            i = wt['i']
            wt['i'] = (i + 1) % NWB
            src = wb[name][l, :, c0:c0 + ncols]
            S.dma('sp', wbuf[i][:, 0:nk, 0:ncols], src.rearrange("(kc p) n -> p kc n", p=128), reads=WK(name, l), writes=[('wbuf', i)], bg='p')
            pref[(name, l, c0, ncols)] = i

        def load_w(name, l, c0, nk, ncols):
            if (name, l, c0, ncols) in pref:
                return pref.pop((name, l, c0, ncols))
            i = wt['i']
            wt['i'] = (i + 1) % NWB
            src = wb[name][l, :, c0:c0 + ncols]
            S.dma('sp', wbuf[i][:, 0:nk, 0:ncols], src.rearrange("(kc p) n -> p kc n", p=128), reads=WK(name, l), writes=[('wbuf', i)])
            return i

        def convert_list(l):
            return [(n, l, r0) for n, r, c in BIGW for r0 in range(0, r, 128)]

        def issue_convert(items):
            for (n, l, r0) in items:
                S.dma('pool', wb[n][l, r0:r0 + 128, :], wf[n][l, r0:r0 + 128, :], writes=[('wb', n, l, r0)], bg=True)

        def nextpp():
            wt['p'] ^= 1
            return wt['p']

        def nextpc():
            wt['c'] ^= 1
            return wt['c']

        def proj(wi, nk, c0, m, rhs, rk, n, psb, prow=None):
            for kc in range(nk):
                MM(pp[psb][0:m, 0:n], wbuf[wi][:, kc, c0:c0 + m], rhs[:, kc, 0:n], [('wbuf', wi)] + rk, [('pp', psb)],
                   start=(kc == 0), stop=(kc == nk - 1), chain=True)

        S.dma('sp', cst[:], consts, writes=CK)
        CP('dve', cstb[:], identf, CK, ['cstb'])
        issue_convert(convert_list(0))
        tiles_all = [(t0, 128) for t0 in range(0, 2048, 128)] + [(2048, 32)]
        for (t0, L) in tiles_all:
            xt, xk = RF[:, 0:1024], 'xt0'
            S.dma('sp', xt[0:L, :], xin[t0:t0 + L, :], writes=[xk])
            for c in range(8):
                b = nextpp()
                TR(pp[b][:, 0:L], xt[0:L, c * 128:(c + 1) * 128], identf[0:L, 0:L], [xk] + CK, [('pp', b)])
                CP('act' if c % 2 else 'dve', xb[:, c, 0:L], pp[b][:, 0:L], [('pp', b)], ['xb'])
            S.dma('sp', xres[:, :, t0:t0 + L], xb[:, :, 0:L], reads=['xb'], writes=['xres'])
        S.barrier()

        def rmsnorm(goff, n):
            sq, sk = FA(2, NTOKMAX)
            rs, rk = FA(1, NTOKMAX)
            for c in range(8):
                ACT(sq[:, c % 2, 0:n], xb[:, c, 0:n], AF.Square, ['xb'], [(sk, c % 2)])
                MM(pn[:, 0:n], onesf, sq[:, c % 2, 0:n], [(sk, c % 2)] + CK, ['pn'], start=(c == 0), stop=(c == 7), chain=True)
            ACT(rs[:, 0, 0:n], pn[:, 0:n], AF.Sqrt, ['pn'], [rk], bias=1e-6, scale=1.0 / 1024)
            RECIP(rs[:, 0, 0:n], rs[:, 0, 0:n], [rk], [rk])
            return rs, rk

        nb_all = 8 if nblk is None else nblk
        for l in range(depth):
            phase()
            S.dma('sp', pvt[:], pv[l], writes=['pvt'])
            S.dma('pool', gw2b[0:16, :], gw2[l], writes=['gw2b'])
            S.dma('pool', w2b[0:64, :], rw2[l], writes=['w2b'])
            S.dma('pool', a2b[64:128, :], ra2[l], writes=['a2b'])
            S.dma('pool', g2b[:], rg2[l], writes=['g2b'])
            S.dma('pool', glub[:], gluw[l].rearrange("(kc p) n -> p kc n", p=128), writes=['glub'])
            PK = ['pvt', 'pvd']
            TS('dve', pvd[:, 0:2], pvt[:, PVO['gla_gate_b']:PVO['gla_gate_b'] + 2], -1.0, M, ['pvt'], ['pvd'])
            TS('dve', pvd[:, 2:6], pvt[:, PVO['rwkv_k_a']:PVO['rwkv_k_a'] + 4], -1.0, M, ['pvt'], ['pvd'], s2=1.0, op1=A)
            c5, c5k = FA(16, 16)
            K5 = [c5k]
            are = pvt[:, PVO['s5_are']:PVO['s5_are'] + 16]
            aim = pvt[:, PVO['s5_aim']:PVO['s5_aim'] + 16]
            ldt = pvt[:, PVO['s5_ldt']:PVO['s5_ldt'] + 16]
            dt_, mag, th, cs, sn, abr, abi, den, nr, cre, cim, t1, t2, iar, iai = [c5[:, i, :] for i in range(15)]
            ACT(dt_, ldt, AF.Exp, ['pvt'], K5)
            TT('dve', mag, are, dt_, M, ['pvt'] + K5, K5)
            ACT(mag, mag, AF.Exp, K5, K5)
            TT('dve', th, aim, dt_, M, ['pvt'] + K5, K5)
            pi = float(np.pi)
            ACT(sn, th, AF.Sin, K5, K5, scale=1.0 / 32)
            ACT(cs, th, AF.Sin, K5, K5, scale=1.0 / 32, bias=pi / 2)
            for _ in range(5):
                TT('dve', t1, sn, cs, M, K5, K5)
                TT('dve', t2, sn, sn, M, K5, K5)
                TS('dve', sn, t1, 2.0, M, K5, K5)
                TS('dve', cs, t2, -2.0, M, K5, K5, s2=1.0, op1=A)
            TT('dve', abr, mag, cs, M, K5, K5)
            TT('dve', abi, mag, sn, M, K5, K5)
            TT('dve', t1, are, are, M, ['pvt'] + K5, K5)
            TT('dve', t2, aim, aim, M, ['pvt'] + K5, K5)
            TT('dve', den, t1, t2, A, K5, K5)
            RECIP(den, den, K5, K5)
            TS('dve', nr, abr, -1.0, A, K5, K5)
            TT('dve', t1, nr, are, M, ['pvt'] + K5, K5)
            TT('dve', t2, abi, aim, M, ['pvt'] + K5, K5)
            TT('dve', t1, t1, t2, A, K5, K5)
            TT('dve', cre, t1, den, M, K5, K5)
            TT('dve', t1, abi, are, M, ['pvt'] + K5, K5)
            TT('dve', t2, nr, aim, M, ['pvt'] + K5, K5)
            TT('dve', t1, t1, t2, SUB, K5, K5)
            TT('dve', cim, t1, den, M, K5, K5)
            TT('dve', t1, abr, abr, M, K5, K5)
            TT('dve', t2, abi, abi, M, K5, K5)
            TT('dve', t1, t1, t2, A, K5, K5)
            RECIP(t1, t1, K5, K5)
            TT('dve', iar, abr, t1, M, K5, K5)
            TT('dve', iai, abi, t1, M, K5, K5)
            TS('dve', iai, iai, -1.0, M, K5, K5)
            s5c = T(f"s5c{l}", [128, 4, 16])
            for i, src in enumerate([abr, abi, cre, cim]):
                CP('dve', s5c[:, i, :], src, K5, ['s5c'])
            tab, tbk = FA(4 * 16, 128)
            tabv = tab.rearrange("p (a s) j -> p a s j", a=4)

            tA, tAk = FA(16, 128)
            tB, tBk = FA(16, 128)
            tC, tCk = FA(16, 128)
            for a_, src in enumerate([abr, abi, iar, iai]):
                CP('dve', tabv[:, a_, :, 0], src, K5, [tbk])

            def cmul_all(ore, oim, ire, iim, sre, sim, m):
                TT('dve', tA[:, :, 0:m], iim, sim, M, [tbk] + K5, [tAk])
                TT('pool', tB[:, :, 0:m], ire, sre, M, [tbk] + K5, [tBk])
                TT('dve', tC[:, :, 0:m], tB[:, :, 0:m], tA[:, :, 0:m], SUB, [tAk, tBk], [tCk])
                TT('dve', tA[:, :, 0:m], ire, sim, M, [tbk] + K5, [tAk])
                TT('pool', tB[:, :, 0:m], iim, sre, M, [tbk] + K5, [tBk])
                TT('dve', oim, tB[:, :, 0:m], tA[:, :, 0:m], A, [tAk, tBk], [tbk])
                CP('pool', ore, tC[:, :, 0:m], [tCk], [tbk])

            m = 1
            while m < 128:
                for a_ in (0, 2):
                    cmul_all(tabv[:, a_, :, m:2 * m], tabv[:, a_ + 1, :, m:2 * m], tabv[:, a_, :, 0:m], tabv[:, a_ + 1, :, 0:m],
                             tabv[:, a_, :, m - 1:m].broadcast_to([128, 16, m]), tabv[:, a_ + 1, :, m - 1:m].broadcast_to([128, 16, m]), m)
                m *= 2
            cmul_all(tabv[:, 2, :, :], tabv[:, 3, :, :], tabv[:, 2, :, :], tabv[:, 3, :, :],
                     cre.unsqueeze(2).broadcast_to([128, 16, 128]), cim.unsqueeze(2).broadcast_to([128, 16, 128]), 128)
            S.dma('sp', tabd.rearrange("p a s j -> p (a s) j"), tab, reads=[tbk], writes=['tabd'])
            MSET('dve', Sg_fs[0][:], 0.0, [('Sg_f', 0)])
            MSET('dve', Sg_bs[0][:], 0.0, [('Sg_b', 0)])
            MSET('dve', Sr_fs[0][:], 0.0, [('Sr_f', 0)])
            MSET('dve', Sr_bs[0][:], 0.0, [('Sr_b', 0)])
            MSET('dve', x5[:], 0.0, [('x5', s_) for s_ in range(16)])
            MSET('dve', shprev[:], 0.0, ['shprev'])
            cvt_next = convert_list(l + 1) if l + 1 < depth else []

            for bi in range(nb_all):
                t0 = bi * BLK
                last = (bi == 7)
                n = 288 if last else 256
                tiles = [(0, 128, None), (128, 128, None)]
                if last:
                    tiles += [(256, 16, None)] + [(272 + s, 1, s) for s in range(NS)]
                npr = 272 if last else 256

                phase()
                ncv = (len(cvt_next) + nb_all - 1) // nb_all
                issue_convert(cvt_next[bi * ncv:(bi + 1) * ncv])
                S.dma('sp', xb[:, :, 0:n], xres[:, :, t0:t0 + n], reads=['xres'], writes=['xb'])
                rs, rk = rmsnorm(PVO['norm_mix'], n)
                for c in range(8):
                    STT(hb[:, c, 0:n], xb[:, c, 0:n], pvt[:, PVO['norm_mix'] + c:PVO['norm_mix'] + c + 1], rs[:, 0, 0:n], M, M,
                        ['xb', 'pvt', rk], ['hb'])
                HK = ['hb']

                prefetch_w('w_in', l, 0, 8, 512)
                prefetch_w('w_in', l, 512, 8, 512)
                phase(soft=True)
                qk, qkk = FA(4, NTOKMAX)
                vbf, vk = BA(4, NTOKMAX)
                gg, ggk = FA(4, NTOKMAX)
                zg, zgk = BA(1, NTOKMAX)
                la, lak = FA(2, NTOKMAX)
                of, ofk = FA(4, NTOKMAX)
                wi = load_w('w_in', l, 0, 8, 512)
                for j in range(4):
                    b = nextpp()
                    proj(wi, 8, j * 128, 128, hb, HK, n, b)
                    CP('act', qk[:, j, 0:n], pp[b][:, 0:n], [('pp', b)], [qkk])
                wi = load_w('w_in', l, 512, 8, 512)
                for j in range(4):
                    b = nextpp()
                    proj(wi, 8, j * 128, 128, hb, HK, n, b)
                    CP('act', vbf[:, j, 0:n], pp[b][:, 0:n], [('pp', b)], [vk])
                wi = load_w('w_in', l, 1024, 8, 512)
                for j in range(4):
                    b = nextpp()
                    proj(wi, 8, j * 128, 128, hb, HK, n, b)
                    ACT(gg[:, j, 0:n], pp[b][:, 0:n], AF.Silu, [('pp', b)], [ggk])
                wi = load_w('w_in', l, 1536, 8, 16)
                b = nextpp()
                proj(wi, 8, 0, 16, hb, HK, n, b)
                CP('act', zg[0:16, 0, 0:n], pp[b][0:16, 0:n], [('pp', b)], [zgk])
                for c in range(2):
                    b = nextpp()
                    MM(pp[b][:, 0:n], gw2b[0:16, c * 128:(c + 1) * 128], zg[0:16, 0, 0:n], ['gw2b', zgk], [('pp', b)])
                    ACT(la[:, c, 0:n], pp[b][:, 0:n], AF.Exp, [('pp', b)] + PK, [lak], bias=pvd[:, c:c + 1], scale=-1.0)
                    ACT(la[:, c, 0:n], la[:, c, 0:n], AF.Ln, [lak], [lak], bias=1.0)
                    TS('dve', la[:, c, 0:n], la[:, c, 0:n], -1.0 / 16, M, [lak], [lak])
                Bc, Bk = FA(2, 128)
                Ep, Epk = FA(2, 128)
                Em, Emk = FA(2, 128)
                tmpS, tmpk = FA(2, 128)
                qt, qtk = BA(2, 128)
                kt, ktk = BA(2, 128)
                Ktm, Ktk = BA(1, 256)
                Vtm, Vtk = BA(1, 512)
                STb, STk = BA(4, 128)
                for (off, L, smp) in tiles:
                    sp_ = 0 if smp is None else 1 + (smp % 2)
                    Sg_f, Sg_b, SgFk, SgBk = Sg_fs[sp_], Sg_bs[sp_], ('Sg_f', sp_), ('Sg_b', sp_)
                    if smp is not None:
                        S.dma('sp', Sg_f[:], sgla[l, smp].rearrange("(c hp) d e -> (hp d) c e", hp=2), writes=[SgFk])
                        CP('dve', Sg_b[:], Sg_f[:], [SgFk], [SgBk])
                    for c in range(2):
                        SCAN(Bc[:, c, 0:L], onesf[:, 0:L], la[:, c, off:off + L], 0.0, [lak] + CK, [Bk])
                    ACT(Ep[:, :, 0:L], Bc[:, :, 0:L], AF.Exp, [Bk], [Epk])
                    ACT(Em[:, :, 0:L], Bc[:, :, 0:L], AF.Exp, [Bk], [Emk], scale=-1.0)
                    STT(qt[:, :, 0:L], qk[:, 0:2, off:off + L], 0.125, Ep[:, :, 0:L], M, M, [qkk, Epk], [qtk])
                    TT('dve', kt[:, :, 0:L], qk[:, 2:4, off:off + L], Em[:, :, 0:L], M, [qkk, Emk], [ktk])
                    for c in range(2):
                        TR(ptb[0:L, c * 128:(c + 1) * 128], kt[:, c, 0:L], cstb[:], [ktk, 'cstb'], ['ptb'])
                    CP('act', Ktm[0:L, 0, :], ptb[0:L, 0:256], ['ptb'], [Ktk])
                    for hh in range(4):
                        TR(ptb[0:L, 256 + hh * 128:256 + (hh + 1) * 128], vbf[:, hh, off:off + L], cstb[:], [vk, 'cstb'], ['ptb'])
                    CP('dve', Vtm[0:L, 0, :], ptb[0:L, 256:768], ['ptb'], [Vtk])
                    gbanks = [(pc[0], ('pc', 0)), (pc[1], ('pc', 1)), (pp[0], ('pp', 0)), (pp[1], ('pp', 1))]
                    obanks = [(py, 'py'), (pn, 'pn')]
                    for h in range(4):
                        c, hp = h // 2, h % 2
                        P = slice(64 * hp, 64 * hp + 64)
                        t_, k_ = gbanks[h]
                        MM(t_[0:L, 0:L], kt[P, c, 0:L], qt[P, c, 0:L], [ktk, qtk], [k_])
                        TT('dve', STb[0:L, h, 0:L], t_[0:L, 0:L], MUI[0:L, 0:L], M, [k_] + CK, [(STk, h)])
                    for h in range(4):
                        c, hp = h // 2, h % 2
                        P = slice(64 * hp, 64 * hp + 64)
                        ob, obk = obanks[h % 2]
                        MM(ob[:, 0:L], Sg_b[P, c, :], qt[P, c, 0:L], [SgBk, qtk], [obk], start=True, stop=False)
                        MM(ob[:, 0:L], Vtm[0:L, 0, h * 128:(h + 1) * 128], STb[0:L, h, 0:L], [Vtk, (STk, h)], [obk], start=False, stop=True)
                        CP('act' if h % 2 else 'dve', of[:, h, off:off + L], ob[:, 0:L], [obk], [ofk])
                    for h in range(4):
                        c, hp = h // 2, h % 2
                        P = slice(64 * hp, 64 * hp + 64)
                        MM(pst[P, c * 128:(c + 1) * 128], Ktm[0:L, 0, c * 128 + 64 * hp:c * 128 + 64 * hp + 64], Vtm[0:L, 0, h * 128:(h + 1) * 128],
                           [Ktk, Vtk], ['pst'])
                    for c in range(2):
                        TT('dve', tmpS[:, c, :], Sg_f[:, c, :], pst[:, c * 128:(c + 1) * 128], A, [SgFk, 'pst'], [tmpk])
                        TS('dve', Sg_f[:, c, :], tmpS[:, c, :], Ep[:, c, L - 1:L], M, [tmpk, Epk], [SgFk])
                    CP('act', Sg_b[:], Sg_f[:], [SgFk], [SgBk])
                    if smp is not None:
                        S.dma('sp', glas[l, smp].rearrange("(c hp) d e -> (hp d) c e", hp=2), Sg_f[:], reads=[SgFk], is_output=True)
                    elif last and off == 256:
                        S.dma('sp', glap[l].rearrange("(c hp) d e -> (hp d) c e", hp=2), Sg_f[:], reads=[SgFk], is_output=True)
                sq, sk = FA(1, NTOKMAX)
                r2, r2k = FA(1, NTOKMAX)
                for h in range(4):
                    ACT(sq[:, 0, 0:n], of[:, h, 0:n], AF.Square, [ofk], [sk])
                    MM(pn[:, 0:n], onesf, sq[:, 0, 0:n], [sk] + CK, ['pn'])
                    ACT(r2[:, 0, 0:n], pn[:, 0:n], AF.Sqrt, ['pn'], [r2k], bias=1e-6, scale=1.0 / 128)
                    RECIP(r2[:, 0, 0:n], r2[:, 0, 0:n], [r2k], [r2k])
                    STT(sq[:, 0, 0:n], of[:, h, 0:n], pvt[:, PVO['gla_norm'] + h:PVO['gla_norm'] + h + 1], r2[:, 0, 0:n], M, M, [ofk, r2k, 'pvt'], [sk])
                    TT('dve', og[:, 0, h, 0:n], sq[:, 0, 0:n], gg[:, h, 0:n], M, [sk, ggk], ['og0'])
                if 'R' in PH:
                    prefetch_w('w_in', l, 1552, 8, 512)
                    prefetch_w('w_in', l, 2064, 8, 512)
                    phase()
                    xs, xsk = FA(14, NTOKMAX)
                    pbufs = [FA(1, NTOKMAX + 2) for _ in range(2)]
                    dds = [FA(1, NTOKMAX) for _ in range(2)]
                    stg, stgk = FA(1, 1792)
                    if last:
                        shin, shink = FA(14, 16)
                        pso, psok = FA(14, 16)
                        S.dma('sp', stg[0:16, 0, :], ssh[l], writes=[stgk])
                        for cc in range(14):
                            b = nextpp()
                            TR(pp[b][:, 0:16], stg[0:16, 0, cc * 128:(cc + 1) * 128], identf[0:16, 0:16], [stgk] + CK, [('pp', b)])
                            CP('act', shin[:, cc, :], pp[b][:, 0:16], [('pp', b)], [shink])
                    mu0 = PVO['rwkv_mu']
                    for gi, (c0, ncol) in enumerate([(1552, 512), (2064, 512), (2576, 512), (3088, 256)]):
                        wi = load_w('w_in', l, c0, 8, ncol)
                        for j in range(ncol // 128):
                            cc = gi * 4 + j
                            pbuf, pbk = pbufs[cc % 2]
                            dd, ddk = dds[cc % 2]
                            b = nextpp()
                            proj(wi, 8, j * 128, 128, hb, HK, n, b)
                            CP('act', pbuf[:, 0, 1:n + 1], pp[b][:, 0:n], [('pp', b)], [pbk])
                            CP('dve', pbuf[:, 0, 0:1], shprev[:, cc, :], ['shprev'], [pbk])
                            TT('dve', dd[:, 0, 0:npr], pbuf[:, 0, 0:npr], pbuf[:, 0, 1:npr + 1], SUB, [pbk], [ddk])
                            if last:
                                TT('dve', dd[:, 0, 272:288], shin[:, cc, :], pbuf[:, 0, 273:289], SUB, [pbk, shink], [ddk])
                                CP('pool', pso[:, cc, :], pbuf[:, 0, 273:289], [pbk], [psok])
                            STT(xs[:, cc, 0:n], dd[:, 0, 0:n], pvt[:, mu0 + cc:mu0 + cc + 1], pbuf[:, 0, 1:n + 1], M, A, [ddk, pbk, 'pvt'], [xsk])
                            CP('pool', shprev[:, cc, :], pbuf[:, 0, npr:npr + 1], [pbk], ['shprev'])
                    if last:
                        b = nextpp()
                        TR(pp[b][0:14, 0:128], shprev[:, :, 0], identf, ['shprev'] + CK, [('pp', b)])
                        CP('act', stg[0:14, 0, 0:128], pp[b][0:14, 0:128], [('pp', b)], [stgk])
                        S.dma('sp', shp[l], stg[0:14, 0, 0:128], reads=[stgk], is_output=True)
                        for cc in range(14):
                            b = nextpp()
                            TR(pp[b][0:16, 0:128], pso[:, cc, :], identf, [psok] + CK, [('pp', b)])
                            CP('act', stg[0:16, 0, cc * 128:(cc + 1) * 128], pp[b][0:16, 0:128], [('pp', b)], [stgk])
                        S.dma('sp', shs[l], stg[0:16, 0, :], reads=[stgk], is_output=True)
                    tzw, tzk = BA(1, NTOKMAX)
                    zab, zak = BA(1, NTOKMAX)
                    szg, szk = BA(1, NTOKMAX)
                    ACT(tzw[0:64, 0, 0:n], xs[0:64, 12, 0:n], AF.Tanh, [xsk], [tzk])
                    CP('dve', zab[64:128, 0, 0:n], xs[64:128, 12, 0:n], [xsk], [zak])
                    ACT(szg[:, 0, 0:n], xs[:, 13, 0:n], AF.Sigmoid, [xsk], [szk])
                    lw, lwk = FA(4, NTOKMAX)
                    kk, kkk = FA(4, NTOKMAX)
                    bb, bbk = FA(4, NTOKMAX)
                    gq, gqk = FA(4, NTOKMAX)
                    bon, bonk = FA(4, NTOKMAX)
                    aa, aak = FA(4, NTOKMAX)
                    t1, t1k = FA(4, NTOKMAX)
                    t2, t2k = FA(4, NTOKMAX)
                    rbs = [(pp[0], ('pp', 0)), (pp[1], ('pp', 1)), (pc[0], ('pc', 0)), (pc[1], ('pc', 1))]
                    rbn = [(pn, 'pn'), (py, 'py'), (pst, 'pst'), (pc[0], ('pc', 0))]
                    C4 = range(4)
                    csl = lambda c: slice(c * 128, (c + 1) * 128)
                    pcol = lambda nm, c: pvt[:, PVO[nm] + c:PVO[nm] + c + 1]
                    for c in C4:
                        t_, k_ = rbs[c]
                        MM(t_[:, 0:n], w2b[0:64, csl(c)], tzw[0:64, 0, 0:n], ['w2b', tzk], [k_])
                        ACT(lw[:, c, 0:n], t_[:, 0:n], AF.Sigmoid, [k_, 'pvt'], [(lwk, c)], bias=pcol('rwkv_w0', c))
                    for c in C4:
                        t_, k_ = rbs[c]
                        MM(t_[:, 0:n], a2b[64:128, csl(c)], zab[64:128, 0, 0:n], ['a2b', zak], [k_])
                        ACT(aa[:, c, 0:n], t_[:, 0:n], AF.Sigmoid, [k_, 'pvt'], [(aak, c)], bias=pcol('rwkv_a0', c))
                    for c in C4:
                        t_, k_ = rbs[c]
                        MM(t_[:, 0:n], g2b[:, csl(c)], szg[:, 0, 0:n], ['g2b', szk], [k_])
                        CP('act', gq[:, c, 0:n], t_[:, 0:n], [k_], [gqk])
                    for c in C4:
                        TS('pool', lw[:, c, 0:n], lw[:, c, 0:n], -0.6065306597126334, M, [(lwk, c)], [(lwk, c)])
                        TS('dve', kk[:, c, 0:n], xs[:, 4 + c, 0:n], pcol('rwkv_k_k', c), M, [xsk, 'pvt'], [(kkk, c)])
                    for c in C4:
                        ACT(t1[:, c, 0:n], kk[:, c, 0:n], AF.Square, [(kkk, c)], [(t1k, c)])
                    for c in C4:
                        t_, k_ = rbn[c]
                        MM(t_[:, 0:n], bonesf, t1[:, c, 0:n], [(t1k, c)] + CK, [k_])
                        ACT(t1[:, c, 0:n], t_[:, 0:n], AF.Sqrt, [k_], [(t1k, c)], bias=1e-12)
                    for c in C4:
                        RECIP(t1[:, c, 0:n], t1[:, c, 0:n], [(t1k, c)], [(t1k, c)])
                        TS('dve', t2[:, c, 0:n], aa[:, c, 0:n], pcol('rwkv_k_a', c), M, [(aak, c)] + PK, [(t2k, c)], s2=pvd[:, 2 + c:3 + c], op1=A)
                    for c in C4:
                        TT('pool', kk[:, c, 0:n], kk[:, c, 0:n], t1[:, c, 0:n], M, [(kkk, c), (t1k, c)], [(kkk, c)])
                        TT('pool', xs[:, 4 + c, 0:n], xs[:, 4 + c, 0:n], t2[:, c, 0:n], M, [xsk, (t2k, c)], [xsk])
                    for c in C4:
                        TT('pool', bb[:, c, 0:n], kk[:, c, 0:n], aa[:, c, 0:n], M, [(kkk, c), (aak, c)], [bbk])
                        STT(t1[:, c, 0:n], xs[:, c, 0:n], pcol('rwkv_r_k', c), xs[:, 4 + c, 0:n], M, M, [xsk, (kkk, c), 'pvt'], [(t1k, c)])
                    for c in C4:
                        t_, k_ = rbn[c]
                        MM(t_[:, 0:n], bonesf, t1[:, c, 0:n], [(t1k, c)] + CK, [k_])
                        TT('dve', bon[:, c, 0:n], t_[:, 0:n], xs[:, 8 + c, 0:n], M, [k_, xsk], [bonk])
                    XK = [xsk]
                    LWK = [(lwk, c) for c in C4]
                    KKK = [(kkk, c) for c in C4]
                    Bc, Bk = FA(4, 128)
                    Ep, Epk = FA(4, 128)
                    Em, Emk = FA(4, 128)
                    Ex, Exk = FA(4, 128)
                    rt, rtk = BA(4, 128)
                    kt, ktk = BA(4, 128)
                    bt, btk = BA(4, 128)
                    at, atk = BA(4, 128)
                    vb, vbk = BA(4, 128)
                    Ktm, Ktk = BA(1, 512)
                    Btm, Btk = BA(1, 512)
                    Vtm, Vtk = BA(1, 512)
                    NH = 8
                    Qb = [[BA(1, 128) for _ in range(2)] for _ in range(NH)]
                    Nb = [[BA(1, 128) for _ in range(2)] for _ in range(NH)]
                    Xb = [[BA(1, 64) for _ in range(2)] for _ in range(NH)]
                    LkT = [BA(1, 128) for _ in range(NH)]
                    ArkT = [BA(1, 128) for _ in range(NH)]
                    ArbT = [BA(1, 128) for _ in range(NH)]
                    Un = [BA(1, 64) for _ in range(NH)]
                    pcs = [(pc[0], ('pc', 0)), (pc[1], ('pc', 1)), (pp[0], ('pp', 0)), (pp[1], ('pp', 1)), (pn, 'pn')]
                    pci = [0]

                    def npc():
                        pci[0] = (pci[0] + 1) % len(pcs)
                        return pcs[pci[0]]
                    yfr, yfrk = FA(4, NTOKMAX)
                    tmpS, tmpk = FA(4, 64)
                    stRs = [FA(8, 64) for _ in range(2)]
                    for (off, L, smp) in tiles:
                        sp_ = 0 if smp is None else 1 + (smp % 2)
                        Sr_f, Sr_b, SrFk, SrBk = Sr_fs[sp_], Sr_bs[sp_], ('Sr_f', sp_), ('Sr_b', sp_)
                        stR, stRk = stRs[sp_ % 2]
                        if smp is not None:
                            S.dma('sp', stR[0:64, :, :], srw[l, smp].rearrange("h v k -> v h k"), writes=[stRk])
                            for c in range(4):
                                b = nextpp()
                                TR(pp[b][:, 0:64], stR[0:64, 2 * c:2 * c + 2, :].rearrange("p a k -> p (a k)"), identf[0:64, 0:64], [stRk] + CK, [('pp', b)])
                                CP('act', Sr_f[:, c, :], pp[b][:, 0:64], [('pp', b)], [SrFk])
                            CP('dve', Sr_b[:], Sr_f[:], [SrFk], [SrBk])
                        for c in range(4):
                            SCAN(Bc[:, c, 0:L], onesf[:, 0:L], lw[:, c, off:off + L], 0.0, LWK + CK, [Bk])
                        ACT(Ep[:, :, 0:L], Bc[:, :, 0:L], AF.Exp, [Bk], [Epk])
                        ACT(Em[:, :, 0:L], Bc[:, :, 0:L], AF.Exp, [Bk], [Emk], scale=-1.0)
                        TT('pool', Ex[:, :, 0:L], Bc[:, :, 0:L], lw[:, :, off:off + L], SUB, [Bk] + LWK, [Exk])
                        ACT(Ex[:, :, 0:L], Ex[:, :, 0:L], AF.Exp, [Exk], [Exk])
                        TT('dve', rt[:, :, 0:L], xs[:, 0:4, off:off + L], Ep[:, :, 0:L], M, [xsk, Epk], [rtk])
                        TT('pool', kt[:, :, 0:L], xs[:, 4:8, off:off + L], Em[:, :, 0:L], M, XK + [Emk], [ktk])
                        TT('pool', bt[:, :, 0:L], bb[:, :, off:off + L], Em[:, :, 0:L], M, [bbk, Emk], [btk])
                        TT('pool', at[:, :, 0:L], kk[:, :, off:off + L], Ex[:, :, 0:L], M, KKK + [Exk], [atk])
                        CP('act', vb[:, :, 0:L], xs[:, 8:12, off:off + L], [xsk], [vbk])
                        for (src, sk_, dst, dk_) in [(kt, ktk, Ktm, Ktk), (bt, btk, Btm, Btk), (vb, vbk, Vtm, Vtk)]:
                            for c in range(4):
                                TR(ptb[0:L, c * 128:(c + 1) * 128], src[:, c, 0:L], cstb[:], [sk_, 'cstb'], ['ptb'])
                            CP('act', dst[0:L, 0, :], ptb[0:L, 0:512], ['ptb'], [dk_])
                        nlev = {128: 7, 16: 4, 1: 0}[L]
                        for grp in range(8 // NH):
                            HD = []
                            for i in range(NH):
                                h = grp * NH + i
                                c, hp = h // 2, h % 2
                                HD.append((i, h, c, slice(64 * hp, 64 * hp + 64), slice(h * 64, (h + 1) * 64)))
                            if nlev > 0:
                                for (i, h, c, P, hs) in HD:
                                    t_, k_ = npc()
                                    MM(t_[0:L, 0:L], bt[P, c, 0:L], at[P, c, 0:L], [btk, atk], [k_])
                                    TT('dve', RR(Qb[i][0][0][0:L, 0, 0:L]), t_[0:L, 0:L], MUn[0:L, 0:L], M, [k_] + CK, [Qb[i][0][1]])
                                for (i, h, c, P, hs) in HD:
                                    t_, k_ = npc()
                                    MM(t_[0:L, 0:L], at[P, c, 0:L], bt[P, c, 0:L], [btk, atk], [k_])
                                    TT('dve', RR(Nb[i][0][0][0:L, 0, 0:L]), t_[0:L, 0:L], MLn[0:L, 0:L], M, [k_] + CK, [Nb[i][0][1]])
                                for (i, h, c, P, hs) in HD:
                                    t_, k_ = npc()
                                    MM(t_[0:L, 0:L], kt[P, c, 0:L], at[P, c, 0:L], [ktk, atk], [k_])
                                    TT('dve', LkT[i][0][0:L, 0, 0:L], t_[0:L, 0:L], MU[0:L, 0:L], M, [k_] + CK, [LkT[i][1]])
                            for (i, h, c, P, hs) in HD:
                                t_, k_ = npc()
                                MM(t_[0:L, 0:L], kt[P, c, 0:L], rt[P, c, 0:L], [ktk, rtk], [k_])
                                TT('dve', ArkT[i][0][0:L, 0, 0:L], t_[0:L, 0:L], MUI[0:L, 0:L], M, [k_] + CK, [ArkT[i][1]])
                            for (i, h, c, P, hs) in HD:
                                t_, k_ = npc()
                                MM(t_[0:L, 0:L], bt[P, c, 0:L], rt[P, c, 0:L], [btk, rtk], [k_])
                                TT('dve', ArbT[i][0][0:L, 0, 0:L], t_[0:L, 0:L], MUI[0:L, 0:L], M, [k_] + CK, [ArbT[i][1]])
                            for (i, h, c, P, hs) in HD:
                                t_, k_ = npc()
                                MM(t_[0:L, 0:64], at[P, c, 0:L], Sr_b[P, c, :], [atk, SrBk], [k_], start=True, stop=(nlev == 0))
                                if nlev > 0:
                                    MM(t_[0:L, 0:64], LkT[i][0][0:L, 0, 0:L], Vtm[0:L, 0, hs], [LkT[i][1], Vtk], [k_], start=False, stop=True)
                                CP('act', RR(Xb[i][0][0][0:L, 0, :]), t_[0:L, 0:64], [k_], [Xb[i][0][1]])
                            for k in range(nlev):
                                a_, b_ = k % 2, (k + 1) % 2
                                for (i, h, c, P, hs) in HD:
                                    t_, k_ = npc()
                                    MM(t_[0:L, 0:64], RR(Qb[i][a_][0][0:L, 0, 0:L]), RR(Xb[i][a_][0][0:L, 0, :]), [Qb[i][a_][1], Xb[i][a_][1]], [k_])
                                    TT('dve', RR(Xb[i][b_][0][0:L, 0, :]), Xb[i][a_][0][0:L, 0, :], t_[0:L, 0:64], A, [Xb[i][a_][1], k_], [Xb[i][b_][1]])
                                if k < nlev - 1:
                                    for (i, h, c, P, hs) in HD:
                                        t_, k_ = npc()
                                        MM(t_[0:L, 0:L], RR(Nb[i][a_][0][0:L, 0, 0:L]), RR(Qb[i][a_][0][0:L, 0, 0:L]), [Nb[i][a_][1], Qb[i][a_][1]], [k_])
                                        CP('act', RR(Qb[i][b_][0][0:L, 0, 0:L]), t_[0:L, 0:L], [k_], [Qb[i][b_][1]])
                                if k < nlev - 2:
                                    for (i, h, c, P, hs) in HD:
                                        t_, k_ = npc()
                                        MM(t_[0:L, 0:L], RR(Qb[i][a_][0][0:L, 0, 0:L]), RR(Nb[i][a_][0][0:L, 0, 0:L]), [Nb[i][a_][1], Qb[i][a_][1]], [k_])
                                        CP('act' if i % 2 else 'dve', RR(Nb[i][b_][0][0:L, 0, 0:L]), t_[0:L, 0:L], [k_], [Nb[i][b_][1]])
                            xf = nlev % 2
                            for (i, h, c, P, hs) in HD:
                                TS('dve', Un[i][0][0:L, 0, :], Xb[i][xf][0][0:L, 0, :], -1.0, M, [Xb[i][xf][1]], [Un[i][1]])
                            for (i, h, c, P, hs) in HD:
                                yo = py[P, c * 128:c * 128 + L]
                                MM(yo, Sr_b[P, c, :], rt[P, c, 0:L], [SrBk, rtk], ['py'], start=True, stop=False)
                                MM(yo, Vtm[0:L, 0, hs], ArkT[i][0][0:L, 0, 0:L], [Vtk, ArkT[i][1]], ['py'], start=False, stop=False)
                                MM(yo, Un[i][0][0:L, 0, :], ArbT[i][0][0:L, 0, 0:L], [Un[i][1], ArbT[i][1]], ['py'], start=False, stop=True)
                                so = pst[P, c * 64:(c + 1) * 64]
                                MM(so, Ktm[0:L, 0, hs], Vtm[0:L, 0, hs], [Ktk, Vtk], ['pst'], start=True, stop=False)
                                MM(so, Btm[0:L, 0, hs], Un[i][0][0:L, 0, :], [Btk, Un[i][1]], ['pst'], start=False, stop=True)
                        CP('act', yfr[:, :, off:off + L], py[:, :].rearrange("p (c j) -> p c j", c=4)[:, :, 0:L], ['py'], [yfrk])
                        for c in range(4):
                            TT('dve', tmpS[:, c, :], Sr_f[:, c, :], pst[:, c * 64:(c + 1) * 64], A, [SrFk, 'pst'], [tmpk])
                            TS('dve', Sr_f[:, c, :], tmpS[:, c, :], Ep[:, c, L - 1:L], M, [tmpk, Epk], [SrFk])
                        CP('act', Sr_b[:], Sr_f[:], [SrFk], [SrBk])
                        if smp is not None or (last and off == 256):
                            for c in range(4):
                                b = nextpp()
                                TR(pp[b][0:64, 0:128], Sr_f[:, c, :], identf, [SrFk] + CK, [('pp', b)])
                                CP('act', stR[0:64, 2 * c:2 * c + 2, :].rearrange("p a k -> p (a k)"), pp[b][0:64, 0:128], [('pp', b)], [stRk])
                            dst = rws[l, smp] if smp is not None else rwp[l]
                            S.dma('sp', dst.rearrange("h v k -> v h k"), stR[0:64, :, :], reads=[stRk], is_output=True)
                    for c in C4:
                        t_, k_ = rbn[c]
                        MM(t_[:, 0:n], bonesf, yfr[:, c, 0:n], [yfrk] + CK, [k_])
                        STT(t1[:, c, 0:n], t_[:, 0:n], -1.0 / 64, yfr[:, c, 0:n], M, A, [k_, yfrk], [(t1k, c)])
                    for c in C4:
                        ACT(t2[:, c, 0:n], t1[:, c, 0:n], AF.Square, [(t1k, c)], [(t2k, c)])
                    for c in C4:
                        t_, k_ = rbn[c]
                        MM(t_[:, 0:n], bonesf, t2[:, c, 0:n], [(t2k, c)] + CK, [k_])
                        ACT(t2[:, c, 0:n], t_[:, 0:n], AF.Sqrt, [k_], [(t2k, c)], bias=64e-5, scale=1.0 / 64)
                    for c in C4:
                        RECIP(t2[:, c, 0:n], t2[:, c, 0:n], [(t2k, c)], [(t2k, c)])
                    for c in C4:
                        TT('pool', t1[:, c, 0:n], t1[:, c, 0:n], t2[:, c, 0:n], M, [(t1k, c), (t2k, c)], [(t1k, c)])
                    for c in C4:
                        TS('dve', t1[:, c, 0:n], t1[:, c, 0:n], pcol('rwkv_ln_w', c), M, [(t1k, c), 'pvt'], [(t1k, c)], s2=pcol('rwkv_ln_b', c), op1=A)
                    for c in C4:
                        TT('pool', t1[:, c, 0:n], t1[:, c, 0:n], bon[:, c, 0:n], A, [(t1k, c), bonk], [(t1k, c)])
                    for c in C4:
                        TT('dve', og[:, 1, c, 0:n], t1[:, c, 0:n], gq[:, c, 0:n], M, [(t1k, c), gqk], ['og1'])
                if 'S' in PH:
                    prefetch_w('w_in', l, 3344, 8, 512)
                    phase()
                    uf, ufk = FA(4, NTOKMAX)
                    ub, ubk = BA(4, NTOKMAX)
                    wi = load_w('w_in', l, 3344, 8, 512)
                    for j in range(4):
                        b = nextpp()
                        proj(wi, 8, j * 128, 128, hb, HK, n, b)
                        CP('act', uf[:, j, 0:n], pp[b][:, 0:n], [('pp', b)], [ufk])
                        CP('dve', ub[:, j, 0:n], pp[b][:, 0:n], [('pp', b)], [ubk])
                    tab, tbk = FA(64, 128)
                    S.dma('sp', tab, tabd.rearrange("p a s j -> p (a s) j"), reads=['tabd'], writes=[tbk])
                    tabv = tab.rearrange("p (a s) j -> p a s j", a=4)
                    bpb = []
                    for i, src in enumerate([bpr, bpi, cpr, cpi]):
                        t_, k_ = BA(16, 128)
                        S.dma('pool', t_, src[l], writes=[k_])
                        bpb.append((t_, k_))
                    TS('dve', bpb[3][0], bpb[3][0], -1.0, M, [bpb[3][1]], [bpb[3][1]])
                    ys5, ysk = FA(4, NTOKMAX)
                    NR5 = 8
                    R5 = [[FA(2, 128) for _ in range(5)] + [BA(2, 128)] for _ in range(NR5)]
                    pcs5 = [(pc[0], ('pc', 0)), (pc[1], ('pc', 1)), (pp[0], ('pp', 0)), (pp[1], ('pp', 1)), (pn, 'pn')]
                    stiles = [(0, 128, None), (128, 128, None)]
                    if last:
                        stiles += [(256, 16, None), (272, 16, 0)]
                        xs0, xs0k = FA(16 * 2, 16)
                        xs0v = xs0.rearrange("p (s a) n -> p s a n", a=2)
                        xso, xsok = FA(16 * 2, 16)
                        xsov = xso.rearrange("p (s a) n -> p s a n", a=2)
                        stg, stgk = FA(1, 2048)
                        for a_, src in enumerate([s5r, s5i]):
                            S.dma('sp', stg[0:16, 0, :], src[l], writes=[stgk])
                            for sc in range(16):
                                b = nextpp()
                                TR(pp[b][:, 0:16], stg[0:16, 0, sc * 128:(sc + 1) * 128], identf[0:16, 0:16], [stgk] + CK, [('pp', b)])
                                CP('act', xs0v[:, sc, a_, :], pp[b][:, 0:16], [('pp', b)], [xs0k])
                    def s5_chain(kidx, off, L, smp, uc):
                        SCS = []
                        for q in range(4):
                            sc = uc * 4 + q
                            SCS.append((q, sc, R5[(kidx % 2) * 4 + q], pcs5[sc % 5]))
                        for (q, sc, R_, (pct, pck)) in SCS:
                            for pl, wsel in enumerate([0, 1, 1, 0]):
                                MM(pct[:, pl * 128:pl * 128 + L], bpb[wsel][0][:, sc, :], ub[:, uc, off:off + L], [bpb[wsel][1], ubk], [pck], chain=True)
                        BU = {}
                        for (q, sc, R_, (pct, pck)) in SCS:
                            BU[sc] = (pct[:, 0:256].rearrange("p (a j) -> p a j", a=2)[:, :, 0:L],
                                      pct[:, 256:512].rearrange("p (a j) -> p a j", a=2)[:, :, 0:L])
                        yield
                        if smp is None:
                            for (q, sc, R_, (pct, pck)) in SCS:
                                (m1, m1k), (m2, m2k), (zz, zzk), (ZZ, ZZk), (xx, xxk), (xbf, xbk) = R_
                                Bu, Bus = BU[sc]
                                TT('dve', m1[:, :, 0:L], Bu, tabv[:, 2, sc, 0:L].unsqueeze(1).broadcast_to([128, 2, L]), M, [pck, tbk], [m1k])
                                TT('dve', m2[:, :, 0:L], Bus, tabv[:, 3, sc, 0:L].unsqueeze(1).broadcast_to([128, 2, L]), M, [pck, tbk], [m2k])
                            yield
                            for (q, sc, R_, (pct, pck)) in SCS:
                                (m1, m1k), (m2, m2k), (zz, zzk), (ZZ, ZZk), (xx, xxk), (xbf, xbk) = R_
                                TT('pool', zz[:, 0, 0:L], m1[:, 0, 0:L], m2[:, 0, 0:L], SUB, [m1k, m2k], [zzk])
                                TT('pool', zz[:, 1, 0:L], m1[:, 1, 0:L], m2[:, 1, 0:L], A, [m1k, m2k], [zzk])
                            yield
                            for (q, sc, R_, (pct, pck)) in SCS:
                                (m1, m1k), (m2, m2k), (zz, zzk), (ZZ, ZZk), (xx, xxk), (xbf, xbk) = R_
                                for a_ in range(2):
                                    SCAN(ZZ[:, a_, 0:L], onesf[:, 0:L], zz[:, a_, 0:L], x5[:, sc, a_:a_ + 1], [zzk, ('x5', sc)] + CK, [ZZk])
                            yield
                            for (q, sc, R_, (pct, pck)) in SCS:
                                (m1, m1k), (m2, m2k), (zz, zzk), (ZZ, ZZk), (xx, xxk), (xbf, xbk) = R_
                                TT('dve', m1[:, :, 0:L], ZZ[:, :, 0:L], tabv[:, 0, sc, 0:L].unsqueeze(1).broadcast_to([128, 2, L]), M, [ZZk, tbk], [m1k])
                                TT('pool', m2[:, 0, 0:L], ZZ[:, 1, 0:L], tabv[:, 1, sc, 0:L], M, [ZZk, tbk], [m2k])
                                TT('pool', m2[:, 1, 0:L], ZZ[:, 0, 0:L], tabv[:, 1, sc, 0:L], M, [ZZk, tbk], [m2k])
                            yield
                            for (q, sc, R_, (pct, pck)) in SCS:
                                (m1, m1k), (m2, m2k), (zz, zzk), (ZZ, ZZk), (xx, xxk), (xbf, xbk) = R_
                                TT('dve', xx[:, 0, 0:L], m1[:, 0, 0:L], m2[:, 0, 0:L], SUB, [m1k, m2k], [xxk])
                                TT('dve', xx[:, 1, 0:L], m1[:, 1, 0:L], m2[:, 1, 0:L], A, [m1k, m2k], [xxk])
                            yield
                            for (q, sc, R_, (pct, pck)) in SCS:
                                (m1, m1k), (m2, m2k), (zz, zzk), (ZZ, ZZk), (xx, xxk), (xbf, xbk) = R_
                                CP('pool', x5[:, sc, :], xx[:, :, L - 1], [xxk], [('x5', sc)])
                                CP('act', xbf[:, :, 0:L], xx[:, :, 0:L], [xxk], [xbk])
                            yield
                        else:
                            for (q, sc, R_, (pct, pck)) in SCS:
                                (m1, m1k), (m2, m2k), (zz, zzk), (ZZ, ZZk), (xx, xxk), (xbf, xbk) = R_
                                Bu, Bus = BU[sc]
                                PCK = [pck]
                                cr = s5c[:, 2, sc:sc + 1]
                                cim_ = s5c[:, 3, sc:sc + 1]
                                ar_ = s5c[:, 0, sc:sc + 1]
                                ai_ = s5c[:, 1, sc:sc + 1]
                                TS('dve', m1[:, :, 0:L], Bu, cr, M, PCK + ['s5c'], [m1k])
                                TS('dve', m2[:, :, 0:L], Bus, cim_, M, PCK + ['s5c'], [m2k])
                                TT('pool', zz[:, 0, 0:L], m1[:, 0, 0:L], m2[:, 0, 0:L], SUB, [m1k, m2k], [zzk])
                                TT('pool', zz[:, 1, 0:L], m1[:, 1, 0:L], m2[:, 1, 0:L], A, [m1k, m2k], [zzk])
                                STT(ZZ[:, :, 0:L], xs0v[:, sc, :, :], ar_, zz[:, :, 0:L], M, A, [xs0k, zzk, 's5c'], [ZZk])
                                TS('dve', m2[:, 0, 0:L], xs0v[:, sc, 1, :], ai_, M, [xs0k, 's5c'], [m2k])
                                TS('dve', m2[:, 1, 0:L], xs0v[:, sc, 0, :], ai_, M, [xs0k, 's5c'], [m2k])
                                TT('dve', xx[:, 0, 0:L], ZZ[:, 0, 0:L], m2[:, 0, 0:L], SUB, [ZZk, m2k], [xxk])
                                TT('dve', xx[:, 1, 0:L], ZZ[:, 1, 0:L], m2[:, 1, 0:L], A, [ZZk, m2k], [xxk])
                                CP('pool', xsov[:, sc, :, :], xx[:, :, 0:L], [xxk], [xsok])
                                CP('act', xbf[:, :, 0:L], xx[:, :, 0:L], [xxk], [xbk])
                            yield
                        for (q, sc, R_, (pct, pck)) in SCS:
                            (m1, m1k), (m2, m2k), (zz, zzk), (ZZ, ZZk), (xx, xxk), (xbf, xbk) = R_
                            MM(py[:, 0:L], bpb[2][0][:, sc, :], xbf[:, 0, 0:L], [bpb[2][1], xbk], ['py'], start=(q == 0), stop=False, chain=True)
                            MM(py[:, 0:L], bpb[3][0][:, sc, :], xbf[:, 1, 0:L], [bpb[3][1], xbk], ['py'], start=False, stop=(q == 3), chain=True)
                        CP('act', ys5[:, uc, off:off + L], py[:, 0:L], ['py'], [ysk])

                    def s5_store_prompt():
                        for a_, dst in enumerate([s5rp, s5ip]):
                            b = nextpp()
                            TR(pp[b][0:16, 0:128], x5[:, :, a_], identf, [('x5', s_) for s_ in range(16)] + CK, [('pp', b)])
                            CP('act', stg[0:16, 0, 0:128], pp[b][0:16, 0:128], [('pp', b)], [stgk])
                            S.dma('sp', dst[l], stg[0:16, 0, 0:128], reads=[stgk], is_output=True)
                        yield

                    pend = []
                    kidx = 0
                    for (off, L, smp) in stiles:
                        for uc in range(4):
                            pend.append(s5_chain(kidx, off, L, smp, uc))
                            kidx += 1
                        if last and off == 256:
                            pend.append('fence')
                            pend.append(s5_store_prompt())
                            pend.append('fence')
                    SKEW = 4
                    active = []
                    while pend or active:
                        while pend:
                            if pend[0] == 'fence':
                                if active:
                                    break
                                pend.pop(0)
                                continue
                            if not active or (len(active) < 2 and active[-1][1] >= SKEW):
                                active.append([pend.pop(0), 0])
                            else:
                                break
                        for ent in list(active):
                            try:
                                next(ent[0])
                                ent[1] += 1
                            except StopIteration:
                                active.remove(ent)
                    if last:
                        for a_, dst in enumerate([s5rs, s5is]):
                            for sc in range(16):
                                b = nextpp()
                                TR(pp[b][0:16, 0:128], xsov[:, sc, a_, :], identf, [xsok] + CK, [('pp', b)])
                                CP('act', stg[0:16, 0, sc * 128:(sc + 1) * 128], pp[b][0:16, 0:128], [('pp', b)], [stgk])
                            S.dma('sp', dst[l], stg[0:16, 0, :], reads=[stgk], is_output=True)
                    yv, yvk = FA(4, NTOKMAX)
                    t5, t5k = FA(1, NTOKMAX)
                    ygb, ygbk = BA(4, NTOKMAX)
                    for uc in range(4):
                        STT(yv[:, uc, 0:n], uf[:, uc, 0:n], pvt[:, PVO['s5_d'] + uc:PVO['s5_d'] + uc + 1], ys5[:, uc, 0:n], M, A, [ufk, ysk, 'pvt'], [yvk])
                        ACT(t5[:, 0, 0:n], yv[:, uc, 0:n], AF.Square, [yvk], [t5k])
                        TS('dve', t5[:, 0, 0:n], t5[:, 0, 0:n], 0.044715, M, [t5k], [t5k], s2=1.0, op1=A)
                        TT('dve', t5[:, 0, 0:n], t5[:, 0, 0:n], yv[:, uc, 0:n], M, [t5k, yvk], [t5k])
                        ACT(t5[:, 0, 0:n], t5[:, 0, 0:n], AF.Sigmoid, [t5k], [t5k], scale=1.5957691216057308)
                        TT('dve', yv[:, uc, 0:n], yv[:, uc, 0:n], t5[:, 0, 0:n], M, [t5k, yvk], [yvk])
                        CP('act', ygb[:, uc, 0:n], yv[:, uc, 0:n], [yvk], [ygbk])
                    for oc in range(4):
                        b = nextpp()
                        for kc in range(4):
                            MM(pp[b][:, 0:n], glub[:, kc, oc * 128:(oc + 1) * 128], ygb[:, kc, 0:n], ['glub', ygbk], [('pp', b)], start=(kc == 0), stop=(kc == 3), chain=True)
                        ACT(t5[:, 0, 0:n], pp[b][:, 0:n], AF.Sigmoid, [('pp', b), 'pvt'], [t5k], bias=pvt[:, PVO['s5_glu_b'] + oc:PVO['s5_glu_b'] + oc + 1])
                        TT('dve', og[:, 2, oc, 0:n], yv[:, oc, 0:n], t5[:, 0, 0:n], M, [yvk, t5k], ['og2'])
                if 'C' in PH:
                    prefetch_w('w_in', l, 3856, 8, 512)
                    prefetch_w('w_br_gla', l, 0, 4, 512)
                    phase()
                    mf, mfk = FA(8, NTOKMAX)
                    sg, sgk = FA(2, NTOKMAX)
                    brn = ['w_br_gla', 'w_br_rwkv', 'w_br_s5']
                    for br in range(3):
                        for half in range(2):
                            c0 = 3856 + br * 1024 + half * 512
                            wi = load_w('w_in', l, c0, 8, 512)
                            wj = load_w(brn[br], l, half * 512, 4, 512)
                            for j in range(4):
                                oc = half * 4 + j
                                b = nextpp()
                                proj(wi, 8, j * 128, 128, hb, HK, n, b)
                                ACT(sg[:, j % 2, 0:n], pp[b][:, 0:n], AF.Sigmoid, [('pp', b)], [(sgk, j % 2)])
                                b2 = nextpp()
                                proj(wj, 4, j * 128, 128, og[:, br], ['og%d' % br], n, b2)
                                if br == 0:
                                    TT('dve', mf[:, oc, 0:n], sg[:, j % 2, 0:n], pp[b2][:, 0:n], M, [(sgk, j % 2), ('pp', b2)], [(mfk, oc)])
                                else:
                                    TT('dve', sg[:, j % 2, 0:n], sg[:, j % 2, 0:n], pp[b2][:, 0:n], M, [(sgk, j % 2), ('pp', b2)], [(sgk, j % 2)])
                                    TT('pool', mf[:, oc, 0:n], mf[:, oc, 0:n], sg[:, j % 2, 0:n], A, [(sgk, j % 2), (mfk, oc)], [(mfk, oc)])
                    mb, mbk = BA(8, NTOKMAX)
                    for oc in range(8):
                        CP('act' if oc % 2 else 'dve', mb[:, oc, 0:n], mf[:, oc, 0:n], [(mfk, oc)], [mbk])
                    for half in range(2):
                        wi = load_w('w_out', l, half * 512, 8, 512)
                        for j in range(4):
                            oc = half * 4 + j
                            b = nextpp()
                            proj(wi, 8, j * 128, 128, mb, [mbk], n, b)
                            TT('dve', xb[:, oc, 0:n], xb[:, oc, 0:n], pp[b][:, 0:n], A, ['xb', ('pp', b)], ['xb'])
                    prefetch_w('ffn_w1', l, 0, 8, 512)
                    prefetch_w('ffn_w3', l, 0, 8, 512)
                    phase(soft=True)
                    rs, rk = rmsnorm(PVO['norm_ffn'], n)
                    for c in range(8):
                        STT(hb[:, c, 0:n], xb[:, c, 0:n], pvt[:, PVO['norm_ffn'] + c:PVO['norm_ffn'] + c + 1], rs[:, 0, 0:n], M, M,
                            ['xb', 'pvt', rk], ['hb'])
                    hid, hidk = BA(22, NTOKMAX)
                    sl, slk = FA(2, NTOKMAX)
                    for g in range(6):
                        c0 = g * 512
                        ncol = min(512, DFF - c0)
                        wi = load_w('ffn_w1', l, c0, 8, ncol)
                        wj = load_w('ffn_w3', l, c0, 8, ncol)
                        for j in range(ncol // 128):
                            hc = g * 4 + j
                            b = nextpp()
                            proj(wi, 8, j * 128, 128, hb, HK, n, b)
                            ACT(sl[:, j % 2, 0:n], pp[b][:, 0:n], AF.Silu, [('pp', b)], [(slk, j % 2)])
                            b2 = nextpp()
                            proj(wj, 8, j * 128, 128, hb, HK, n, b2)
                            TT('dve', hid[:, hc, 0:n], sl[:, j % 2, 0:n], pp[b2][:, 0:n], M, [(slk, j % 2), ('pp', b2)], [hidk])
                    for oc in range(8):
                        i = wt['i']
                        wt['i'] = (i + 1) % NWB
                        w2v = wbuf[i][:, :, :].rearrange("p a b -> p (a b)")[:, 0:22 * 128].rearrange("p (k m) -> p k m", k=22)
                        S.dma('sp', w2v, wb['ffn_w2'][l, :, oc * 128:(oc + 1) * 128].rearrange("(kc p) n -> p kc n", p=128),
                              reads=WK('ffn_w2', l), writes=[('wbuf', i)])
                        b = nextpp()
                        for kc in range(22):
                            MM(pp[b][:, 0:n], w2v[:, kc, :], hid[:, kc, 0:n], [('wbuf', i), hidk], [('pp', b)], start=(kc == 0), stop=(kc == 21), chain=True)
                        TT('dve', xb[:, oc, 0:n], xb[:, oc, 0:n], pp[b][:, 0:n], A, ['xb', ('pp', b)], ['xb'])
                    S.dma('sp', xres[:, :, t0:t0 + n], xb[:, :, 0:n], reads=['xb'], writes=['xres'])
                    if l == depth - 1:
                        rs, rk = rmsnorm(PVO['final_norm'], n)
                        yf, yfk = FA(8, NTOKMAX)
                        for c in range(8):
                            STT(yf[:, c, 0:n], xb[:, c, 0:n], pvt[:, PVO['final_norm'] + c:PVO['final_norm'] + c + 1], rs[:, 0, 0:n], M, M,
                                ['xb', 'pvt', rk], [yfk])
                        yt, ytk = FA(2, 1024)
                        otl = [(0, 128), (128, 128)] + ([(256, 32)] if last else [])
                        for ti, (off, L) in enumerate(otl):
                            for c in range(8):
                                b = nextpp()
                                TR(pp[b][0:L, 0:128], yf[:, c, off:off + L], identf, [yfk] + CK, [('pp', b)])
                                CP('act' if c % 2 else 'dve', yt[0:L, ti % 2, c * 128:(c + 1) * 128], pp[b][0:L, 0:128], [('pp', b)], [(ytk, ti % 2)])
                            g0 = t0 + off
                            lo, hi = max(g0, 16), min(g0 + L, NPR)
                            if hi > lo:
                                S.dma('sp', yp[lo - 16:hi - 16, :], yt[lo - g0:hi - g0, ti % 2, :], reads=[(ytk, ti % 2)], is_output=True)
                            lo2 = max(g0, NPR)
                            if g0 + L > lo2:
                                S.dma('sp', ys[lo2 - NPR:g0 + L - NPR, :], yt[lo2 - g0:L, ti % 2, :], reads=[(ytk, ti % 2)], is_output=True)
        S.finish(block)
        print("n_instr", S.n_instr, {e: len(v) for e, v in S.q.items()})
    return nc


def _fm(v, nch):
    return np.ascontiguousarray(np.asarray(v, np.float32).reshape(nch, 128).T)


def _consts():
    p = np.arange(128)[:, None]
    f = np.arange(128)[None, :]
    mu = (p < f).astype(np.float32)
    mui = (p <= f).astype(np.float32)
    mln = -(p > f).astype(np.float32)
    mun = -(p < f).astype(np.float32)
    ident = np.eye(128, dtype=np.float32)
    ones = np.ones((128, 128), np.float32)
    bones = np.zeros((128, 128), np.float32)
    bones[:64, :64] = 1
    bones[64:, 64:] = 1
    return np.ascontiguousarray(np.concatenate([mu, mui, mln, mun, ident, ones, bones], axis=1))


def _prep_shared(inp):
    L = 4
    pvs = np.zeros((L, 128, NV), np.float32)
    for l in range(L):
        def put(name, arr, nch):
            pvs[l, :, PVO[name]:PVO[name] + nch] = _fm(arr, nch)
        put('norm_mix', inp['norm_mix'][l], 8)
        put('norm_ffn', inp['norm_ffn'][l], 8)
        put('gla_gate_b', inp['gla_gate_b'][l], 2)
        put('gla_norm', inp['gla_norm'][l], 4)
        put('rwkv_mu', inp['rwkv_mu'][l], 14)
        put('rwkv_w0', inp['rwkv_w0'][l], 4)
        put('rwkv_a0', inp['rwkv_a0'][l], 4)
        put('rwkv_k_k', inp['rwkv_k_k'][l], 4)
        put('rwkv_k_a', inp['rwkv_k_a'][l], 4)
        put('rwkv_r_k', np.asarray(inp['rwkv_r_k'][l]).reshape(-1), 4)
        put('rwkv_ln_w', inp['rwkv_ln_w'][l], 4)
        put('rwkv_ln_b', inp['rwkv_ln_b'][l], 4)
        put('s5_d', inp['s5_d'][l], 4)
        put('s5_glu_b', inp['s5_glu_b'][l], 4)
        put('s5_are', np.asarray(inp['s5_a_re'][l]).reshape(-1), 16)
        put('s5_aim', np.asarray(inp['s5_a_im'][l]).reshape(-1), 16)
        put('s5_ldt', np.repeat(np.asarray(inp['s5_log_dt'][l]), 64), 16)
        put('final_norm', inp['final_norm'], 8)
    def bpad(B):
        B = np.asarray(B, np.float32)
        out = np.zeros((L, 128, 16, 128), np.float32)
        for sc in range(16):
            uc, q = sc // 4, sc % 4
            for g2 in range(2):
                g = 2 * sc + g2
                gl = 2 * q + g2
                out[:, gl * 16:(gl + 1) * 16, sc, g2 * 64:(g2 + 1) * 64] = np.transpose(B[:, g], (0, 2, 1))
        return out
    def cpad(C):
        C = np.asarray(C, np.float32)
        out = np.zeros((L, 128, 16, 128), np.float32)
        for sc in range(16):
            q = sc % 4
            for g2 in range(2):
                g = 2 * sc + g2
                gl = 2 * q + g2
                out[:, g2 * 64:(g2 + 1) * 64, sc, gl * 16:(gl + 1) * 16] = np.transpose(C[:, g], (0, 2, 1))
        return out
    sh = {
        'pv': pvs, 'consts': _consts(),
        'bpad_re': bpad(inp['s5_b_re']), 'bpad_im': bpad(inp['s5_b_im']),
        'cpad_re': cpad(inp['s5_c_re']), 'cpad_im': cpad(inp['s5_c_im']),
    }
    for k in ['gla_gate_w2', 'rwkv_w2', 'rwkv_a2', 'rwkv_g2', 's5_glu_w'] + [n for n, _, _ in BIGW]:
        sh[k] = np.ascontiguousarray(np.asarray(inp[k], np.float32))
    return sh


def kernel(_depth=4, _nblk=None, _cores=8, **inp):
    sh = _prep_shared(inp)
    xp = np.asarray(inp['x_prompt'], np.float32)
    xs = np.asarray(inp['x_sample'], np.float32)
    meta = np.asarray(inp['meta_tokens'], np.float32)
    in_maps = []
    for c in range(_cores):
        sl = slice(c * NS, (c + 1) * NS)
        m = dict(sh)
        m['xin'] = np.ascontiguousarray(np.concatenate([meta, xp[c], xs[sl, 0]], axis=0))
        m['sgla'] = np.ascontiguousarray(np.asarray(inp['state_gla'], np.float32)[:, sl])
        m['srw'] = np.ascontiguousarray(np.asarray(inp['state_rwkv'], np.float32)[:, sl])
        m['ssh'] = np.ascontiguousarray(np.asarray(inp['state_rwkv_shift'], np.float32)[:, sl])
        m['s5r'] = np.ascontiguousarray(np.asarray(inp['state_s5_re'], np.float32)[:, sl].reshape(4, NS, 2048))
        m['s5i'] = np.ascontiguousarray(np.asarray(inp['state_s5_im'], np.float32)[:, sl].reshape(4, NS, 2048))
        in_maps.append(m)
    nc = build_nc(_depth, _nblk)
    res = run_bass_kernel_spmd(nc, in_maps, core_ids=list(range(_cores)))
    R = res.results
    cat = lambda k, ax: np.concatenate([np.asarray(r[k], np.float32) for r in R], axis=ax)
    stk = lambda k: np.stack([np.asarray(r[k], np.float32) for r in R], axis=1)
    y_prompt = np.stack([np.asarray(r['yp'], np.float32) for r in R], axis=0)
    y_sample = cat('ys', 0).reshape(-1, 1, D)
    gla_p = stk('glap')
    rwkv_p = stk('rwp')
    shift_p = stk('shp').reshape(4, -1, 1792)
    s5re_p = stk('s5rp').reshape(4, -1, 32, 64)
    s5im_p = stk('s5ip').reshape(4, -1, 32, 64)
    gla_s = cat('glas', 1)
    rwkv_s = cat('rws', 1)
    shift_s = cat('shs', 1)
    s5re_s = cat('s5rs', 1).reshape(4, -1, 32, 64)
    s5im_s = cat('s5is', 1).reshape(4, -1, 32, 64)
    return (y_prompt, y_sample, gla_p, rwkv_p, shift_p, s5re_p, s5im_p, gla_s, rwkv_s, shift_s, s5re_s, s5im_s)
```

```python
import contextlib
import numpy as np
import concourse.bass as bass
import concourse.mybir as mybir
from concourse.bass_utils import run_bass_kernel_spmd
PH = set('RSC')
MARK = True
FP32R = False

F32 = mybir.dt.float32
BF16 = mybir.dt.bfloat16
AF = mybir.ActivationFunctionType
ALU = mybir.AluOpType

EPOCH = 30000
DEPOCH = 1500


class Sched:
    CE = ('pe', 'act', 'dve', 'pool')

    def __init__(self, nc, stack, n_dma_slots=12, n_bg=10, n_pf=4):
        self.nc = nc
        self.stack = stack
        self.eng = {'pe': nc.tensor, 'act': nc.scalar, 'dve': nc.vector,
                    'pool': nc.gpsimd, 'sp': nc.sync}
        self.q = {e: [] for e in self.eng}
        self.cnt = {e: 0 for e in self.CE}
        self.sems = {}
        self.seen = {e: {} for e in self.eng}
        self.last_w = {}
        self.readers = {}
        self.nslots = n_dma_slots
        self.slot_sems = {}
        self.slot_cnt = [0] * (n_dma_slots + n_bg)
        self.bgslots = list(range(n_dma_slots, n_dma_slots + n_bg))
        self.bgnext = 0
        self.slot_cnt += [0] * n_pf
        self.pfslots = list(range(n_dma_slots + n_bg, n_dma_slots + n_bg + n_pf))
        self.pfnext = 0
        self.qslots = {'sp': list(range(0, 6)), 'act': list(range(6, 8)), 'pool': list(range(8, n_dma_slots))}
        self.qnext = {'sp': 0, 'act': 0, 'pool': 0}
        self.out_dmas = []
        self.n_instr = 0

    def _sem(self, e, epoch):
        k = (e, epoch)
        if k not in self.sems:
            self.sems[k] = self.stack.enter_context(self.nc.semaphore(f"p_{e}_{epoch}"))
        return self.sems[k]

    def _slot_sem(self, s, ep):
        k = (s, ep)
        if k not in self.slot_sems:
            self.slot_sems[k] = self.stack.enter_context(self.nc.semaphore(f"dslot{s}_{ep}"))
        return self.slot_sems[k]

    def _wait(self, eng, src, c):
        if self.seen[eng].get(src, 0) >= c:
            return
        self.seen[eng][src] = c
        e = self.eng[eng]
        if isinstance(src, tuple):
            sem = self._slot_sem(src[1], (c - 1) // DEPOCH)
            val = ((c - 1) % DEPOCH + 1) * 16
        else:
            idx = c - 1
            sem = self._sem(src, idx // EPOCH)
            val = (idx % EPOCH) + 1
        self.q[eng].append(lambda e=e, sem=sem, val=val: e.wait_ge(sem, val))
        self.n_instr += 1

    def _deps(self, reads, writes):
        deps = set()
        for k in reads:
            w = self.last_w.get(k)
            if w:
                deps.add(w)
        for k in writes:
            w = self.last_w.get(k)
            if w:
                deps.add(w)
            for r in self.readers.get(k, ()):
                deps.add(r)
        return deps

    def _commit(self, me, reads, writes):
        for k in writes:
            self.last_w[k] = me
            self.readers[k] = []
        for k in reads:
            self.readers.setdefault(k, []).append(me)

    @staticmethod
    def _is_psum(k):
        return k in ('pn', 'ptb', 'py', 'pst') or (isinstance(k, tuple) and k[0] in ('pp', 'pc'))

    def op(self, eng, fn, reads=(), writes=(), skip_same=False):
        pr = [k for k in reads if self._is_psum(k)]
        if pr:
            writes = list(writes) + pr
            reads = [k for k in reads if not self._is_psum(k)]
        for (src, c) in sorted(self._deps(reads, writes), key=str):
            if skip_same and src == eng:
                continue
            self._wait(eng, src, c)
        self.cnt[eng] += 1
        c = self.cnt[eng]
        idx = c - 1
        sem = self._sem(eng, idx // EPOCH)
        self.q[eng].append(lambda fn=fn, sem=sem: fn().then_inc(sem, 1))
        self.n_instr += 1
        self._commit((eng, c), reads, writes)

    def dma(self, qeng, out, in_, reads=(), writes=(), is_output=False, bg=False, **kw):
        if bg == 'p':
            s = self.pfslots[self.pfnext % len(self.pfslots)]
            self.pfnext += 1
        elif bg:
            s = self.bgslots[self.bgnext % len(self.bgslots)]
            self.bgnext += 1
        else:
            sl = self.qslots[qeng]
            s = sl[self.qnext[qeng] % len(sl)]
            self.qnext[qeng] += 1
        if self.slot_cnt[s] > 0:
            self._wait(qeng, ('d', s), self.slot_cnt[s])
        for (src, c) in sorted(self._deps(reads, writes), key=str):
            self._wait(qeng, src, c)
        self.slot_cnt[s] += 1
        c = self.slot_cnt[s]
        e = self.eng[qeng]
        sem = self._slot_sem(s, (c - 1) // DEPOCH)
        self.q[qeng].append(lambda e=e, out=out, in_=in_, sem=sem, kw=kw: e.dma_start(out=out, in_=in_, **kw).then_inc(sem, 16))
        self.n_instr += 1
        me = (('d', s), c)
        self._commit(me, reads, writes)
        if is_output:
            self.out_dmas.append(me)

    def finish(self, block):
        for (src, c) in self.out_dmas:
            self.seen['sp'].pop(src, None) if self.seen['sp'].get(src, 0) < c else None
            self._wait('sp', src, c)
        q = self.q

        if q['sp']:
            @block.sync
            def _(e):
                for f in q['sp']:
                    f()
        if q['pe']:
            @block.tensor
            def _(e):
                for f in q['pe']:
                    f()
        if q['act']:
            @block.scalar
            def _(e):
                for f in q['act']:
                    f()
        if q['dve']:
            @block.vector
            def _(e):
                for f in q['dve']:
                    f()
        if q['pool']:
            @block.gpsimd
            def _(e):
                for f in q['pool']:
                    f()


def _barrier(self):
    for e in self.eng:
        for src in self.CE:
            if self.cnt[src] > 0:
                self._wait(e, src, self.cnt[src])
        for qn, slots in self.qslots.items():
            for s in slots:
                if self.slot_cnt[s] > 0:
                    self._wait(e, ('d', s), self.slot_cnt[s])


Sched.barrier = _barrier


NT = 2080
NPR = 2064
NS = 16
BLK = 256
NTOKMAX = 288
D = 1024
NIN = 6928
DFF = 2816

PVO = {}
_o = 0
for _n, _w in [('norm_mix', 8), ('norm_ffn', 8), ('gla_gate_b', 2), ('gla_norm', 4), ('rwkv_mu', 14),
               ('rwkv_w0', 4), ('rwkv_a0', 4), ('rwkv_k_k', 4), ('rwkv_k_a', 4), ('rwkv_r_k', 4),
               ('rwkv_ln_w', 4), ('rwkv_ln_b', 4), ('s5_d', 4), ('s5_glu_b', 4), ('s5_are', 16),
               ('s5_aim', 16), ('s5_ldt', 16), ('final_norm', 8)]:
    PVO[_n] = _o
    _o += _w
NV = _o
CO = {n: i * 128 for i, n in enumerate(['MU', 'MUI', 'MLn', 'MUn', 'ident', 'ones', 'bones'])}
NCONST = 7 * 128

BIGW = [('w_in', D, NIN), ('w_br_gla', 512, D), ('w_br_rwkv', 512, D), ('w_br_s5', 512, D), ('w_out', D, D),
        ('ffn_w1', D, DFF), ('ffn_w3', D, DFF), ('ffn_w2', DFF, D)]


def build_nc(depth=4, nblk=None):
    nc = bass.Bass("TRN2", target_bir_lowering=False)
    din = {}

    def I(name, shape, dt=F32):
        din[name] = nc.dram_tensor(name, shape, dt, kind="ExternalInput").ap()
        return din[name]

    def O(name, shape):
        return nc.dram_tensor(name, shape, F32, kind="ExternalOutput").ap()

    def SC(name, shape, dt):
        return nc.dram_tensor(name, shape, dt, kind="Internal").ap()

    xin = I("xin", [NT, D])
    sgla = I("sgla", [4, NS, 4, 64, 128])
    srw = I("srw", [4, NS, 8, 64, 64])
    ssh = I("ssh", [4, NS, 1792])
    s5r = I("s5r", [4, NS, 2048])
    s5i = I("s5i", [4, NS, 2048])
    pv = I("pv", [4, 128, NV])
    consts = I("consts", [128, NCONST])
    gw2 = I("gla_gate_w2", [4, 16, 256])
    rw2 = I("rwkv_w2", [4, 64, 512])
    ra2 = I("rwkv_a2", [4, 64, 512])
    rg2 = I("rwkv_g2", [4, 128, 512])
    gluw = I("s5_glu_w", [4, 512, 512])
    bpr = I("bpad_re", [4, 128, 16, 128])
    bpi = I("bpad_im", [4, 128, 16, 128])
    cpr = I("cpad_re", [4, 128, 16, 128])
    cpi = I("cpad_im", [4, 128, 16, 128])
    wf = {n: I(n, [4, r, c]) for n, r, c in BIGW}
    wb = {n: SC(n + "_b", [4, r, c], BF16) for n, r, c in BIGW}
    xres = SC("xres", [128, 8, NT], F32)
    tabd = SC("tabd", [128, 4, 16, 128], F32)

    yp = O("yp", [2048, D])
    ys = O("ys", [NS, D])
    glap = O("glap", [4, 4, 64, 128])
    rwp = O("rwp", [4, 8, 64, 64])
    shp = O("shp", [4, 14, 128])
    s5rp = O("s5rp", [4, 16, 128])
    s5ip = O("s5ip", [4, 16, 128])
    glas = O("glas", [4, NS, 4, 64, 128])
    rws = O("rws", [4, NS, 8, 64, 64])
    shs = O("shs", [4, NS, 1792])
    s5rs = O("s5rs", [4, NS, 2048])
    s5is = O("s5is", [4, NS, 2048])

    st = contextlib.ExitStack()
    with st:
        S = Sched(nc, st)
        T = lambda name, shape, dt=F32: st.enter_context(nc.sbuf_tensor(name, shape, dt))
        PS = lambda name, shape, dt=F32: st.enter_context(nc.psum_tensor(name, shape, dt))

        def ACT(out, in_, func, R, W, bias=None, scale=None):
            kw = {}
            if bias is not None:
                kw['bias'] = bias
            if scale is not None:
                kw['scale'] = scale
            S.op('act', lambda: nc.scalar.activation(out=out, in_=in_, func=func, **kw), R, W)

        def EN(eng):
            return nc.vector if eng == 'dve' else nc.gpsimd

        def TT(eng, out, a, b, op, R, W):
            S.op(eng, lambda: EN(eng).tensor_tensor(out=out, in0=a, in1=b, op=op), R, W)

        def TS(eng, out, a, s1, op0, R, W, s2=None, op1=None):
            if op1 is None:
                S.op(eng, lambda: EN(eng).tensor_scalar(out=out, in0=a, scalar1=s1, scalar2=None, op0=op0), R, W)
            else:
                S.op(eng, lambda: EN(eng).tensor_scalar(out=out, in0=a, scalar1=s1, scalar2=s2, op0=op0, op1=op1), R, W)

        def STT(out, a, s, b, op0, op1, R, W):
            S.op('dve', lambda: nc.vector.scalar_tensor_tensor(out=out, in0=a, scalar=s, in1=b, op0=op0, op1=op1), R, W)

        def CP(eng, out, in_, R, W):
            if eng == 'act':
                S.op('act', lambda: nc.scalar.copy(out=out, in_=in_), R, W)
            else:
                S.op(eng, lambda: EN(eng).tensor_copy(out=out, in_=in_), R, W)

        def MM(out, lhsT, rhs, R, W, start=True, stop=True, chain=False):
            S.op('pe', lambda: nc.tensor.matmul(out, lhsT=lhsT, rhs=rhs, start=start, stop=stop), R, W, skip_same=chain)

        def TR(out, in_, ident, R, W):
            S.op('pe', lambda: nc.tensor.transpose(out=out, in_=in_, identity=ident), R, W)

        def SCAN(out, d0, d1, init, R, W):
            S.op('dve', lambda: nc.vector.tensor_tensor_scan(out=out, data0=d0, data1=d1, initial=init, op0=ALU.mult, op1=ALU.add), R, W)

        def RECIP(out, in_, R, W):
            S.op('dve', lambda: nc.vector.reciprocal(out=out, in_=in_), R, W)

        def MSET(eng, ap, v, W):
            S.op(eng, lambda: EN(eng).memset(ap, v), (), W)

        M, A, SUB = ALU.mult, ALU.add, ALU.subtract

        def RR(ap):
            return ap.bitcast(mybir.dt.float32r) if FP32R else ap

        cst = T("cst", [128, NCONST])
        cstb = T("cstb", [128, 128], BF16)
        pvt = T("pvt", [128, NV])
        pvd = T("pvd", [128, 8])
        gw2b = T("gw2b", [128, 256], BF16)
        w2b = T("w2b", [128, 512], BF16)
        a2b = T("a2b", [128, 512], BF16)
        g2b = T("g2b", [128, 512], BF16)
        glub = T("glub", [128, 4, 512], BF16)
        xb = T("xb", [128, 8, NTOKMAX])
        hb = T("hb", [128, 8, NTOKMAX], BF16)
        og = T("og", [128, 3, 4, NTOKMAX], BF16)
        Sg_fs = [T(f"Sg_f{i}", [128, 2, 128]) for i in range(3)]
        Sg_bs = [T(f"Sg_b{i}", [128, 2, 128], BF16) for i in range(3)]
        Sr_fs = [T(f"Sr_f{i}", [128, 4, 64]) for i in range(3)]
        Sr_bs = [T(f"Sr_b{i}", [128, 4, 64], BF16) for i in range(3)]
        x5 = T("x5", [128, 16, 2])
        shprev = T("shprev", [128, 14, 1])
        NWB = 4
        wbuf = [T(f"wbuf{i}", [128, 8, 512], BF16) for i in range(NWB)]
        RF = T("RF", [128, 26000])
        RB = T("RB", [128, 14200], BF16)
        pp = [PS(f"pp{i}", [128, 512]) for i in range(2)]
        pn = PS("pn", [128, 512])
        ptb = PS("ptb", [128, 1024], BF16)
        pc = [PS(f"pc{i}", [128, 512]) for i in range(2)]
        py = PS("py", [128, 512])
        pst = PS("pst", [128, 512])
        block = st.enter_context(nc.Block())

        MU = cst[:, CO['MU']:CO['MU'] + 128]
        MUI = cst[:, CO['MUI']:CO['MUI'] + 128]
        MLn = cst[:, CO['MLn']:CO['MLn'] + 128]
        MUn = cst[:, CO['MUn']:CO['MUn'] + 128]
        identf = cst[:, CO['ident']:CO['ident'] + 128]
        onesf = cst[:, CO['ones']:CO['ones'] + 128]
        bonesf = cst[:, CO['bones']:CO['bones'] + 128]
        CK = ['cst']

        al = {'f': 0, 'b': 0, 'n': 0}

        mark = T("mark", [128, 2])

        def phase(soft=False):
            if not soft:
                S.barrier()
                al['f'] = 0
                al['b'] = 0
            if MARK:
                S.op('pool', lambda: nc.gpsimd.memset(mark[:, 0:1], 1.0), (), ['mark'])

        def FA(c, t):
            o = al['f']
            al['f'] += c * t
            assert al['f'] <= 25900, al['f']
            al['n'] += 1
            return RF[:, o:o + c * t].rearrange("p (c t) -> p c t", c=c), f"F{al['n']}"

        def BA(c, t):
            o = al['b']
            al['b'] += c * t
            assert al['b'] <= 14200, al['b']
            al['n'] += 1
            return RB[:, o:o + c * t].rearrange("p (c t) -> p c t", c=c), f"B{al['n']}"

        wt = {'i': 0, 'j': 0, 'p': 0, 'c': 0}

        WROWS = {n: r for n, r, c in BIGW}

        def WK(name, l):
            return [('wb', name, l, r0) for r0 in range(0, WROWS[name], 128)]

        pref = {}

        def prefetch_w(name, l, c0, nk, ncols):
            i = wt['i']
            wt['i'] = (i + 1) % NWB
            src = wb[name][l, :, c0:c0 + ncols]
            S.dma('sp', wbuf[i][:, 0:nk, 0:ncols], src.rearrange("(kc p) n -> p kc n", p=128), reads=WK(name, l), writes=[('wbuf', i)], bg='p')
            pref[(name, l, c0, ncols)] = i

        def load_w(name, l, c0, nk, ncols):
            if (name, l, c0, ncols) in pref:
                return pref.pop((name, l, c0, ncols))
            i = wt['i']
            wt['i'] = (i + 1) % NWB
            src = wb[name][l, :, c0:c0 + ncols]
            S.dma('sp', wbuf[i][:, 0:nk, 0:ncols], src.rearrange("(kc p) n -> p kc n", p=128), reads=WK(name, l), writes=[('wbuf', i)])
            return i

        def convert_list(l):
            return [(n, l, r0) for n, r, c in BIGW for r0 in range(0, r, 128)]

        def issue_convert(items):
            for (n, l, r0) in items:
                S.dma('pool', wb[n][l, r0:r0 + 128, :], wf[n][l, r0:r0 + 128, :], writes=[('wb', n, l, r0)], bg=True)

        def nextpp():
            wt['p'] ^= 1
            return wt['p']

        def nextpc():
            wt['c'] ^= 1
            return wt['c']

        def proj(wi, nk, c0, m, rhs, rk, n, psb, prow=None):
            for kc in range(nk):
                MM(pp[psb][0:m, 0:n], wbuf[wi][:, kc, c0:c0 + m], rhs[:, kc, 0:n], [('wbuf', wi)] + rk, [('pp', psb)],
                   start=(kc == 0), stop=(kc == nk - 1), chain=True)

        S.dma('sp', cst[:], consts, writes=CK)
        CP('dve', cstb[:], identf, CK, ['cstb'])
        issue_convert(convert_list(0))
        tiles_all = [(t0, 128) for t0 in range(0, 2048, 128)] + [(2048, 32)]
        for (t0, L) in tiles_all:
            xt, xk = RF[:, 0:1024], 'xt0'
            S.dma('sp', xt[0:L, :], xin[t0:t0 + L, :], writes=[xk])
            for c in range(8):
                b = nextpp()
                TR(pp[b][:, 0:L], xt[0:L, c * 128:(c + 1) * 128], identf[0:L, 0:L], [xk] + CK, [('pp', b)])
                CP('act' if c % 2 else 'dve', xb[:, c, 0:L], pp[b][:, 0:L], [('pp', b)], ['xb'])
            S.dma('sp', xres[:, :, t0:t0 + L], xb[:, :, 0:L], reads=['xb'], writes=['xres'])
        S.barrier()

        def rmsnorm(goff, n):
            sq, sk = FA(2, NTOKMAX)
            rs, rk = FA(1, NTOKMAX)
            for c in range(8):
                ACT(sq[:, c % 2, 0:n], xb[:, c, 0:n], AF.Square, ['xb'], [(sk, c % 2)])
                MM(pn[:, 0:n], onesf, sq[:, c % 2, 0:n], [(sk, c % 2)] + CK, ['pn'], start=(c == 0), stop=(c == 7), chain=True)
            ACT(rs[:, 0, 0:n], pn[:, 0:n], AF.Sqrt, ['pn'], [rk], bias=1e-6, scale=1.0 / 1024)
            RECIP(rs[:, 0, 0:n], rs[:, 0, 0:n], [rk], [rk])
            return rs, rk

        nb_all = 8 if nblk is None else nblk
        for l in range(depth):
            phase()
            S.dma('sp', pvt[:], pv[l], writes=['pvt'])
            S.dma('pool', gw2b[0:16, :], gw2[l], writes=['gw2b'])
            S.dma('pool', w2b[0:64, :], rw2[l], writes=['w2b'])
            S.dma('pool', a2b[64:128, :], ra2[l], writes=['a2b'])
            S.dma('pool', g2b[:], rg2[l], writes=['g2b'])
            S.dma('pool', glub[:], gluw[l].rearrange("(kc p) n -> p kc n", p=128), writes=['glub'])
            PK = ['pvt', 'pvd']
            TS('dve', pvd[:, 0:2], pvt[:, PVO['gla_gate_b']:PVO['gla_gate_b'] + 2], -1.0, M, ['pvt'], ['pvd'])
            TS('dve', pvd[:, 2:6], pvt[:, PVO['rwkv_k_a']:PVO['rwkv_k_a'] + 4], -1.0, M, ['pvt'], ['pvd'], s2=1.0, op1=A)
            c5, c5k = FA(16, 16)
            K5 = [c5k]
            are = pvt[:, PVO['s5_are']:PVO['s5_are'] + 16]
            aim = pvt[:, PVO['s5_aim']:PVO['s5_aim'] + 16]
            ldt = pvt[:, PVO['s5_ldt']:PVO['s5_ldt'] + 16]
            dt_, mag, th, cs, sn, abr, abi, den, nr, cre, cim, t1, t2, iar, iai = [c5[:, i, :] for i in range(15)]
            ACT(dt_, ldt, AF.Exp, ['pvt'], K5)
            TT('dve', mag, are, dt_, M, ['pvt'] + K5, K5)
            ACT(mag, mag, AF.Exp, K5, K5)
            TT('dve', th, aim, dt_, M, ['pvt'] + K5, K5)
            pi = float(np.pi)
            ACT(sn, th, AF.Sin, K5, K5, scale=1.0 / 32)
            ACT(cs, th, AF.Sin, K5, K5, scale=1.0 / 32, bias=pi / 2)
            for _ in range(5):
                TT('dve', t1, sn, cs, M, K5, K5)
                TT('dve', t2, sn, sn, M, K5, K5)
                TS('dve', sn, t1, 2.0, M, K5, K5)
                TS('dve', cs, t2, -2.0, M, K5, K5, s2=1.0, op1=A)
            TT('dve', abr, mag, cs, M, K5, K5)
            TT('dve', abi, mag, sn, M, K5, K5)
            TT('dve', t1, are, are, M, ['pvt'] + K5, K5)
            TT('dve', t2, aim, aim, M, ['pvt'] + K5, K5)
            TT('dve', den, t1, t2, A, K5, K5)
            RECIP(den, den, K5, K5)
            TS('dve', nr, abr, -1.0, A, K5, K5)
            TT('dve', t1, nr, are, M, ['pvt'] + K5, K5)
            TT('dve', t2, abi, aim, M, ['pvt'] + K5, K5)
            TT('dve', t1, t1, t2, A, K5, K5)
            TT('dve', cre, t1, den, M, K5, K5)
            TT('dve', t1, abi, are, M, ['pvt'] + K5, K5)
            TT('dve', t2, nr, aim, M, ['pvt'] + K5, K5)
            TT('dve', t1, t1, t2, SUB, K5, K5)
            TT('dve', cim, t1, den, M, K5, K5)
            TT('dve', t1, abr, abr, M, K5, K5)
            TT('dve', t2, abi, abi, M, K5, K5)
            TT('dve', t1, t1, t2, A, K5, K5)
            RECIP(t1, t1, K5, K5)
            TT('dve', iar, abr, t1, M, K5, K5)
            TT('dve', iai, abi, t1, M, K5, K5)
            TS('dve', iai, iai, -1.0, M, K5, K5)
            s5c = T(f"s5c{l}", [128, 4, 16])
            for i, src in enumerate([abr, abi, cre, cim]):
                CP('dve', s5c[:, i, :], src, K5, ['s5c'])
            tab, tbk = FA(4 * 16, 128)
            tabv = tab.rearrange("p (a s) j -> p a s j", a=4)

            tA, tAk = FA(16, 128)
            tB, tBk = FA(16, 128)
            tC, tCk = FA(16, 128)
            for a_, src in enumerate([abr, abi, iar, iai]):
                CP('dve', tabv[:, a_, :, 0], src, K5, [tbk])

            def cmul_all(ore, oim, ire, iim, sre, sim, m):
                TT('dve', tA[:, :, 0:m], iim, sim, M, [tbk] + K5, [tAk])
                TT('pool', tB[:, :, 0:m], ire, sre, M, [tbk] + K5, [tBk])
                TT('dve', tC[:, :, 0:m], tB[:, :, 0:m], tA[:, :, 0:m], SUB, [tAk, tBk], [tCk])
                TT('dve', tA[:, :, 0:m], ire, sim, M, [tbk] + K5, [tAk])
                TT('pool', tB[:, :, 0:m], iim, sre, M, [tbk] + K5, [tBk])
                TT('dve', oim, tB[:, :, 0:m], tA[:, :, 0:m], A, [tAk, tBk], [tbk])
                CP('pool', ore, tC[:, :, 0:m], [tCk], [tbk])

            m = 1
            while m < 128:
                for a_ in (0, 2):
                    cmul_all(tabv[:, a_, :, m:2 * m], tabv[:, a_ + 1, :, m:2 * m], tabv[:, a_, :, 0:m], tabv[:, a_ + 1, :, 0:m],
                             tabv[:, a_, :, m - 1:m].broadcast_to([128, 16, m]), tabv[:, a_ + 1, :, m - 1:m].broadcast_to([128, 16, m]), m)
                m *= 2
            cmul_all(tabv[:, 2, :, :], tabv[:, 3, :, :], tabv[:, 2, :, :], tabv[:, 3, :, :],
                     cre.unsqueeze(2).broadcast_to([128, 16, 128]), cim.unsqueeze(2).broadcast_to([128, 16, 128]), 128)
            S.dma('sp', tabd.rearrange("p a s j -> p (a s) j"), tab, reads=[tbk], writes=['tabd'])
            MSET('dve', Sg_fs[0][:], 0.0, [('Sg_f', 0)])
            MSET('dve', Sg_bs[0][:], 0.0, [('Sg_b', 0)])
            MSET('dve', Sr_fs[0][:], 0.0, [('Sr_f', 0)])
            MSET('dve', Sr_bs[0][:], 0.0, [('Sr_b', 0)])
            MSET('dve', x5[:], 0.0, [('x5', s_) for s_ in range(16)])
            MSET('dve', shprev[:], 0.0, ['shprev'])
            cvt_next = convert_list(l + 1) if l + 1 < depth else []

            for bi in range(nb_all):
                t0 = bi * BLK
                last = (bi == 7)
                n = 288 if last else 256
                tiles = [(0, 128, None), (128, 128, None)]
                if last:
                    tiles += [(256, 16, None)] + [(272 + s, 1, s) for s in range(NS)]
                npr = 272 if last else 256

                phase()
                ncv = (len(cvt_next) + nb_all - 1) // nb_all
                issue_convert(cvt_next[bi * ncv:(bi + 1) * ncv])
                S.dma('sp', xb[:, :, 0:n], xres[:, :, t0:t0 + n], reads=['xres'], writes=['xb'])
                rs, rk = rmsnorm(PVO['norm_mix'], n)
                for c in range(8):
                    STT(hb[:, c, 0:n], xb[:, c, 0:n], pvt[:, PVO['norm_mix'] + c:PVO['norm_mix'] + c + 1], rs[:, 0, 0:n], M, M,
                        ['xb', 'pvt', rk], ['hb'])
                HK = ['hb']

                prefetch_w('w_in', l, 0, 8, 512)
                prefetch_w('w_in', l, 512, 8, 512)
                phase()
                qk, qkk = FA(4, NTOKMAX)
                vbf, vk = BA(4, NTOKMAX)
                gg, ggk = FA(4, NTOKMAX)
                zg, zgk = BA(1, NTOKMAX)
                la, lak = FA(2, NTOKMAX)
                of, ofk = FA(4, NTOKMAX)
                wi = load_w('w_in', l, 0, 8, 512)
                for j in range(4):
                    b = nextpp()
                    proj(wi, 8, j * 128, 128, hb, HK, n, b)
                    CP('act', qk[:, j, 0:n], pp[b][:, 0:n], [('pp', b)], [qkk])
                wi = load_w('w_in', l, 512, 8, 512)
                for j in range(4):
                    b = nextpp()
                    proj(wi, 8, j * 128, 128, hb, HK, n, b)
                    CP('act', vbf[:, j, 0:n], pp[b][:, 0:n], [('pp', b)], [vk])
                wi = load_w('w_in', l, 1024, 8, 512)
                for j in range(4):
                    b = nextpp()
                    proj(wi, 8, j * 128, 128, hb, HK, n, b)
                    ACT(gg[:, j, 0:n], pp[b][:, 0:n], AF.Silu, [('pp', b)], [ggk])
                wi = load_w('w_in', l, 1536, 8, 16)
                b = nextpp()
                proj(wi, 8, 0, 16, hb, HK, n, b)
                CP('act', zg[0:16, 0, 0:n], pp[b][0:16, 0:n], [('pp', b)], [zgk])
                for c in range(2):
                    b = nextpp()
                    MM(pp[b][:, 0:n], gw2b[0:16, c * 128:(c + 1) * 128], zg[0:16, 0, 0:n], ['gw2b', zgk], [('pp', b)])
                    ACT(la[:, c, 0:n], pp[b][:, 0:n], AF.Exp, [('pp', b)] + PK, [lak], bias=pvd[:, c:c + 1], scale=-1.0)
                    ACT(la[:, c, 0:n], la[:, c, 0:n], AF.Ln, [lak], [lak], bias=1.0)
                    TS('dve', la[:, c, 0:n], la[:, c, 0:n], -1.0 / 16, M, [lak], [lak])
                Bc, Bk = FA(2, 128)
                Ep, Epk = FA(2, 128)
                Em, Emk = FA(2, 128)
                tmpS, tmpk = FA(2, 128)
                qt, qtk = BA(2, 128)
                kt, ktk = BA(2, 128)
                Ktm, Ktk = BA(1, 256)
                Vtm, Vtk = BA(1, 512)
                STb, STk = BA(4, 128)
                for (off, L, smp) in tiles:
                    sp_ = 0 if smp is None else 1 + (smp % 2)
                    Sg_f, Sg_b, SgFk, SgBk = Sg_fs[sp_], Sg_bs[sp_], ('Sg_f', sp_), ('Sg_b', sp_)
                    if smp is not None:
                        S.dma('sp', Sg_f[:], sgla[l, smp].rearrange("(c hp) d e -> (hp d) c e", hp=2), writes=[SgFk])
                        CP('dve', Sg_b[:], Sg_f[:], [SgFk], [SgBk])
                    for c in range(2):
                        SCAN(Bc[:, c, 0:L], onesf[:, 0:L], la[:, c, off:off + L], 0.0, [lak] + CK, [Bk])
                    ACT(Ep[:, :, 0:L], Bc[:, :, 0:L], AF.Exp, [Bk], [Epk])
                    ACT(Em[:, :, 0:L], Bc[:, :, 0:L], AF.Exp, [Bk], [Emk], scale=-1.0)
                    STT(qt[:, :, 0:L], qk[:, 0:2, off:off + L], 0.125, Ep[:, :, 0:L], M, M, [qkk, Epk], [qtk])
                    TT('dve', kt[:, :, 0:L], qk[:, 2:4, off:off + L], Em[:, :, 0:L], M, [qkk, Emk], [ktk])
                    for c in range(2):
                        TR(ptb[0:L, c * 128:(c + 1) * 128], kt[:, c, 0:L], cstb[:], [ktk, 'cstb'], ['ptb'])
                    CP('act', Ktm[0:L, 0, :], ptb[0:L, 0:256], ['ptb'], [Ktk])
                    for hh in range(4):
                        TR(ptb[0:L, 256 + hh * 128:256 + (hh + 1) * 128], vbf[:, hh, off:off + L], cstb[:], [vk, 'cstb'], ['ptb'])
                    CP('dve', Vtm[0:L, 0, :], ptb[0:L, 256:768], ['ptb'], [Vtk])
                    gbanks = [(pc[0], ('pc', 0)), (pc[1], ('pc', 1)), (pp[0], ('pp', 0)), (pp[1], ('pp', 1))]
                    obanks = [(py, 'py'), (pn, 'pn')]
                    for h in range(4):
                        c, hp = h // 2, h % 2
                        P = slice(64 * hp, 64 * hp + 64)
                        t_, k_ = gbanks[h]
                        MM(t_[0:L, 0:L], kt[P, c, 0:L], qt[P, c, 0:L], [ktk, qtk], [k_])
                        TT('dve', STb[0:L, h, 0:L], t_[0:L, 0:L], MUI[0:L, 0:L], M, [k_] + CK, [(STk, h)])
                    for h in range(4):
                        c, hp = h // 2, h % 2
                        P = slice(64 * hp, 64 * hp + 64)
                        ob, obk = obanks[h % 2]
                        MM(ob[:, 0:L], Sg_b[P, c, :], qt[P, c, 0:L], [SgBk, qtk], [obk], start=True, stop=False)
                        MM(ob[:, 0:L], Vtm[0:L, 0, h * 128:(h + 1) * 128], STb[0:L, h, 0:L], [Vtk, (STk, h)], [obk], start=False, stop=True)
                        CP('act' if h % 2 else 'dve', of[:, h, off:off + L], ob[:, 0:L], [obk], [ofk])
                    for h in range(4):
                        c, hp = h // 2, h % 2
                        P = slice(64 * hp, 64 * hp + 64)
                        MM(pst[P, c * 128:(c + 1) * 128], Ktm[0:L, 0, c * 128 + 64 * hp:c * 128 + 64 * hp + 64], Vtm[0:L, 0, h * 128:(h + 1) * 128],
                           [Ktk, Vtk], ['pst'])
                    for c in range(2):
                        TT('dve', tmpS[:, c, :], Sg_f[:, c, :], pst[:, c * 128:(c + 1) * 128], A, [SgFk, 'pst'], [tmpk])
                        TS('dve', Sg_f[:, c, :], tmpS[:, c, :], Ep[:, c, L - 1:L], M, [tmpk, Epk], [SgFk])
                    CP('act', Sg_b[:], Sg_f[:], [SgFk], [SgBk])
                    if smp is not None:
                        S.dma('sp', glas[l, smp].rearrange("(c hp) d e -> (hp d) c e", hp=2), Sg_f[:], reads=[SgFk], is_output=True)
                    elif last and off == 256:
                        S.dma('sp', glap[l].rearrange("(c hp) d e -> (hp d) c e", hp=2), Sg_f[:], reads=[SgFk], is_output=True)
                sq, sk = FA(1, NTOKMAX)
                r2, r2k = FA(1, NTOKMAX)
                for h in range(4):
                    ACT(sq[:, 0, 0:n], of[:, h, 0:n], AF.Square, [ofk], [sk])
                    MM(pn[:, 0:n], onesf, sq[:, 0, 0:n], [sk] + CK, ['pn'])
                    ACT(r2[:, 0, 0:n], pn[:, 0:n], AF.Sqrt, ['pn'], [r2k], bias=1e-6, scale=1.0 / 128)
                    RECIP(r2[:, 0, 0:n], r2[:, 0, 0:n], [r2k], [r2k])
                    STT(sq[:, 0, 0:n], of[:, h, 0:n], pvt[:, PVO['gla_norm'] + h:PVO['gla_norm'] + h + 1], r2[:, 0, 0:n], M, M, [ofk, r2k, 'pvt'], [sk])
                    TT('dve', og[:, 0, h, 0:n], sq[:, 0, 0:n], gg[:, h, 0:n], M, [sk, ggk], ['og0'])
                if 'R' in PH:
                    prefetch_w('w_in', l, 1552, 8, 512)
                    prefetch_w('w_in', l, 2064, 8, 512)
                    phase()
                    xs, xsk = FA(14, NTOKMAX)
                    pbufs = [FA(1, NTOKMAX + 2) for _ in range(2)]
                    dds = [FA(1, NTOKMAX) for _ in range(2)]
                    stg, stgk = FA(1, 1792)
                    if last:
                        shin, shink = FA(14, 16)
                        pso, psok = FA(14, 16)
                        S.dma('sp', stg[0:16, 0, :], ssh[l], writes=[stgk])
                        for cc in range(14):
                            b = nextpp()
                            TR(pp[b][:, 0:16], stg[0:16, 0, cc * 128:(cc + 1) * 128], identf[0:16, 0:16], [stgk] + CK, [('pp', b)])
                            CP('act', shin[:, cc, :], pp[b][:, 0:16], [('pp', b)], [shink])
                    mu0 = PVO['rwkv_mu']
                    for gi, (c0, ncol) in enumerate([(1552, 512), (2064, 512), (2576, 512), (3088, 256)]):
                        wi = load_w('w_in', l, c0, 8, ncol)
                        for j in range(ncol // 128):
                            cc = gi * 4 + j
                            pbuf, pbk = pbufs[cc % 2]
                            dd, ddk = dds[cc % 2]
                            b = nextpp()
                            proj(wi, 8, j * 128, 128, hb, HK, n, b)
                            CP('act', pbuf[:, 0, 1:n + 1], pp[b][:, 0:n], [('pp', b)], [pbk])
                            CP('dve', pbuf[:, 0, 0:1], shprev[:, cc, :], ['shprev'], [pbk])
                            TT('dve', dd[:, 0, 0:npr], pbuf[:, 0, 0:npr], pbuf[:, 0, 1:npr + 1], SUB, [pbk], [ddk])
                            if last:
                                TT('dve', dd[:, 0, 272:288], shin[:, cc, :], pbuf[:, 0, 273:289], SUB, [pbk, shink], [ddk])
                                CP('pool', pso[:, cc, :], pbuf[:, 0, 273:289], [pbk], [psok])
                            STT(xs[:, cc, 0:n], dd[:, 0, 0:n], pvt[:, mu0 + cc:mu0 + cc + 1], pbuf[:, 0, 1:n + 1], M, A, [ddk, pbk, 'pvt'], [xsk])
                            CP('pool', shprev[:, cc, :], pbuf[:, 0, npr:npr + 1], [pbk], ['shprev'])
                    if last:
                        b = nextpp()
                        TR(pp[b][0:14, 0:128], shprev[:, :, 0], identf, ['shprev'] + CK, [('pp', b)])
                        CP('act', stg[0:14, 0, 0:128], pp[b][0:14, 0:128], [('pp', b)], [stgk])
                        S.dma('sp', shp[l], stg[0:14, 0, 0:128], reads=[stgk], is_output=True)
                        for cc in range(14):
                            b = nextpp()
                            TR(pp[b][0:16, 0:128], pso[:, cc, :], identf, [psok] + CK, [('pp', b)])
                            CP('act', stg[0:16, 0, cc * 128:(cc + 1) * 128], pp[b][0:16, 0:128], [('pp', b)], [stgk])
                        S.dma('sp', shs[l], stg[0:16, 0, :], reads=[stgk], is_output=True)
                    tzw, tzk = BA(1, NTOKMAX)
                    zab, zak = BA(1, NTOKMAX)
                    szg, szk = BA(1, NTOKMAX)
                    ACT(tzw[0:64, 0, 0:n], xs[0:64, 12, 0:n], AF.Tanh, [xsk], [tzk])
                    CP('dve', zab[64:128, 0, 0:n], xs[64:128, 12, 0:n], [xsk], [zak])
                    ACT(szg[:, 0, 0:n], xs[:, 13, 0:n], AF.Sigmoid, [xsk], [szk])
                    lw, lwk = FA(4, NTOKMAX)
                    kk, kkk = FA(4, NTOKMAX)
                    bb, bbk = FA(4, NTOKMAX)
                    gq, gqk = FA(4, NTOKMAX)
                    bon, bonk = FA(4, NTOKMAX)
                    aa, aak = FA(4, NTOKMAX)
                    t1, t1k = FA(4, NTOKMAX)
                    t2, t2k = FA(4, NTOKMAX)
                    rbs = [(pp[0], ('pp', 0)), (pp[1], ('pp', 1)), (pc[0], ('pc', 0)), (pc[1], ('pc', 1))]
                    rbn = [(pn, 'pn'), (py, 'py'), (pst, 'pst'), (pc[0], ('pc', 0))]
                    C4 = range(4)
                    csl = lambda c: slice(c * 128, (c + 1) * 128)
                    pcol = lambda nm, c: pvt[:, PVO[nm] + c:PVO[nm] + c + 1]
                    for c in C4:
                        t_, k_ = rbs[c]
                        MM(t_[:, 0:n], w2b[0:64, csl(c)], tzw[0:64, 0, 0:n], ['w2b', tzk], [k_])
                        ACT(lw[:, c, 0:n], t_[:, 0:n], AF.Sigmoid, [k_, 'pvt'], [(lwk, c)], bias=pcol('rwkv_w0', c))
                    for c in C4:
                        t_, k_ = rbs[c]
                        MM(t_[:, 0:n], a2b[64:128, csl(c)], zab[64:128, 0, 0:n], ['a2b', zak], [k_])
                        ACT(aa[:, c, 0:n], t_[:, 0:n], AF.Sigmoid, [k_, 'pvt'], [(aak, c)], bias=pcol('rwkv_a0', c))
                    for c in C4:
                        t_, k_ = rbs[c]
                        MM(t_[:, 0:n], g2b[:, csl(c)], szg[:, 0, 0:n], ['g2b', szk], [k_])
                        CP('act', gq[:, c, 0:n], t_[:, 0:n], [k_], [gqk])
                    for c in C4:
                        TS('pool', lw[:, c, 0:n], lw[:, c, 0:n], -0.6065306597126334, M, [(lwk, c)], [(lwk, c)])
                        TS('dve', kk[:, c, 0:n], xs[:, 4 + c, 0:n], pcol('rwkv_k_k', c), M, [xsk, 'pvt'], [(kkk, c)])
                    for c in C4:
                        ACT(t1[:, c, 0:n], kk[:, c, 0:n], AF.Square, [(kkk, c)], [(t1k, c)])
                    for c in C4:
                        t_, k_ = rbn[c]
                        MM(t_[:, 0:n], bonesf, t1[:, c, 0:n], [(t1k, c)] + CK, [k_])
                        ACT(t1[:, c, 0:n], t_[:, 0:n], AF.Sqrt, [k_], [(t1k, c)], bias=1e-12)
                    for c in C4:
                        RECIP(t1[:, c, 0:n], t1[:, c, 0:n], [(t1k, c)], [(t1k, c)])
                        TS('dve', t2[:, c, 0:n], aa[:, c, 0:n], pcol('rwkv_k_a', c), M, [(aak, c)] + PK, [(t2k, c)], s2=pvd[:, 2 + c:3 + c], op1=A)
                    for c in C4:
                        TT('pool', kk[:, c, 0:n], kk[:, c, 0:n], t1[:, c, 0:n], M, [(kkk, c), (t1k, c)], [(kkk, c)])
                        TT('pool', xs[:, 4 + c, 0:n], xs[:, 4 + c, 0:n], t2[:, c, 0:n], M, [xsk, (t2k, c)], [xsk])
                    for c in C4:
                        TT('pool', bb[:, c, 0:n], kk[:, c, 0:n], aa[:, c, 0:n], M, [(kkk, c), (aak, c)], [bbk])
                        STT(t1[:, c, 0:n], xs[:, c, 0:n], pcol('rwkv_r_k', c), xs[:, 4 + c, 0:n], M, M, [xsk, (kkk, c), 'pvt'], [(t1k, c)])
                    for c in C4:
                        t_, k_ = rbn[c]
                        MM(t_[:, 0:n], bonesf, t1[:, c, 0:n], [(t1k, c)] + CK, [k_])
                        TT('dve', bon[:, c, 0:n], t_[:, 0:n], xs[:, 8 + c, 0:n], M, [k_, xsk], [bonk])
                    XK = [xsk]
                    LWK = [(lwk, c) for c in C4]
                    KKK = [(kkk, c) for c in C4]
                    Bc, Bk = FA(4, 128)
                    Ep, Epk = FA(4, 128)
                    Em, Emk = FA(4, 128)
                    Ex, Exk = FA(4, 128)
                    rt, rtk = BA(4, 128)
                    kt, ktk = BA(4, 128)
                    bt, btk = BA(4, 128)
                    at, atk = BA(4, 128)
                    vb, vbk = BA(4, 128)
                    Ktm, Ktk = BA(1, 512)
                    Btm, Btk = BA(1, 512)
                    Vtm, Vtk = BA(1, 512)
                    NH = 8
                    Qb = [[BA(1, 128) for _ in range(2)] for _ in range(NH)]
                    NX = [[BA(1, 192) for _ in range(2)] for _ in range(NH)]
                    Nb = [[(NX[i][a][0][:, :, 0:128], (NX[i][a][1], 'n')) for a in range(2)] for i in range(NH)]
                    Xb = [[(NX[i][a][0][:, :, 128:192], (NX[i][a][1], 'x')) for a in range(2)] for i in range(NH)]
                    LkT = [BA(1, 128) for _ in range(NH)]
                    ArkT = [BA(1, 128) for _ in range(NH)]
                    ArbT = [BA(1, 128) for _ in range(NH)]
                    Un = [BA(1, 64) for _ in range(NH)]
                    pcs = [(pc[0], ('pc', 0)), (pc[1], ('pc', 1)), (pp[0], ('pp', 0)), (pp[1], ('pp', 1)), (pn, 'pn')]
                    pci = [0]

                    def npc():
                        pci[0] = (pci[0] + 1) % len(pcs)
                        return pcs[pci[0]]
                    yfr, yfrk = FA(4, NTOKMAX)
                    tmpS, tmpk = FA(4, 64)
                    stRs = [FA(8, 64) for _ in range(2)]
                    for (off, L, smp) in tiles:
                        sp_ = 0 if smp is None else 1 + (smp % 2)
                        Sr_f, Sr_b, SrFk, SrBk = Sr_fs[sp_], Sr_bs[sp_], ('Sr_f', sp_), ('Sr_b', sp_)
                        stR, stRk = stRs[sp_ % 2]
                        if smp is not None:
                            S.dma('sp', stR[0:64, :, :], srw[l, smp].rearrange("h v k -> v h k"), writes=[stRk])
                            for c in range(4):
                                b = nextpp()
                                TR(pp[b][:, 0:64], stR[0:64, 2 * c:2 * c + 2, :].rearrange("p a k -> p (a k)"), identf[0:64, 0:64], [stRk] + CK, [('pp', b)])
                                CP('act', Sr_f[:, c, :], pp[b][:, 0:64], [('pp', b)], [SrFk])
                            CP('dve', Sr_b[:], Sr_f[:], [SrFk], [SrBk])
                        for c in range(4):
                            SCAN(Bc[:, c, 0:L], onesf[:, 0:L], lw[:, c, off:off + L], 0.0, LWK + CK, [Bk])
                        ACT(Ep[:, :, 0:L], Bc[:, :, 0:L], AF.Exp, [Bk], [Epk])
                        ACT(Em[:, :, 0:L], Bc[:, :, 0:L], AF.Exp, [Bk], [Emk], scale=-1.0)
                        TT('pool', Ex[:, :, 0:L], Bc[:, :, 0:L], lw[:, :, off:off + L], SUB, [Bk] + LWK, [Exk])
                        ACT(Ex[:, :, 0:L], Ex[:, :, 0:L], AF.Exp, [Exk], [Exk])
                        TT('dve', rt[:, :, 0:L], xs[:, 0:4, off:off + L], Ep[:, :, 0:L], M, [xsk, Epk], [rtk])
                        TT('pool', kt[:, :, 0:L], xs[:, 4:8, off:off + L], Em[:, :, 0:L], M, XK + [Emk], [ktk])
                        TT('pool', bt[:, :, 0:L], bb[:, :, off:off + L], Em[:, :, 0:L], M, [bbk, Emk], [btk])
                        TT('pool', at[:, :, 0:L], kk[:, :, off:off + L], Ex[:, :, 0:L], M, KKK + [Exk], [atk])
                        CP('act', vb[:, :, 0:L], xs[:, 8:12, off:off + L], [xsk], [vbk])
                        for (src, sk_, dst, dk_) in [(kt, ktk, Ktm, Ktk), (bt, btk, Btm, Btk), (vb, vbk, Vtm, Vtk)]:
                            for c in range(4):
                                TR(ptb[0:L, c * 128:(c + 1) * 128], src[:, c, 0:L], cstb[:], [sk_, 'cstb'], ['ptb'])
                            CP('act', dst[0:L, 0, :], ptb[0:L, 0:512], ['ptb'], [dk_])
                        nlev = {128: 7, 16: 4, 1: 0}[L]
                        for grp in range(8 // NH):
                            HD = []
                            for i in range(NH):
                                h = grp * NH + i
                                c, hp = h // 2, h % 2
                                HD.append((i, h, c, slice(64 * hp, 64 * hp + 64), slice(h * 64, (h + 1) * 64)))
                            if nlev > 0:
                                for (i, h, c, P, hs) in HD:
                                    t_, k_ = npc()
                                    MM(t_[0:L, 0:L], bt[P, c, 0:L], at[P, c, 0:L], [btk, atk], [k_])
                                    TT('dve', RR(Qb[i][0][0][0:L, 0, 0:L]), t_[0:L, 0:L], MUn[0:L, 0:L], M, [k_] + CK, [Qb[i][0][1]])
                                for (i, h, c, P, hs) in HD:
                                    t_, k_ = npc()
                                    MM(t_[0:L, 0:L], at[P, c, 0:L], bt[P, c, 0:L], [btk, atk], [k_])
                                    TT('dve', RR(Nb[i][0][0][0:L, 0, 0:L]), t_[0:L, 0:L], MLn[0:L, 0:L], M, [k_] + CK, [Nb[i][0][1]])
                                for (i, h, c, P, hs) in HD:
                                    t_, k_ = npc()
                                    MM(t_[0:L, 0:L], kt[P, c, 0:L], at[P, c, 0:L], [ktk, atk], [k_])
                                    TT('dve', LkT[i][0][0:L, 0, 0:L], t_[0:L, 0:L], MU[0:L, 0:L], M, [k_] + CK, [LkT[i][1]])
                            for (i, h, c, P, hs) in HD:
                                t_, k_ = npc()
                                MM(t_[0:L, 0:L], kt[P, c, 0:L], rt[P, c, 0:L], [ktk, rtk], [k_])
                                TT('dve', ArkT[i][0][0:L, 0, 0:L], t_[0:L, 0:L], MUI[0:L, 0:L], M, [k_] + CK, [ArkT[i][1]])
                            for (i, h, c, P, hs) in HD:
                                t_, k_ = npc()
                                MM(t_[0:L, 0:L], bt[P, c, 0:L], rt[P, c, 0:L], [btk, rtk], [k_])
                                TT('dve', ArbT[i][0][0:L, 0, 0:L], t_[0:L, 0:L], MUI[0:L, 0:L], M, [k_] + CK, [ArbT[i][1]])
                            for (i, h, c, P, hs) in HD:
                                t_, k_ = npc()
                                MM(t_[0:L, 0:64], at[P, c, 0:L], Sr_b[P, c, :], [atk, SrBk], [k_], start=True, stop=(nlev == 0))
                                if nlev > 0:
                                    MM(t_[0:L, 0:64], LkT[i][0][0:L, 0, 0:L], Vtm[0:L, 0, hs], [LkT[i][1], Vtk], [k_], start=False, stop=True)
                                CP('act', RR(Xb[i][0][0][0:L, 0, :]), t_[0:L, 0:64], [k_], [Xb[i][0][1]])
                            for k in range(nlev):
                                a_, b_ = k % 2, (k + 1) % 2
                                merged = (L == 128 and k < nlev - 2)
                                for (i, h, c, P, hs) in HD:
                                    t_, k_ = npc()
                                    if merged:
                                        MM(t_[0:L, 0:192], Qb[i][a_][0][0:L, 0, 0:L], NX[i][a_][0][0:L, 0, 0:192], [Qb[i][a_][1], Nb[i][a_][1], Xb[i][a_][1]], [k_])
                                        TT('dve', Xb[i][b_][0][0:L, 0, :], Xb[i][a_][0][0:L, 0, :], t_[0:L, 128:192], A, [Xb[i][a_][1], k_], [Xb[i][b_][1]])
                                        CP('act', Nb[i][b_][0][0:L, 0, 0:L], t_[0:L, 0:L], [k_], [Nb[i][b_][1]])
                                    else:
                                        MM(t_[0:L, 0:64], Qb[i][a_][0][0:L, 0, 0:L], Xb[i][a_][0][0:L, 0, :], [Qb[i][a_][1], Xb[i][a_][1]], [k_])
                                        TT('dve', Xb[i][b_][0][0:L, 0, :], Xb[i][a_][0][0:L, 0, :], t_[0:L, 0:64], A, [Xb[i][a_][1], k_], [Xb[i][b_][1]])
                                if k < nlev - 1:
                                    for (i, h, c, P, hs) in HD:
                                        t_, k_ = npc()
                                        MM(t_[0:L, 0:L], Nb[i][a_][0][0:L, 0, 0:L], Qb[i][a_][0][0:L, 0, 0:L], [Nb[i][a_][1], Qb[i][a_][1]], [k_])
                                        CP('act' if i % 2 else 'dve', Qb[i][b_][0][0:L, 0, 0:L], t_[0:L, 0:L], [k_], [Qb[i][b_][1]])
                                if k < nlev - 2 and not merged:
                                    for (i, h, c, P, hs) in HD:
                                        t_, k_ = npc()
                                        MM(t_[0:L, 0:L], Qb[i][a_][0][0:L, 0, 0:L], Nb[i][a_][0][0:L, 0, 0:L], [Nb[i][a_][1], Qb[i][a_][1]], [k_])
                                        CP('act' if i % 2 else 'dve', Nb[i][b_][0][0:L, 0, 0:L], t_[0:L, 0:L], [k_], [Nb[i][b_][1]])
                            xf = nlev % 2
                            for (i, h, c, P, hs) in HD:
                                TS('dve', Un[i][0][0:L, 0, :], Xb[i][xf][0][0:L, 0, :], -1.0, M, [Xb[i][xf][1]], [Un[i][1]])
                            for (i, h, c, P, hs) in HD:
                                yo = py[P, c * 128:c * 128 + L]
                                MM(yo, Sr_b[P, c, :], rt[P, c, 0:L], [SrBk, rtk], ['py'], start=True, stop=False)
                                MM(yo, Vtm[0:L, 0, hs], ArkT[i][0][0:L, 0, 0:L], [Vtk, ArkT[i][1]], ['py'], start=False, stop=False)
                                MM(yo, Un[i][0][0:L, 0, :], ArbT[i][0][0:L, 0, 0:L], [Un[i][1], ArbT[i][1]], ['py'], start=False, stop=True)
                                so = pst[P, c * 64:(c + 1) * 64]
                                MM(so, Ktm[0:L, 0, hs], Vtm[0:L, 0, hs], [Ktk, Vtk], ['pst'], start=True, stop=False)
                                MM(so, Btm[0:L, 0, hs], Un[i][0][0:L, 0, :], [Btk, Un[i][1]], ['pst'], start=False, stop=True)
                        CP('act', yfr[:, :, off:off + L], py[:, :].rearrange("p (c j) -> p c j", c=4)[:, :, 0:L], ['py'], [yfrk])
                        for c in range(4):
                            TT('dve', tmpS[:, c, :], Sr_f[:, c, :], pst[:, c * 64:(c + 1) * 64], A, [SrFk, 'pst'], [tmpk])
                            TS('dve', Sr_f[:, c, :], tmpS[:, c, :], Ep[:, c, L - 1:L], M, [tmpk, Epk], [SrFk])
                        CP('act', Sr_b[:], Sr_f[:], [SrFk], [SrBk])
                        if smp is not None or (last and off == 256):
                            for c in range(4):
                                b = nextpp()
                                TR(pp[b][0:64, 0:128], Sr_f[:, c, :], identf, [SrFk] + CK, [('pp', b)])
                                CP('act', stR[0:64, 2 * c:2 * c + 2, :].rearrange("p a k -> p (a k)"), pp[b][0:64, 0:128], [('pp', b)], [stRk])
                            dst = rws[l, smp] if smp is not None else rwp[l]
                            S.dma('sp', dst.rearrange("h v k -> v h k"), stR[0:64, :, :], reads=[stRk], is_output=True)
                    for c in C4:
                        t_, k_ = rbn[c]
                        MM(t_[:, 0:n], bonesf, yfr[:, c, 0:n], [yfrk] + CK, [k_])
                        STT(t1[:, c, 0:n], t_[:, 0:n], -1.0 / 64, yfr[:, c, 0:n], M, A, [k_, yfrk], [(t1k, c)])
                    for c in C4:
                        ACT(t2[:, c, 0:n], t1[:, c, 0:n], AF.Square, [(t1k, c)], [(t2k, c)])
                    for c in C4:
                        t_, k_ = rbn[c]
                        MM(t_[:, 0:n], bonesf, t2[:, c, 0:n], [(t2k, c)] + CK, [k_])
                        ACT(t2[:, c, 0:n], t_[:, 0:n], AF.Sqrt, [k_], [(t2k, c)], bias=64e-5, scale=1.0 / 64)
                    for c in C4:
                        RECIP(t2[:, c, 0:n], t2[:, c, 0:n], [(t2k, c)], [(t2k, c)])
                    for c in C4:
                        TT('pool', t1[:, c, 0:n], t1[:, c, 0:n], t2[:, c, 0:n], M, [(t1k, c), (t2k, c)], [(t1k, c)])
                    for c in C4:
                        TS('dve', t1[:, c, 0:n], t1[:, c, 0:n], pcol('rwkv_ln_w', c), M, [(t1k, c), 'pvt'], [(t1k, c)], s2=pcol('rwkv_ln_b', c), op1=A)
                    for c in C4:
                        TT('pool', t1[:, c, 0:n], t1[:, c, 0:n], bon[:, c, 0:n], A, [(t1k, c), bonk], [(t1k, c)])
                    for c in C4:
                        TT('dve', og[:, 1, c, 0:n], t1[:, c, 0:n], gq[:, c, 0:n], M, [(t1k, c), gqk], ['og1'])
                if 'S' in PH:
                    prefetch_w('w_in', l, 3344, 8, 512)
                    phase()
                    uf, ufk = FA(4, NTOKMAX)
                    ub, ubk = BA(4, NTOKMAX)
                    wi = load_w('w_in', l, 3344, 8, 512)
                    for j in range(4):
                        b = nextpp()
                        proj(wi, 8, j * 128, 128, hb, HK, n, b)
                        CP('act', uf[:, j, 0:n], pp[b][:, 0:n], [('pp', b)], [ufk])
                        CP('dve', ub[:, j, 0:n], pp[b][:, 0:n], [('pp', b)], [ubk])
                    tab, tbk = FA(64, 128)
                    S.dma('sp', tab, tabd.rearrange("p a s j -> p (a s) j"), reads=['tabd'], writes=[tbk])
                    tabv = tab.rearrange("p (a s) j -> p a s j", a=4)
                    bpb = []
                    for i, src in enumerate([bpr, bpi, cpr, cpi]):
                        t_, k_ = BA(16, 128)
                        S.dma('pool', t_, src[l], writes=[k_])
                        bpb.append((t_, k_))
                    TS('dve', bpb[3][0], bpb[3][0], -1.0, M, [bpb[3][1]], [bpb[3][1]])
                    ys5, ysk = FA(4, NTOKMAX)
                    NR5 = 8
                    R5 = [[FA(2, 128) for _ in range(5)] + [BA(2, 128)] for _ in range(NR5)]
                    pcs5 = [(pc[0], ('pc', 0)), (pc[1], ('pc', 1)), (pp[0], ('pp', 0)), (pp[1], ('pp', 1)), (pn, 'pn')]
                    stiles = [(0, 128, None), (128, 128, None)]
                    if last:
                        stiles += [(256, 16, None), (272, 16, 0)]
                        xs0, xs0k = FA(16 * 2, 16)
                        xs0v = xs0.rearrange("p (s a) n -> p s a n", a=2)
                        xso, xsok = FA(16 * 2, 16)
                        xsov = xso.rearrange("p (s a) n -> p s a n", a=2)
                        stg, stgk = FA(1, 2048)
                        for a_, src in enumerate([s5r, s5i]):
                            S.dma('sp', stg[0:16, 0, :], src[l], writes=[stgk])
                            for sc in range(16):
                                b = nextpp()
                                TR(pp[b][:, 0:16], stg[0:16, 0, sc * 128:(sc + 1) * 128], identf[0:16, 0:16], [stgk] + CK, [('pp', b)])
                                CP('act', xs0v[:, sc, a_, :], pp[b][:, 0:16], [('pp', b)], [xs0k])
                    def s5_chain(kidx, off, L, smp, uc):
                        SCS = []
                        for q in range(4):
                            sc = uc * 4 + q
                            SCS.append((q, sc, R5[(kidx % 2) * 4 + q], pcs5[sc % 5]))
                        for (q, sc, R_, (pct, pck)) in SCS:
                            for pl, wsel in enumerate([0, 1, 1, 0]):
                                MM(pct[:, pl * 128:pl * 128 + L], bpb[wsel][0][:, sc, :], ub[:, uc, off:off + L], [bpb[wsel][1], ubk], [pck], chain=True)
                        BU = {}
                        for (q, sc, R_, (pct, pck)) in SCS:
                            BU[sc] = (pct[:, 0:256].rearrange("p (a j) -> p a j", a=2)[:, :, 0:L],
                                      pct[:, 256:512].rearrange("p (a j) -> p a j", a=2)[:, :, 0:L])
                        yield
                        if smp is None:
                            for (q, sc, R_, (pct, pck)) in SCS:
                                (m1, m1k), (m2, m2k), (zz, zzk), (ZZ, ZZk), (xx, xxk), (xbf, xbk) = R_
                                Bu, Bus = BU[sc]
                                TT('dve', m1[:, :, 0:L], Bu, tabv[:, 2, sc, 0:L].unsqueeze(1).broadcast_to([128, 2, L]), M, [pck, tbk], [m1k])
                                TT('dve', m2[:, :, 0:L], Bus, tabv[:, 3, sc, 0:L].unsqueeze(1).broadcast_to([128, 2, L]), M, [pck, tbk], [m2k])
                            yield
                            for (q, sc, R_, (pct, pck)) in SCS:
                                (m1, m1k), (m2, m2k), (zz, zzk), (ZZ, ZZk), (xx, xxk), (xbf, xbk) = R_
                                TT('pool', zz[:, 0, 0:L], m1[:, 0, 0:L], m2[:, 0, 0:L], SUB, [m1k, m2k], [zzk])
                                TT('pool', zz[:, 1, 0:L], m1[:, 1, 0:L], m2[:, 1, 0:L], A, [m1k, m2k], [zzk])
                            yield
                            for (q, sc, R_, (pct, pck)) in SCS:
                                (m1, m1k), (m2, m2k), (zz, zzk), (ZZ, ZZk), (xx, xxk), (xbf, xbk) = R_
                                for a_ in range(2):
                                    SCAN(ZZ[:, a_, 0:L], onesf[:, 0:L], zz[:, a_, 0:L], x5[:, sc, a_:a_ + 1], [zzk, ('x5', sc)] + CK, [ZZk])
                            yield
                            for (q, sc, R_, (pct, pck)) in SCS:
                                (m1, m1k), (m2, m2k), (zz, zzk), (ZZ, ZZk), (xx, xxk), (xbf, xbk) = R_
                                TT('dve', m1[:, :, 0:L], ZZ[:, :, 0:L], tabv[:, 0, sc, 0:L].unsqueeze(1).broadcast_to([128, 2, L]), M, [ZZk, tbk], [m1k])
                                TT('pool', m2[:, 0, 0:L], ZZ[:, 1, 0:L], tabv[:, 1, sc, 0:L], M, [ZZk, tbk], [m2k])
                                TT('pool', m2[:, 1, 0:L], ZZ[:, 0, 0:L], tabv[:, 1, sc, 0:L], M, [ZZk, tbk], [m2k])
                            yield
                            for (q, sc, R_, (pct, pck)) in SCS:
                                (m1, m1k), (m2, m2k), (zz, zzk), (ZZ, ZZk), (xx, xxk), (xbf, xbk) = R_
                                TT('dve', xx[:, 0, 0:L], m1[:, 0, 0:L], m2[:, 0, 0:L], SUB, [m1k, m2k], [xxk])
                                TT('dve', xx[:, 1, 0:L], m1[:, 1, 0:L], m2[:, 1, 0:L], A, [m1k, m2k], [xxk])
                            yield
                            for (q, sc, R_, (pct, pck)) in SCS:
                                (m1, m1k), (m2, m2k), (zz, zzk), (ZZ, ZZk), (xx, xxk), (xbf, xbk) = R_
                                CP('pool', x5[:, sc, :], xx[:, :, L - 1], [xxk], [('x5', sc)])
                                CP('act', xbf[:, :, 0:L], xx[:, :, 0:L], [xxk], [xbk])
                            yield
                        else:
                            for (q, sc, R_, (pct, pck)) in SCS:
                                (m1, m1k), (m2, m2k), (zz, zzk), (ZZ, ZZk), (xx, xxk), (xbf, xbk) = R_
                                Bu, Bus = BU[sc]
                                PCK = [pck]
                                cr = s5c[:, 2, sc:sc + 1]
                                cim_ = s5c[:, 3, sc:sc + 1]
                                ar_ = s5c[:, 0, sc:sc + 1]
                                ai_ = s5c[:, 1, sc:sc + 1]
                                TS('dve', m1[:, :, 0:L], Bu, cr, M, PCK + ['s5c'], [m1k])
                                TS('dve', m2[:, :, 0:L], Bus, cim_, M, PCK + ['s5c'], [m2k])
                                TT('pool', zz[:, 0, 0:L], m1[:, 0, 0:L], m2[:, 0, 0:L], SUB, [m1k, m2k], [zzk])
                                TT('pool', zz[:, 1, 0:L], m1[:, 1, 0:L], m2[:, 1, 0:L], A, [m1k, m2k], [zzk])
                                STT(ZZ[:, :, 0:L], xs0v[:, sc, :, :], ar_, zz[:, :, 0:L], M, A, [xs0k, zzk, 's5c'], [ZZk])
                                TS('dve', m2[:, 0, 0:L], xs0v[:, sc, 1, :], ai_, M, [xs0k, 's5c'], [m2k])
                                TS('dve', m2[:, 1, 0:L], xs0v[:, sc, 0, :], ai_, M, [xs0k, 's5c'], [m2k])
                                TT('dve', xx[:, 0, 0:L], ZZ[:, 0, 0:L], m2[:, 0, 0:L], SUB, [ZZk, m2k], [xxk])
                                TT('dve', xx[:, 1, 0:L], ZZ[:, 1, 0:L], m2[:, 1, 0:L], A, [ZZk, m2k], [xxk])
                                CP('pool', xsov[:, sc, :, :], xx[:, :, 0:L], [xxk], [xsok])
                                CP('act', xbf[:, :, 0:L], xx[:, :, 0:L], [xxk], [xbk])
                            yield
                        for (q, sc, R_, (pct, pck)) in SCS:
                            (m1, m1k), (m2, m2k), (zz, zzk), (ZZ, ZZk), (xx, xxk), (xbf, xbk) = R_
                            MM(py[:, 0:L], bpb[2][0][:, sc, :], xbf[:, 0, 0:L], [bpb[2][1], xbk], ['py'], start=(q == 0), stop=False, chain=True)
                            MM(py[:, 0:L], bpb[3][0][:, sc, :], xbf[:, 1, 0:L], [bpb[3][1], xbk], ['py'], start=False, stop=(q == 3), chain=True)
                        CP('act', ys5[:, uc, off:off + L], py[:, 0:L], ['py'], [ysk])

                    def s5_store_prompt():
                        for a_, dst in enumerate([s5rp, s5ip]):
                            b = nextpp()
                            TR(pp[b][0:16, 0:128], x5[:, :, a_], identf, [('x5', s_) for s_ in range(16)] + CK, [('pp', b)])
                            CP('act', stg[0:16, 0, 0:128], pp[b][0:16, 0:128], [('pp', b)], [stgk])
                            S.dma('sp', dst[l], stg[0:16, 0, 0:128], reads=[stgk], is_output=True)
                        yield

                    pend = []
                    kidx = 0
                    for (off, L, smp) in stiles:
                        for uc in range(4):
                            pend.append(s5_chain(kidx, off, L, smp, uc))
                            kidx += 1
                        if last and off == 256:
                            pend.append('fence')
                            pend.append(s5_store_prompt())
                            pend.append('fence')
                    SKEW = 4
                    active = []
                    while pend or active:
                        while pend:
                            if pend[0] == 'fence':
                                if active:
                                    break
                                pend.pop(0)
                                continue
                            if not active or (len(active) < 2 and active[-1][1] >= SKEW):
                                active.append([pend.pop(0), 0])
                            else:
                                break
                        for ent in list(active):
                            try:
                                next(ent[0])
                                ent[1] += 1
                            except StopIteration:
                                active.remove(ent)
                    if last:
                        for a_, dst in enumerate([s5rs, s5is]):
                            for sc in range(16):
                                b = nextpp()
                                TR(pp[b][0:16, 0:128], xsov[:, sc, a_, :], identf, [xsok] + CK, [('pp', b)])
                                CP('act', stg[0:16, 0, sc * 128:(sc + 1) * 128], pp[b][0:16, 0:128], [('pp', b)], [stgk])
                            S.dma('sp', dst[l], stg[0:16, 0, :], reads=[stgk], is_output=True)
                    yv, yvk = FA(4, NTOKMAX)
                    t5, t5k = FA(1, NTOKMAX)
                    ygb, ygbk = BA(4, NTOKMAX)
                    for uc in range(4):
                        STT(yv[:, uc, 0:n], uf[:, uc, 0:n], pvt[:, PVO['s5_d'] + uc:PVO['s5_d'] + uc + 1], ys5[:, uc, 0:n], M, A, [ufk, ysk, 'pvt'], [yvk])
                        ACT(t5[:, 0, 0:n], yv[:, uc, 0:n], AF.Square, [yvk], [t5k])
                        TS('dve', t5[:, 0, 0:n], t5[:, 0, 0:n], 0.044715, M, [t5k], [t5k], s2=1.0, op1=A)
                        TT('dve', t5[:, 0, 0:n], t5[:, 0, 0:n], yv[:, uc, 0:n], M, [t5k, yvk], [t5k])
                        ACT(t5[:, 0, 0:n], t5[:, 0, 0:n], AF.Sigmoid, [t5k], [t5k], scale=1.5957691216057308)
                        TT('dve', yv[:, uc, 0:n], yv[:, uc, 0:n], t5[:, 0, 0:n], M, [t5k, yvk], [yvk])
                        CP('act', ygb[:, uc, 0:n], yv[:, uc, 0:n], [yvk], [ygbk])
                    for oc in range(4):
                        b = nextpp()
                        for kc in range(4):
                            MM(pp[b][:, 0:n], glub[:, kc, oc * 128:(oc + 1) * 128], ygb[:, kc, 0:n], ['glub', ygbk], [('pp', b)], start=(kc == 0), stop=(kc == 3), chain=True)
                        ACT(t5[:, 0, 0:n], pp[b][:, 0:n], AF.Sigmoid, [('pp', b), 'pvt'], [t5k], bias=pvt[:, PVO['s5_glu_b'] + oc:PVO['s5_glu_b'] + oc + 1])
                        TT('dve', og[:, 2, oc, 0:n], yv[:, oc, 0:n], t5[:, 0, 0:n], M, [yvk, t5k], ['og2'])
                if 'C' in PH:
                    prefetch_w('w_in', l, 3856, 8, 512)
                    prefetch_w('w_br_gla', l, 0, 4, 512)
                    phase()
                    mf, mfk = FA(8, NTOKMAX)
                    sg, sgk = FA(2, NTOKMAX)
                    brn = ['w_br_gla', 'w_br_rwkv', 'w_br_s5']
                    for br in range(3):
                        for half in range(2):
                            c0 = 3856 + br * 1024 + half * 512
                            wi = load_w('w_in', l, c0, 8, 512)
                            wj = load_w(brn[br], l, half * 512, 4, 512)
                            for j in range(4):
                                oc = half * 4 + j
                                b = nextpp()
                                proj(wi, 8, j * 128, 128, hb, HK, n, b)
                                ACT(sg[:, j % 2, 0:n], pp[b][:, 0:n], AF.Sigmoid, [('pp', b)], [(sgk, j % 2)])
                                b2 = nextpp()
                                proj(wj, 4, j * 128, 128, og[:, br], ['og%d' % br], n, b2)
                                if br == 0:
                                    TT('dve', mf[:, oc, 0:n], sg[:, j % 2, 0:n], pp[b2][:, 0:n], M, [(sgk, j % 2), ('pp', b2)], [(mfk, oc)])
                                else:
                                    TT('dve', sg[:, j % 2, 0:n], sg[:, j % 2, 0:n], pp[b2][:, 0:n], M, [(sgk, j % 2), ('pp', b2)], [(sgk, j % 2)])
                                    TT('pool', mf[:, oc, 0:n], mf[:, oc, 0:n], sg[:, j % 2, 0:n], A, [(sgk, j % 2), (mfk, oc)], [(mfk, oc)])
                    mb, mbk = BA(8, NTOKMAX)
                    for oc in range(8):
                        CP('act' if oc % 2 else 'dve', mb[:, oc, 0:n], mf[:, oc, 0:n], [(mfk, oc)], [mbk])
                    for half in range(2):
                        wi = load_w('w_out', l, half * 512, 8, 512)
                        for j in range(4):
                            oc = half * 4 + j
                            b = nextpp()
                            proj(wi, 8, j * 128, 128, mb, [mbk], n, b)
                            TT('dve', xb[:, oc, 0:n], xb[:, oc, 0:n], pp[b][:, 0:n], A, ['xb', ('pp', b)], ['xb'])
                    prefetch_w('ffn_w1', l, 0, 8, 512)
                    prefetch_w('ffn_w3', l, 0, 8, 512)
                    phase()
                    rs, rk = rmsnorm(PVO['norm_ffn'], n)
                    for c in range(8):
                        STT(hb[:, c, 0:n], xb[:, c, 0:n], pvt[:, PVO['norm_ffn'] + c:PVO['norm_ffn'] + c + 1], rs[:, 0, 0:n], M, M,
                            ['xb', 'pvt', rk], ['hb'])
                    hid, hidk = BA(22, NTOKMAX)
                    sl, slk = FA(2, NTOKMAX)
                    for g in range(6):
                        c0 = g * 512
                        ncol = min(512, DFF - c0)
                        wi = load_w('ffn_w1', l, c0, 8, ncol)
                        wj = load_w('ffn_w3', l, c0, 8, ncol)
                        for j in range(ncol // 128):
                            hc = g * 4 + j
                            b = nextpp()
                            proj(wi, 8, j * 128, 128, hb, HK, n, b)
                            ACT(sl[:, j % 2, 0:n], pp[b][:, 0:n], AF.Silu, [('pp', b)], [(slk, j % 2)])
                            b2 = nextpp()
                            proj(wj, 8, j * 128, 128, hb, HK, n, b2)
                            TT('dve', hid[:, hc, 0:n], sl[:, j % 2, 0:n], pp[b2][:, 0:n], M, [(slk, j % 2), ('pp', b2)], [hidk])
                    for ocp in range(4):
                        halves = []
                        for kh in range(2):
                            i = wt['i']
                            wt['i'] = (i + 1) % NWB
                            w2v = wbuf[i][:, :, :].rearrange("p a b -> p (a b)")[:, 0:11 * 256].rearrange("p (k m) -> p k m", k=11)
                            S.dma('sp', w2v, wb['ffn_w2'][l, kh * 1408:(kh + 1) * 1408, ocp * 256:(ocp + 1) * 256].rearrange("(kc p) n -> p kc n", p=128),
                                  reads=WK('ffn_w2', l), writes=[('wbuf', i)])
                            halves.append((w2v, i))
                        for j in range(2):
                            oc = ocp * 2 + j
                            b = nextpp()
                            for kc in range(22):
                                w2v, i = halves[kc // 11]
                                MM(pp[b][:, 0:n], w2v[:, kc % 11, j * 128:(j + 1) * 128], hid[:, kc, 0:n], [('wbuf', i), hidk], [('pp', b)],
                                   start=(kc == 0), stop=(kc == 21), chain=True)
                            TT('dve', xb[:, oc, 0:n], xb[:, oc, 0:n], pp[b][:, 0:n], A, ['xb', ('pp', b)], ['xb'])
                    S.dma('sp', xres[:, :, t0:t0 + n], xb[:, :, 0:n], reads=['xb'], writes=['xres'])
                    if l == depth - 1:
                        rs, rk = rmsnorm(PVO['final_norm'], n)
                        yf, yfk = FA(8, NTOKMAX)
                        for c in range(8):
                            STT(yf[:, c, 0:n], xb[:, c, 0:n], pvt[:, PVO['final_norm'] + c:PVO['final_norm'] + c + 1], rs[:, 0, 0:n], M, M,
                                ['xb', 'pvt', rk], [yfk])
                        yt, ytk = FA(2, 1024)
                        otl = [(0, 128), (128, 128)] + ([(256, 32)] if last else [])
                        for ti, (off, L) in enumerate(otl):
                            for c in range(8):
                                b = nextpp()
                                TR(pp[b][0:L, 0:128], yf[:, c, off:off + L], identf, [yfk] + CK, [('pp', b)])
                                CP('act' if c % 2 else 'dve', yt[0:L, ti % 2, c * 128:(c + 1) * 128], pp[b][0:L, 0:128], [('pp', b)], [(ytk, ti % 2)])
                            g0 = t0 + off
                            lo, hi = max(g0, 16), min(g0 + L, NPR)
                            if hi > lo:
                                S.dma('sp', yp[lo - 16:hi - 16, :], yt[lo - g0:hi - g0, ti % 2, :], reads=[(ytk, ti % 2)], is_output=True)
                            lo2 = max(g0, NPR)
                            if g0 + L > lo2:
                                S.dma('sp', ys[lo2 - NPR:g0 + L - NPR, :], yt[lo2 - g0:L, ti % 2, :], reads=[(ytk, ti % 2)], is_output=True)
        S.finish(block)
        print("n_instr", S.n_instr, {e: len(v) for e, v in S.q.items()})
    return nc


def _fm(v, nch):
    return np.ascontiguousarray(np.asarray(v, np.float32).reshape(nch, 128).T)


def _consts():
    p = np.arange(128)[:, None]
    f = np.arange(128)[None, :]
    mu = (p < f).astype(np.float32)
    mui = (p <= f).astype(np.float32)
    mln = -(p > f).astype(np.float32)
    mun = -(p < f).astype(np.float32)
    ident = np.eye(128, dtype=np.float32)
    ones = np.ones((128, 128), np.float32)
    bones = np.zeros((128, 128), np.float32)
    bones[:64, :64] = 1
    bones[64:, 64:] = 1
    return np.ascontiguousarray(np.concatenate([mu, mui, mln, mun, ident, ones, bones], axis=1))


def _prep_shared(inp):
    L = 4
    pvs = np.zeros((L, 128, NV), np.float32)
    for l in range(L):
        def put(name, arr, nch):
            pvs[l, :, PVO[name]:PVO[name] + nch] = _fm(arr, nch)
        put('norm_mix', inp['norm_mix'][l], 8)
        put('norm_ffn', inp['norm_ffn'][l], 8)
        put('gla_gate_b', inp['gla_gate_b'][l], 2)
        put('gla_norm', inp['gla_norm'][l], 4)
        put('rwkv_mu', inp['rwkv_mu'][l], 14)
        put('rwkv_w0', inp['rwkv_w0'][l], 4)
        put('rwkv_a0', inp['rwkv_a0'][l], 4)
        put('rwkv_k_k', inp['rwkv_k_k'][l], 4)
        put('rwkv_k_a', inp['rwkv_k_a'][l], 4)
        put('rwkv_r_k', np.asarray(inp['rwkv_r_k'][l]).reshape(-1), 4)
        put('rwkv_ln_w', inp['rwkv_ln_w'][l], 4)
        put('rwkv_ln_b', inp['rwkv_ln_b'][l], 4)
        put('s5_d', inp['s5_d'][l], 4)
        put('s5_glu_b', inp['s5_glu_b'][l], 4)
        put('s5_are', np.asarray(inp['s5_a_re'][l]).reshape(-1), 16)
        put('s5_aim', np.asarray(inp['s5_a_im'][l]).reshape(-1), 16)
        put('s5_ldt', np.repeat(np.asarray(inp['s5_log_dt'][l]), 64), 16)
        put('final_norm', inp['final_norm'], 8)
    def bpad(B):
        B = np.asarray(B, np.float32)
        out = np.zeros((L, 128, 16, 128), np.float32)
        for sc in range(16):
            uc, q = sc // 4, sc % 4
            for g2 in range(2):
                g = 2 * sc + g2
                gl = 2 * q + g2
                out[:, gl * 16:(gl + 1) * 16, sc, g2 * 64:(g2 + 1) * 64] = np.transpose(B[:, g], (0, 2, 1))
        return out
    def cpad(C):
        C = np.asarray(C, np.float32)
        out = np.zeros((L, 128, 16, 128), np.float32)
        for sc in range(16):
            q = sc % 4
            for g2 in range(2):
                g = 2 * sc + g2
                gl = 2 * q + g2
                out[:, g2 * 64:(g2 + 1) * 64, sc, gl * 16:(gl + 1) * 16] = np.transpose(C[:, g], (0, 2, 1))
        return out
    sh = {
        'pv': pvs, 'consts': _consts(),
        'bpad_re': bpad(inp['s5_b_re']), 'bpad_im': bpad(inp['s5_b_im']),
        'cpad_re': cpad(inp['s5_c_re']), 'cpad_im': cpad(inp['s5_c_im']),
    }
    for k in ['gla_gate_w2', 'rwkv_w2', 'rwkv_a2', 'rwkv_g2', 's5_glu_w'] + [n for n, _, _ in BIGW]:
        sh[k] = np.ascontiguousarray(np.asarray(inp[k], np.float32))
    return sh


def kernel(_depth=4, _nblk=None, _cores=8, **inp):
    sh = _prep_shared(inp)
    xp = np.asarray(inp['x_prompt'], np.float32)
    xs = np.asarray(inp['x_sample'], np.float32)
    meta = np.asarray(inp['meta_tokens'], np.float32)
    in_maps = []
    for c in range(_cores):
        sl = slice(c * NS, (c + 1) * NS)
        m = dict(sh)
        m['xin'] = np.ascontiguousarray(np.concatenate([meta, xp[c], xs[sl, 0]], axis=0))
        m['sgla'] = np.ascontiguousarray(np.asarray(inp['state_gla'], np.float32)[:, sl])
        m['srw'] = np.ascontiguousarray(np.asarray(inp['state_rwkv'], np.float32)[:, sl])
        m['ssh'] = np.ascontiguousarray(np.asarray(inp['state_rwkv_shift'], np.float32)[:, sl])
        m['s5r'] = np.ascontiguousarray(np.asarray(inp['state_s5_re'], np.float32)[:, sl].reshape(4, NS, 2048))
        m['s5i'] = np.ascontiguousarray(np.asarray(inp['state_s5_im'], np.float32)[:, sl].reshape(4, NS, 2048))
        in_maps.append(m)
    nc = build_nc(_depth, _nblk)
    res = run_bass_kernel_spmd(nc, in_maps, core_ids=list(range(_cores)))
    R = res.results
    cat = lambda k, ax: np.concatenate([np.asarray(r[k], np.float32) for r in R], axis=ax)
    stk = lambda k: np.stack([np.asarray(r[k], np.float32) for r in R], axis=1)
    y_prompt = np.stack([np.asarray(r['yp'], np.float32) for r in R], axis=0)
    y_sample = cat('ys', 0).reshape(-1, 1, D)
    gla_p = stk('glap')
    rwkv_p = stk('rwp')
    shift_p = stk('shp').reshape(4, -1, 1792)
    s5re_p = stk('s5rp').reshape(4, -1, 32, 64)
    s5im_p = stk('s5ip').reshape(4, -1, 32, 64)
    gla_s = cat('glas', 1)
    rwkv_s = cat('rws', 1)
    shift_s = cat('shs', 1)
    s5re_s = cat('s5rs', 1).reshape(4, -1, 32, 64)
    s5im_s = cat('s5is', 1).reshape(4, -1, 32, 64)
    return (y_prompt, y_sample, gla_p, rwkv_p, shift_p, s5re_p, s5im_p, gla_s, rwkv_s, shift_s, s5re_s, s5im_s)
```

```python
import contextlib
import numpy as np
import concourse.bass as bass
import concourse.mybir as mybir
from concourse.bass_utils import run_bass_kernel_spmd
PH = set('RSC')
MARK = True
FP32R = False

F32 = mybir.dt.float32
BF16 = mybir.dt.bfloat16
AF = mybir.ActivationFunctionType
ALU = mybir.AluOpType

EPOCH = 30000
DEPOCH = 1500


class Sched:
    CE = ('pe', 'act', 'dve', 'pool')

    def __init__(self, nc, stack, n_dma_slots=12, n_bg=10, n_pf=4):
        self.nc = nc
        self.stack = stack
        self.eng = {'pe': nc.tensor, 'act': nc.scalar, 'dve': nc.vector,
                    'pool': nc.gpsimd, 'sp': nc.sync}
        self.q = {e: [] for e in self.eng}
        self.cnt = {e: 0 for e in self.CE}
        self.sems = {}
        self.seen = {e: {} for e in self.eng}
        self.last_w = {}
        self.readers = {}
        self.nslots = n_dma_slots
        self.slot_sems = {}
        self.slot_cnt = [0] * (n_dma_slots + n_bg)
        self.bgslots = list(range(n_dma_slots, n_dma_slots + n_bg))
        self.bgnext = 0
        self.slot_cnt += [0] * n_pf
        self.pfslots = list(range(n_dma_slots + n_bg, n_dma_slots + n_bg + n_pf))
        self.pfnext = 0
        self.qslots = {'sp': list(range(0, 6)), 'act': list(range(6, 8)), 'pool': list(range(8, n_dma_slots))}
        self.qnext = {'sp': 0, 'act': 0, 'pool': 0}
        self.out_dmas = []
        self.n_instr = 0

    def _sem(self, e, epoch):
        k = (e, epoch)
        if k not in self.sems:
            self.sems[k] = self.stack.enter_context(self.nc.semaphore(f"p_{e}_{epoch}"))
        return self.sems[k]

    def _slot_sem(self, s, ep):
        k = (s, ep)
        if k not in self.slot_sems:
            self.slot_sems[k] = self.stack.enter_context(self.nc.semaphore(f"dslot{s}_{ep}"))
        return self.slot_sems[k]

    def _wait(self, eng, src, c):
        if self.seen[eng].get(src, 0) >= c:
            return
        self.seen[eng][src] = c
        e = self.eng[eng]
        if isinstance(src, tuple):
            sem = self._slot_sem(src[1], (c - 1) // DEPOCH)
            val = ((c - 1) % DEPOCH + 1) * 16
        else:
            idx = c - 1
            sem = self._sem(src, idx // EPOCH)
            val = (idx % EPOCH) + 1
        self.q[eng].append(lambda e=e, sem=sem, val=val: e.wait_ge(sem, val))
        self.n_instr += 1

    def _deps(self, reads, writes):
        deps = set()
        for k in reads:
            w = self.last_w.get(k)
            if w:
                deps.add(w)
        for k in writes:
            w = self.last_w.get(k)
            if w:
                deps.add(w)
            for r in self.readers.get(k, ()):
                deps.add(r)
        return deps

    def _commit(self, me, reads, writes):
        for k in writes:
            self.last_w[k] = me
            self.readers[k] = []
        for k in reads:
            self.readers.setdefault(k, []).append(me)

    @staticmethod
    def _is_psum(k):
        return k in ('pn', 'ptb', 'py', 'pst') or (isinstance(k, tuple) and k[0] in ('pp', 'pc'))

    def op(self, eng, fn, reads=(), writes=(), skip_same=False):
        pr = [k for k in reads if self._is_psum(k)]
        if pr:
            writes = list(writes) + pr
            reads = [k for k in reads if not self._is_psum(k)]
        for (src, c) in sorted(self._deps(reads, writes), key=str):
            if skip_same and src == eng:
                continue
            self._wait(eng, src, c)
        self.cnt[eng] += 1
        c = self.cnt[eng]
        idx = c - 1
        sem = self._sem(eng, idx // EPOCH)
        self.q[eng].append(lambda fn=fn, sem=sem: fn().then_inc(sem, 1))
        self.n_instr += 1
        self._commit((eng, c), reads, writes)

    def dma(self, qeng, out, in_, reads=(), writes=(), is_output=False, bg=False, **kw):
        if bg == 'p':
            s = self.pfslots[self.pfnext % len(self.pfslots)]
            self.pfnext += 1
        elif bg:
            s = self.bgslots[self.bgnext % len(self.bgslots)]
            self.bgnext += 1
        else:
            sl = self.qslots[qeng]
            s = sl[self.qnext[qeng] % len(sl)]
            self.qnext[qeng] += 1
        if self.slot_cnt[s] > 0:
            self._wait(qeng, ('d', s), self.slot_cnt[s])
        for (src, c) in sorted(self._deps(reads, writes), key=str):
            self._wait(qeng, src, c)
        self.slot_cnt[s] += 1
        c = self.slot_cnt[s]
        e = self.eng[qeng]
        sem = self._slot_sem(s, (c - 1) // DEPOCH)
        self.q[qeng].append(lambda e=e, out=out, in_=in_, sem=sem, kw=kw: e.dma_start(out=out, in_=in_, **kw).then_inc(sem, 16))
        self.n_instr += 1
        me = (('d', s), c)
        self._commit(me, reads, writes)
        if is_output:
            self.out_dmas.append(me)

    def finish(self, block):
        for (src, c) in self.out_dmas:
            self.seen['sp'].pop(src, None) if self.seen['sp'].get(src, 0) < c else None
            self._wait('sp', src, c)
        q = self.q

        if q['sp']:
            @block.sync
            def _(e):
                for f in q['sp']:
                    f()
        if q['pe']:
            @block.tensor
            def _(e):
                for f in q['pe']:
                    f()
        if q['act']:
            @block.scalar
            def _(e):
                for f in q['act']:
                    f()
        if q['dve']:
            @block.vector
            def _(e):
                for f in q['dve']:
                    f()
        if q['pool']:
            @block.gpsimd
            def _(e):
                for f in q['pool']:
                    f()


def _barrier(self):
    for e in self.eng:
        for src in self.CE:
            if self.cnt[src] > 0:
                self._wait(e, src, self.cnt[src])
        for qn, slots in self.qslots.items():
            for s in slots:
                if self.slot_cnt[s] > 0:
                    self._wait(e, ('d', s), self.slot_cnt[s])


Sched.barrier = _barrier


NT = 2080
NPR = 2064
NS = 16
BLK = 256
NTOKMAX = 288
D = 1024
NIN = 6928
DFF = 2816

PVO = {}
_o = 0
for _n, _w in [('norm_mix', 8), ('norm_ffn', 8), ('gla_gate_b', 2), ('gla_norm', 4), ('rwkv_mu', 14),
               ('rwkv_w0', 4), ('rwkv_a0', 4), ('rwkv_k_k', 4), ('rwkv_k_a', 4), ('rwkv_r_k', 4),
               ('rwkv_ln_w', 4), ('rwkv_ln_b', 4), ('s5_d', 4), ('s5_glu_b', 4), ('s5_are', 16),
               ('s5_aim', 16), ('s5_ldt', 16), ('final_norm', 8)]:
    PVO[_n] = _o
    _o += _w
NV = _o
CO = {n: i * 128 for i, n in enumerate(['MU', 'MUI', 'MLn', 'MUn', 'ident', 'ones', 'bones'])}
NCONST = 7 * 128

BIGW = [('w_in', D, NIN), ('w_br_gla', 512, D), ('w_br_rwkv', 512, D), ('w_br_s5', 512, D), ('w_out', D, D),
        ('ffn_w1', D, DFF), ('ffn_w3', D, DFF), ('ffn_w2', DFF, D)]


def build_nc(depth=4, nblk=None):
    nc = bass.Bass("TRN2", target_bir_lowering=False)
    din = {}

    def I(name, shape, dt=F32):
        din[name] = nc.dram_tensor(name, shape, dt, kind="ExternalInput").ap()
        return din[name]

    def O(name, shape):
        return nc.dram_tensor(name, shape, F32, kind="ExternalOutput").ap()

    def SC(name, shape, dt):
        return nc.dram_tensor(name, shape, dt, kind="Internal").ap()

    xin = I("xin", [NT, D])
    sgla = I("sgla", [4, NS, 4, 64, 128])
    srw = I("srw", [4, NS, 8, 64, 64])
    ssh = I("ssh", [4, NS, 1792])
    s5r = I("s5r", [4, NS, 2048])
    s5i = I("s5i", [4, NS, 2048])
    pv = I("pv", [4, 128, NV])
    consts = I("consts", [128, NCONST])
    gw2 = I("gla_gate_w2", [4, 16, 256])
    rw2 = I("rwkv_w2", [4, 64, 512])
    ra2 = I("rwkv_a2", [4, 64, 512])
    rg2 = I("rwkv_g2", [4, 128, 512])
    gluw = I("s5_glu_w", [4, 512, 512])
    bpr = I("bpad_re", [4, 128, 16, 128])
    bpi = I("bpad_im", [4, 128, 16, 128])
    cpr = I("cpad_re", [4, 128, 16, 128])
    cpi = I("cpad_im", [4, 128, 16, 128])
    wf = {n: I(n, [4, r, c]) for n, r, c in BIGW}
    wb = {n: SC(n + "_b", [4, r, c], BF16) for n, r, c in BIGW}
    xres = SC("xres", [128, 8, NT], F32)
    tabd = SC("tabd", [128, 4, 16, 128], F32)

    yp = O("yp", [2048, D])
    ys = O("ys", [NS, D])
    glap = O("glap", [4, 4, 64, 128])
    rwp = O("rwp", [4, 8, 64, 64])
    shp = O("shp", [4, 14, 128])
    s5rp = O("s5rp", [4, 16, 128])
    s5ip = O("s5ip", [4, 16, 128])
    glas = O("glas", [4, NS, 4, 64, 128])
    rws = O("rws", [4, NS, 8, 64, 64])
    shs = O("shs", [4, NS, 1792])
    s5rs = O("s5rs", [4, NS, 2048])
    s5is = O("s5is", [4, NS, 2048])

    st = contextlib.ExitStack()
    with st:
        S = Sched(nc, st)
        T = lambda name, shape, dt=F32: st.enter_context(nc.sbuf_tensor(name, shape, dt))
        PS = lambda name, shape, dt=F32: st.enter_context(nc.psum_tensor(name, shape, dt))

        def ACT(out, in_, func, R, W, bias=None, scale=None):
            kw = {}
            if bias is not None:
                kw['bias'] = bias
            if scale is not None:
                kw['scale'] = scale
            S.op('act', lambda: nc.scalar.activation(out=out, in_=in_, func=func, **kw), R, W)

        def EN(eng):
            return nc.vector if eng == 'dve' else nc.gpsimd

        def TT(eng, out, a, b, op, R, W):
            S.op(eng, lambda: EN(eng).tensor_tensor(out=out, in0=a, in1=b, op=op), R, W)

        def TS(eng, out, a, s1, op0, R, W, s2=None, op1=None):
            if op1 is None:
                S.op(eng, lambda: EN(eng).tensor_scalar(out=out, in0=a, scalar1=s1, scalar2=None, op0=op0), R, W)
            else:
                S.op(eng, lambda: EN(eng).tensor_scalar(out=out, in0=a, scalar1=s1, scalar2=s2, op0=op0, op1=op1), R, W)

        def STT(out, a, s, b, op0, op1, R, W):
            S.op('dve', lambda: nc.vector.scalar_tensor_tensor(out=out, in0=a, scalar=s, in1=b, op0=op0, op1=op1), R, W)

        def CP(eng, out, in_, R, W):
            if eng == 'act':
                S.op('act', lambda: nc.scalar.copy(out=out, in_=in_), R, W)
            else:
                S.op(eng, lambda: EN(eng).tensor_copy(out=out, in_=in_), R, W)

        def MM(out, lhsT, rhs, R, W, start=True, stop=True, chain=False):
            S.op('pe', lambda: nc.tensor.matmul(out, lhsT=lhsT, rhs=rhs, start=start, stop=stop), R, W, skip_same=chain)

        def TR(out, in_, ident, R, W):
            S.op('pe', lambda: nc.tensor.transpose(out=out, in_=in_, identity=ident), R, W)

        def SCAN(out, d0, d1, init, R, W):
            S.op('dve', lambda: nc.vector.tensor_tensor_scan(out=out, data0=d0, data1=d1, initial=init, op0=ALU.mult, op1=ALU.add), R, W)

        def RECIP(out, in_, R, W):
            S.op('dve', lambda: nc.vector.reciprocal(out=out, in_=in_), R, W)

        def MSET(eng, ap, v, W):
            S.op(eng, lambda: EN(eng).memset(ap, v), (), W)

        M, A, SUB = ALU.mult, ALU.add, ALU.subtract

        def RR(ap):
            return ap.bitcast(mybir.dt.float32r) if FP32R else ap

        cst = T("cst", [128, NCONST])
        cstb = T("cstb", [128, 128], BF16)
        pvt = T("pvt", [128, NV])
        pvd = T("pvd", [128, 8])
        gw2b = T("gw2b", [128, 256], BF16)
        w2b = T("w2b", [128, 512], BF16)
        a2b = T("a2b", [128, 512], BF16)
        g2b = T("g2b", [128, 512], BF16)
        glub = T("glub", [128, 4, 512], BF16)
        xb = T("xb", [128, 8, NTOKMAX])
        hb = T("hb", [128, 8, NTOKMAX], BF16)
        og = T("og", [128, 3, 4, NTOKMAX], BF16)
        Sg_fs = [T(f"Sg_f{i}", [128, 2, 128]) for i in range(3)]
        Sg_bs = [T(f"Sg_b{i}", [128, 2, 128], BF16) for i in range(3)]
        Sr_fs = [T(f"Sr_f{i}", [128, 4, 64]) for i in range(3)]
        Sr_bs = [T(f"Sr_b{i}", [128, 4, 64], BF16) for i in range(3)]
        x5 = T("x5", [128, 16, 2])
        shprev = T("shprev", [128, 14, 1])
        NWB = 4
        wbuf = [T(f"wbuf{i}", [128, 8, 512], BF16) for i in range(NWB)]
        RF = T("RF", [128, 26000])
        RB = T("RB", [128, 14200], BF16)
        pp = [PS(f"pp{i}", [128, 512]) for i in range(2)]
        pn = PS("pn", [128, 512])
        ptb = PS("ptb", [128, 1024], BF16)
        pc = [PS(f"pc{i}", [128, 512]) for i in range(2)]
        py = PS("py", [128, 512])
        pst = PS("pst", [128, 512])
        block = st.enter_context(nc.Block())

        MU = cst[:, CO['MU']:CO['MU'] + 128]
        MUI = cst[:, CO['MUI']:CO['MUI'] + 128]
        MLn = cst[:, CO['MLn']:CO['MLn'] + 128]
        MUn = cst[:, CO['MUn']:CO['MUn'] + 128]
        identf = cst[:, CO['ident']:CO['ident'] + 128]
        onesf = cst[:, CO['ones']:CO['ones'] + 128]
        bonesf = cst[:, CO['bones']:CO['bones'] + 128]
        CK = ['cst']

        al = {'f': 0, 'b': 0, 'n': 0}

        mark = T("mark", [128, 2])

        def phase(soft=False):
            if not soft:
                S.barrier()
                al['f'] = 0
                al['b'] = 0
            if MARK:
                S.op('pool', lambda: nc.gpsimd.memset(mark[:, 0:1], 1.0), (), ['mark'])

        def FA(c, t):
            o = al['f']
            al['f'] += c * t
            assert al['f'] <= 25900, al['f']
            al['n'] += 1
            return RF[:, o:o + c * t].rearrange("p (c t) -> p c t", c=c), f"F{al['n']}"

        def BA(c, t):
            o = al['b']
            al['b'] += c * t
            assert al['b'] <= 14200, al['b']
            al['n'] += 1
            return RB[:, o:o + c * t].rearrange("p (c t) -> p c t", c=c), f"B{al['n']}"

        wt = {'i': 0, 'j': 0, 'p': 0, 'c': 0}

        WROWS = {n: r for n, r, c in BIGW}

        def WK(name, l):
            return [('wb', name, l, r0) for r0 in range(0, WROWS[name], 128)]

        pref = {}

        def prefetch_w(name, l, c0, nk, ncols):
            i = wt['i']
            wt['i'] = (i + 1) % NWB
            src = wb[name][l, :, c0:c0 + ncols]
            S.dma('sp', wbuf[i][:, 0:nk, 0:ncols], src.rearrange("(kc p) n -> p kc n", p=128), reads=WK(name, l), writes=[('wbuf', i)], bg='p')
            pref[(name, l, c0, ncols)] = i

        def load_w(name, l, c0, nk, ncols):
            if (name, l, c0, ncols) in pref:
                return pref.pop((name, l, c0, ncols))
            i = wt['i']
            wt['i'] = (i + 1) % NWB
            src = wb[name][l, :, c0:c0 + ncols]
            S.dma('sp', wbuf[i][:, 0:nk, 0:ncols], src.rearrange("(kc p) n -> p kc n", p=128), reads=WK(name, l), writes=[('wbuf', i)])
            return i

        def convert_list(l):
            return [(n, l, r0) for n, r, c in BIGW for r0 in range(0, r, 128)]

        def issue_convert(items):
            for (n, l, r0) in items:
                S.dma('pool', wb[n][l, r0:r0 + 128, :], wf[n][l, r0:r0 + 128, :], writes=[('wb', n, l, r0)], bg=True)

        def nextpp():
            wt['p'] ^= 1
            return wt['p']

        def nextpc():
            wt['c'] ^= 1
            return wt['c']

        def proj(wi, nk, c0, m, rhs, rk, n, psb, prow=None):
            for kc in range(nk):
                MM(pp[psb][0:m, 0:n], wbuf[wi][:, kc, c0:c0 + m], rhs[:, kc, 0:n], [('wbuf', wi)] + rk, [('pp', psb)],
                   start=(kc == 0), stop=(kc == nk - 1), chain=True)

        S.dma('sp', cst[:], consts, writes=CK)
        CP('dve', cstb[:], identf, CK, ['cstb'])
        issue_convert(convert_list(0))
        tiles_all = [(t0, 128) for t0 in range(0, 2048, 128)] + [(2048, 32)]
        for (t0, L) in tiles_all:
            xt, xk = RF[:, 0:1024], 'xt0'
            S.dma('sp', xt[0:L, :], xin[t0:t0 + L, :], writes=[xk])
            for c in range(8):
                b = nextpp()
                TR(pp[b][:, 0:L], xt[0:L, c * 128:(c + 1) * 128], identf[0:L, 0:L], [xk] + CK, [('pp', b)])
                CP('act' if c % 2 else 'dve', xb[:, c, 0:L], pp[b][:, 0:L], [('pp', b)], ['xb'])
            S.dma('sp', xres[:, :, t0:t0 + L], xb[:, :, 0:L], reads=['xb'], writes=['xres'])
        S.barrier()

        def rmsnorm(goff, n):
            sq, sk = FA(2, NTOKMAX)
            rs, rk = FA(1, NTOKMAX)
            for c in range(8):
                ACT(sq[:, c % 2, 0:n], xb[:, c, 0:n], AF.Square, ['xb'], [(sk, c % 2)])
                MM(pn[:, 0:n], onesf, sq[:, c % 2, 0:n], [(sk, c % 2)] + CK, ['pn'], start=(c == 0), stop=(c == 7), chain=True)
            ACT(rs[:, 0, 0:n], pn[:, 0:n], AF.Sqrt, ['pn'], [rk], bias=1e-6, scale=1.0 / 1024)
            RECIP(rs[:, 0, 0:n], rs[:, 0, 0:n], [rk], [rk])
            return rs, rk

        nb_all = 8 if nblk is None else nblk
        for l in range(depth):
            phase()
            S.dma('sp', pvt[:], pv[l], writes=['pvt'])
            S.dma('pool', gw2b[0:16, :], gw2[l], writes=['gw2b'])
            S.dma('pool', w2b[0:64, :], rw2[l], writes=['w2b'])
            S.dma('pool', a2b[64:128, :], ra2[l], writes=['a2b'])
            S.dma('pool', g2b[:], rg2[l], writes=['g2b'])
            S.dma('pool', glub[:], gluw[l].rearrange("(kc p) n -> p kc n", p=128), writes=['glub'])
            PK = ['pvt', 'pvd']
            TS('dve', pvd[:, 0:2], pvt[:, PVO['gla_gate_b']:PVO['gla_gate_b'] + 2], -1.0, M, ['pvt'], ['pvd'])
            TS('dve', pvd[:, 2:6], pvt[:, PVO['rwkv_k_a']:PVO['rwkv_k_a'] + 4], -1.0, M, ['pvt'], ['pvd'], s2=1.0, op1=A)
            c5, c5k = FA(16, 16)
            K5 = [c5k]
            are = pvt[:, PVO['s5_are']:PVO['s5_are'] + 16]
            aim = pvt[:, PVO['s5_aim']:PVO['s5_aim'] + 16]
            ldt = pvt[:, PVO['s5_ldt']:PVO['s5_ldt'] + 16]
            dt_, mag, th, cs, sn, abr, abi, den, nr, cre, cim, t1, t2, iar, iai = [c5[:, i, :] for i in range(15)]
            ACT(dt_, ldt, AF.Exp, ['pvt'], K5)
            TT('dve', mag, are, dt_, M, ['pvt'] + K5, K5)
            ACT(mag, mag, AF.Exp, K5, K5)
            TT('dve', th, aim, dt_, M, ['pvt'] + K5, K5)
            pi = float(np.pi)
            ACT(sn, th, AF.Sin, K5, K5, scale=1.0 / 32)
            ACT(cs, th, AF.Sin, K5, K5, scale=1.0 / 32, bias=pi / 2)
            for _ in range(5):
                TT('dve', t1, sn, cs, M, K5, K5)
                TT('dve', t2, sn, sn, M, K5, K5)
                TS('dve', sn, t1, 2.0, M, K5, K5)
                TS('dve', cs, t2, -2.0, M, K5, K5, s2=1.0, op1=A)
            TT('dve', abr, mag, cs, M, K5, K5)
            TT('dve', abi, mag, sn, M, K5, K5)
            TT('dve', t1, are, are, M, ['pvt'] + K5, K5)
            TT('dve', t2, aim, aim, M, ['pvt'] + K5, K5)
            TT('dve', den, t1, t2, A, K5, K5)
            RECIP(den, den, K5, K5)
            TS('dve', nr, abr, -1.0, A, K5, K5)
            TT('dve', t1, nr, are, M, ['pvt'] + K5, K5)
            TT('dve', t2, abi, aim, M, ['pvt'] + K5, K5)
            TT('dve', t1, t1, t2, A, K5, K5)
            TT('dve', cre, t1, den, M, K5, K5)
            TT('dve', t1, abi, are, M, ['pvt'] + K5, K5)
            TT('dve', t2, nr, aim, M, ['pvt'] + K5, K5)
            TT('dve', t1, t1, t2, SUB, K5, K5)
            TT('dve', cim, t1, den, M, K5, K5)
            TT('dve', t1, abr, abr, M, K5, K5)
            TT('dve', t2, abi, abi, M, K5, K5)
            TT('dve', t1, t1, t2, A, K5, K5)
            RECIP(t1, t1, K5, K5)
            TT('dve', iar, abr, t1, M, K5, K5)
            TT('dve', iai, abi, t1, M, K5, K5)
            TS('dve', iai, iai, -1.0, M, K5, K5)
            s5c = T(f"s5c{l}", [128, 4, 16])
            for i, src in enumerate([abr, abi, cre, cim]):
                CP('dve', s5c[:, i, :], src, K5, ['s5c'])
            tab, tbk = FA(4 * 16, 128)
            tabv = tab.rearrange("p (a s) j -> p a s j", a=4)

            tA, tAk = FA(16, 128)
            tB, tBk = FA(16, 128)
            tC, tCk = FA(16, 128)
            for a_, src in enumerate([abr, abi, iar, iai]):
                CP('dve', tabv[:, a_, :, 0], src, K5, [tbk])

            def cmul_all(ore, oim, ire, iim, sre, sim, m):
                TT('dve', tA[:, :, 0:m], iim, sim, M, [tbk] + K5, [tAk])
                TT('pool', tB[:, :, 0:m], ire, sre, M, [tbk] + K5, [tBk])
                TT('dve', tC[:, :, 0:m], tB[:, :, 0:m], tA[:, :, 0:m], SUB, [tAk, tBk], [tCk])
                TT('dve', tA[:, :, 0:m], ire, sim, M, [tbk] + K5, [tAk])
                TT('pool', tB[:, :, 0:m], iim, sre, M, [tbk] + K5, [tBk])
                TT('dve', oim, tB[:, :, 0:m], tA[:, :, 0:m], A, [tAk, tBk], [tbk])
                CP('pool', ore, tC[:, :, 0:m], [tCk], [tbk])

            m = 1
            while m < 128:
                for a_ in (0, 2):
                    cmul_all(tabv[:, a_, :, m:2 * m], tabv[:, a_ + 1, :, m:2 * m], tabv[:, a_, :, 0:m], tabv[:, a_ + 1, :, 0:m],
                             tabv[:, a_, :, m - 1:m].broadcast_to([128, 16, m]), tabv[:, a_ + 1, :, m - 1:m].broadcast_to([128, 16, m]), m)
                m *= 2
            cmul_all(tabv[:, 2, :, :], tabv[:, 3, :, :], tabv[:, 2, :, :], tabv[:, 3, :, :],
                     cre.unsqueeze(2).broadcast_to([128, 16, 128]), cim.unsqueeze(2).broadcast_to([128, 16, 128]), 128)
            S.dma('sp', tabd.rearrange("p a s j -> p (a s) j"), tab, reads=[tbk], writes=['tabd'])
            MSET('dve', Sg_fs[0][:], 0.0, [('Sg_f', 0)])
            MSET('dve', Sg_bs[0][:], 0.0, [('Sg_b', 0)])
            MSET('dve', Sr_fs[0][:], 0.0, [('Sr_f', 0)])
            MSET('dve', Sr_bs[0][:], 0.0, [('Sr_b', 0)])
            MSET('dve', x5[:], 0.0, [('x5', s_) for s_ in range(16)])
            MSET('dve', shprev[:], 0.0, ['shprev'])
            cvt_next = convert_list(l + 1) if l + 1 < depth else []

            for bi in range(nb_all):
                t0 = bi * BLK
                last = (bi == 7)
                n = 288 if last else 256
                tiles = [(0, 128, None), (128, 128, None)]
                if last:
                    tiles += [(256, 16, None)] + [(272 + s, 1, s) for s in range(NS)]
                npr = 272 if last else 256

                phase()
                ncv = (len(cvt_next) + nb_all - 1) // nb_all
                issue_convert(cvt_next[bi * ncv:(bi + 1) * ncv])
                S.dma('sp', xb[:, :, 0:n], xres[:, :, t0:t0 + n], reads=['xres'], writes=['xb'])
                rs, rk = rmsnorm(PVO['norm_mix'], n)
                for c in range(8):
                    STT(hb[:, c, 0:n], xb[:, c, 0:n], pvt[:, PVO['norm_mix'] + c:PVO['norm_mix'] + c + 1], rs[:, 0, 0:n], M, M,
                        ['xb', 'pvt', rk], ['hb'])
                HK = ['hb']

                prefetch_w('w_in', l, 0, 8, 512)
                prefetch_w('w_in', l, 512, 8, 512)
                phase()
                qk, qkk = FA(4, NTOKMAX)
                vbf, vk = BA(4, NTOKMAX)
                gg, ggk = FA(4, NTOKMAX)
                zg, zgk = BA(1, NTOKMAX)
                la, lak = FA(2, NTOKMAX)
                of, ofk = FA(4, NTOKMAX)
                wi = load_w('w_in', l, 0, 8, 512)
                for j in range(4):
                    b = nextpp()
                    proj(wi, 8, j * 128, 128, hb, HK, n, b)
                    CP('act', qk[:, j, 0:n], pp[b][:, 0:n], [('pp', b)], [qkk])
                wi = load_w('w_in', l, 512, 8, 512)
                for j in range(4):
                    b = nextpp()
                    proj(wi, 8, j * 128, 128, hb, HK, n, b)
                    CP('act', vbf[:, j, 0:n], pp[b][:, 0:n], [('pp', b)], [vk])
                wi = load_w('w_in', l, 1024, 8, 512)
                for j in range(4):
                    b = nextpp()
                    proj(wi, 8, j * 128, 128, hb, HK, n, b)
                    ACT(gg[:, j, 0:n], pp[b][:, 0:n], AF.Silu, [('pp', b)], [ggk])
                wi = load_w('w_in', l, 1536, 8, 16)
                b = nextpp()
                proj(wi, 8, 0, 16, hb, HK, n, b)
                CP('act', zg[0:16, 0, 0:n], pp[b][0:16, 0:n], [('pp', b)], [zgk])
                for c in range(2):
                    b = nextpp()
                    MM(pp[b][:, 0:n], gw2b[0:16, c * 128:(c + 1) * 128], zg[0:16, 0, 0:n], ['gw2b', zgk], [('pp', b)])
                    ACT(la[:, c, 0:n], pp[b][:, 0:n], AF.Exp, [('pp', b)] + PK, [lak], bias=pvd[:, c:c + 1], scale=-1.0)
                    ACT(la[:, c, 0:n], la[:, c, 0:n], AF.Ln, [lak], [lak], bias=1.0)
                    TS('dve', la[:, c, 0:n], la[:, c, 0:n], -1.0 / 16, M, [lak], [lak])
                Bc, Bk = FA(2, 128)
                Ep, Epk = FA(2, 128)
                Em, Emk = FA(2, 128)
                tmpS, tmpk = FA(2, 128)
                qt, qtk = BA(2, 128)
                kt, ktk = BA(2, 128)
                Ktm, Ktk = BA(1, 256)
                Vtm, Vtk = BA(1, 512)
                STb, STk = BA(4, 128)
                for (off, L, smp) in tiles:
                    sp_ = 0 if smp is None else 1 + (smp % 2)
                    Sg_f, Sg_b, SgFk, SgBk = Sg_fs[sp_], Sg_bs[sp_], ('Sg_f', sp_), ('Sg_b', sp_)
                    if smp is not None:
                        S.dma('sp', Sg_f[:], sgla[l, smp].rearrange("(c hp) d e -> (hp d) c e", hp=2), writes=[SgFk])
                        CP('dve', Sg_b[:], Sg_f[:], [SgFk], [SgBk])
                    for c in range(2):
                        SCAN(Bc[:, c, 0:L], onesf[:, 0:L], la[:, c, off:off + L], 0.0, [lak] + CK, [Bk])
                    ACT(Ep[:, :, 0:L], Bc[:, :, 0:L], AF.Exp, [Bk], [Epk])
                    ACT(Em[:, :, 0:L], Bc[:, :, 0:L], AF.Exp, [Bk], [Emk], scale=-1.0)
                    STT(qt[:, :, 0:L], qk[:, 0:2, off:off + L], 0.125, Ep[:, :, 0:L], M, M, [qkk, Epk], [qtk])
                    TT('dve', kt[:, :, 0:L], qk[:, 2:4, off:off + L], Em[:, :, 0:L], M, [qkk, Emk], [ktk])
                    for c in range(2):
                        TR(ptb[0:L, c * 128:(c + 1) * 128], kt[:, c, 0:L], cstb[:], [ktk, 'cstb'], ['ptb'])
                    CP('act', Ktm[0:L, 0, :], ptb[0:L, 0:256], ['ptb'], [Ktk])
                    for hh in range(4):
                        TR(ptb[0:L, 256 + hh * 128:256 + (hh + 1) * 128], vbf[:, hh, off:off + L], cstb[:], [vk, 'cstb'], ['ptb'])
                    CP('dve', Vtm[0:L, 0, :], ptb[0:L, 256:768], ['ptb'], [Vtk])
                    gbanks = [(pc[0], ('pc', 0)), (pc[1], ('pc', 1)), (pp[0], ('pp', 0)), (pp[1], ('pp', 1))]
                    obanks = [(py, 'py'), (pn, 'pn')]
                    for h in range(4):
                        c, hp = h // 2, h % 2
                        P = slice(64 * hp, 64 * hp + 64)
                        t_, k_ = gbanks[h]
                        MM(t_[0:L, 0:L], kt[P, c, 0:L], qt[P, c, 0:L], [ktk, qtk], [k_])
                        TT('dve', STb[0:L, h, 0:L], t_[0:L, 0:L], MUI[0:L, 0:L], M, [k_] + CK, [(STk, h)])
                    for h in range(4):
                        c, hp = h // 2, h % 2
                        P = slice(64 * hp, 64 * hp + 64)
                        ob, obk = obanks[h % 2]
                        MM(ob[:, 0:L], Sg_b[P, c, :], qt[P, c, 0:L], [SgBk, qtk], [obk], start=True, stop=False)
                        MM(ob[:, 0:L], Vtm[0:L, 0, h * 128:(h + 1) * 128], STb[0:L, h, 0:L], [Vtk, (STk, h)], [obk], start=False, stop=True)
                        CP('act' if h % 2 else 'dve', of[:, h, off:off + L], ob[:, 0:L], [obk], [ofk])
                    for h in range(4):
                        c, hp = h // 2, h % 2
                        P = slice(64 * hp, 64 * hp + 64)
                        MM(pst[P, c * 128:(c + 1) * 128], Ktm[0:L, 0, c * 128 + 64 * hp:c * 128 + 64 * hp + 64], Vtm[0:L, 0, h * 128:(h + 1) * 128],
                           [Ktk, Vtk], ['pst'])
                    for c in range(2):
                        TT('dve', tmpS[:, c, :], Sg_f[:, c, :], pst[:, c * 128:(c + 1) * 128], A, [SgFk, 'pst'], [tmpk])
                        TS('dve', Sg_f[:, c, :], tmpS[:, c, :], Ep[:, c, L - 1:L], M, [tmpk, Epk], [SgFk])
                    CP('act', Sg_b[:], Sg_f[:], [SgFk], [SgBk])
                    if smp is not None:
                        S.dma('sp', glas[l, smp].rearrange("(c hp) d e -> (hp d) c e", hp=2), Sg_f[:], reads=[SgFk], is_output=True)
                    elif last and off == 256:
                        S.dma('sp', glap[l].rearrange("(c hp) d e -> (hp d) c e", hp=2), Sg_f[:], reads=[SgFk], is_output=True)
                sq, sk = FA(1, NTOKMAX)
                r2, r2k = FA(1, NTOKMAX)
                for h in range(4):
                    ACT(sq[:, 0, 0:n], of[:, h, 0:n], AF.Square, [ofk], [sk])
                    MM(pn[:, 0:n], onesf, sq[:, 0, 0:n], [sk] + CK, ['pn'])
                    ACT(r2[:, 0, 0:n], pn[:, 0:n], AF.Sqrt, ['pn'], [r2k], bias=1e-6, scale=1.0 / 128)
                    RECIP(r2[:, 0, 0:n], r2[:, 0, 0:n], [r2k], [r2k])
                    STT(sq[:, 0, 0:n], of[:, h, 0:n], pvt[:, PVO['gla_norm'] + h:PVO['gla_norm'] + h + 1], r2[:, 0, 0:n], M, M, [ofk, r2k, 'pvt'], [sk])
                    TT('dve', og[:, 0, h, 0:n], sq[:, 0, 0:n], gg[:, h, 0:n], M, [sk, ggk], ['og0'])
                if 'R' in PH:
                    prefetch_w('w_in', l, 1552, 8, 512)
                    prefetch_w('w_in', l, 2064, 8, 512)
                    phase()
                    xs, xsk = FA(14, NTOKMAX)
                    pbufs = [FA(1, NTOKMAX + 2) for _ in range(2)]
                    dds = [FA(1, NTOKMAX) for _ in range(2)]
                    stg, stgk = FA(1, 1792)
                    if last:
                        shin, shink = FA(14, 16)
                        pso, psok = FA(14, 16)
                        S.dma('sp', stg[0:16, 0, :], ssh[l], writes=[stgk])
                        for cc in range(14):
                            b = nextpp()
                            TR(pp[b][:, 0:16], stg[0:16, 0, cc * 128:(cc + 1) * 128], identf[0:16, 0:16], [stgk] + CK, [('pp', b)])
                            CP('act', shin[:, cc, :], pp[b][:, 0:16], [('pp', b)], [shink])
                    mu0 = PVO['rwkv_mu']
                    for gi, (c0, ncol) in enumerate([(1552, 512), (2064, 512), (2576, 512), (3088, 256)]):
                        wi = load_w('w_in', l, c0, 8, ncol)
                        for j in range(ncol // 128):
                            cc = gi * 4 + j
                            pbuf, pbk = pbufs[cc % 2]
                            dd, ddk = dds[cc % 2]
                            b = nextpp()
                            proj(wi, 8, j * 128, 128, hb, HK, n, b)
                            CP('act', pbuf[:, 0, 1:n + 1], pp[b][:, 0:n], [('pp', b)], [pbk])
                            CP('dve', pbuf[:, 0, 0:1], shprev[:, cc, :], ['shprev'], [pbk])
                            TT('dve', dd[:, 0, 0:npr], pbuf[:, 0, 0:npr], pbuf[:, 0, 1:npr + 1], SUB, [pbk], [ddk])
                            if last:
                                TT('dve', dd[:, 0, 272:288], shin[:, cc, :], pbuf[:, 0, 273:289], SUB, [pbk, shink], [ddk])
                                CP('pool', pso[:, cc, :], pbuf[:, 0, 273:289], [pbk], [psok])
                            STT(xs[:, cc, 0:n], dd[:, 0, 0:n], pvt[:, mu0 + cc:mu0 + cc + 1], pbuf[:, 0, 1:n + 1], M, A, [ddk, pbk, 'pvt'], [xsk])
                            CP('pool', shprev[:, cc, :], pbuf[:, 0, npr:npr + 1], [pbk], ['shprev'])
                    if last:
                        b = nextpp()
                        TR(pp[b][0:14, 0:128], shprev[:, :, 0], identf, ['shprev'] + CK, [('pp', b)])
                        CP('act', stg[0:14, 0, 0:128], pp[b][0:14, 0:128], [('pp', b)], [stgk])
                        S.dma('sp', shp[l], stg[0:14, 0, 0:128], reads=[stgk], is_output=True)
                        for cc in range(14):
                            b = nextpp()
                            TR(pp[b][0:16, 0:128], pso[:, cc, :], identf, [psok] + CK, [('pp', b)])
                            CP('act', stg[0:16, 0, cc * 128:(cc + 1) * 128], pp[b][0:16, 0:128], [('pp', b)], [stgk])
                        S.dma('sp', shs[l], stg[0:16, 0, :], reads=[stgk], is_output=True)
                    tzw, tzk = BA(1, NTOKMAX)
                    zab, zak = BA(1, NTOKMAX)
                    szg, szk = BA(1, NTOKMAX)
                    ACT(tzw[0:64, 0, 0:n], xs[0:64, 12, 0:n], AF.Tanh, [xsk], [tzk])
                    CP('dve', zab[64:128, 0, 0:n], xs[64:128, 12, 0:n], [xsk], [zak])
                    ACT(szg[:, 0, 0:n], xs[:, 13, 0:n], AF.Sigmoid, [xsk], [szk])
                    lw, lwk = FA(4, NTOKMAX)
                    kk, kkk = FA(4, NTOKMAX)
                    bb, bbk = FA(4, NTOKMAX)
                    gq, gqk = FA(4, NTOKMAX)
                    bon, bonk = FA(4, NTOKMAX)
                    aa, aak = FA(4, NTOKMAX)
                    t1, t1k = FA(4, NTOKMAX)
                    t2, t2k = FA(4, NTOKMAX)
                    rbs = [(pp[0], ('pp', 0)), (pp[1], ('pp', 1)), (pc[0], ('pc', 0)), (pc[1], ('pc', 1))]
                    rbn = [(pn, 'pn'), (py, 'py'), (pst, 'pst'), (pc[0], ('pc', 0))]
                    C4 = range(4)
                    csl = lambda c: slice(c * 128, (c + 1) * 128)
                    pcol = lambda nm, c: pvt[:, PVO[nm] + c:PVO[nm] + c + 1]
                    for c in C4:
                        t_, k_ = rbs[c]
                        MM(t_[:, 0:n], w2b[0:64, csl(c)], tzw[0:64, 0, 0:n], ['w2b', tzk], [k_])
                        ACT(lw[:, c, 0:n], t_[:, 0:n], AF.Sigmoid, [k_, 'pvt'], [(lwk, c)], bias=pcol('rwkv_w0', c))
                    for c in C4:
                        t_, k_ = rbs[c]
                        MM(t_[:, 0:n], a2b[64:128, csl(c)], zab[64:128, 0, 0:n], ['a2b', zak], [k_])
                        ACT(aa[:, c, 0:n], t_[:, 0:n], AF.Sigmoid, [k_, 'pvt'], [(aak, c)], bias=pcol('rwkv_a0', c))
                    for c in C4:
                        t_, k_ = rbs[c]
                        MM(t_[:, 0:n], g2b[:, csl(c)], szg[:, 0, 0:n], ['g2b', szk], [k_])
                        CP('act', gq[:, c, 0:n], t_[:, 0:n], [k_], [gqk])
                    for c in C4:
                        TS('pool', lw[:, c, 0:n], lw[:, c, 0:n], -0.6065306597126334, M, [(lwk, c)], [(lwk, c)])
                        TS('dve', kk[:, c, 0:n], xs[:, 4 + c, 0:n], pcol('rwkv_k_k', c), M, [xsk, 'pvt'], [(kkk, c)])
                    for c in C4:
                        ACT(t1[:, c, 0:n], kk[:, c, 0:n], AF.Square, [(kkk, c)], [(t1k, c)])
                    for c in C4:
                        t_, k_ = rbn[c]
                        MM(t_[:, 0:n], bonesf, t1[:, c, 0:n], [(t1k, c)] + CK, [k_])
                        ACT(t1[:, c, 0:n], t_[:, 0:n], AF.Sqrt, [k_], [(t1k, c)], bias=1e-12)
                    for c in C4:
                        RECIP(t1[:, c, 0:n], t1[:, c, 0:n], [(t1k, c)], [(t1k, c)])
                        TS('dve', t2[:, c, 0:n], aa[:, c, 0:n], pcol('rwkv_k_a', c), M, [(aak, c)] + PK, [(t2k, c)], s2=pvd[:, 2 + c:3 + c], op1=A)
                    for c in C4:
                        TT('pool', kk[:, c, 0:n], kk[:, c, 0:n], t1[:, c, 0:n], M, [(kkk, c), (t1k, c)], [(kkk, c)])
                        TT('pool', xs[:, 4 + c, 0:n], xs[:, 4 + c, 0:n], t2[:, c, 0:n], M, [xsk, (t2k, c)], [xsk])
                    for c in C4:
                        TT('pool', bb[:, c, 0:n], kk[:, c, 0:n], aa[:, c, 0:n], M, [(kkk, c), (aak, c)], [bbk])
                        STT(t1[:, c, 0:n], xs[:, c, 0:n], pcol('rwkv_r_k', c), xs[:, 4 + c, 0:n], M, M, [xsk, (kkk, c), 'pvt'], [(t1k, c)])
                    for c in C4:
                        t_, k_ = rbn[c]
                        MM(t_[:, 0:n], bonesf, t1[:, c, 0:n], [(t1k, c)] + CK, [k_])
                        TT('dve', bon[:, c, 0:n], t_[:, 0:n], xs[:, 8 + c, 0:n], M, [k_, xsk], [bonk])
                    XK = [xsk]
                    LWK = [(lwk, c) for c in C4]
                    KKK = [(kkk, c) for c in C4]
                    Bc, Bk = FA(4, 128)
                    Ep, Epk = FA(4, 128)
                    Em, Emk = FA(4, 128)
                    Ex, Exk = FA(4, 128)
                    rt, rtk = BA(4, 128)
                    kt, ktk = BA(4, 128)
                    bt, btk = BA(4, 128)
                    at, atk = BA(4, 128)
                    vb, vbk = BA(4, 128)
                    Ktm, Ktk = BA(1, 512)
                    Btm, Btk = BA(1, 512)
                    Vtm, Vtk = BA(1, 512)
                    NH = 8
                    Qb = [[BA(1, 128) for _ in range(2)] for _ in range(NH)]
                    NX = [[BA(1, 192) for _ in range(2)] for _ in range(NH)]
                    Nb = [[(NX[i][a][0][:, :, 0:128], (NX[i][a][1], 'n')) for a in range(2)] for i in range(NH)]
                    Xb = [[(NX[i][a][0][:, :, 128:192], (NX[i][a][1], 'x')) for a in range(2)] for i in range(NH)]
                    LkT = [BA(1, 128) for _ in range(NH)]
                    ArkT = [BA(1, 128) for _ in range(NH)]
                    ArbT = [BA(1, 128) for _ in range(NH)]
                    Un = [BA(1, 64) for _ in range(NH)]
                    pcs = [(pc[0], ('pc', 0)), (pc[1], ('pc', 1)), (pp[0], ('pp', 0)), (pp[1], ('pp', 1)), (pn, 'pn')]
                    pci = [0]

                    def npc():
                        pci[0] = (pci[0] + 1) % len(pcs)
                        return pcs[pci[0]]
                    yfr, yfrk = FA(4, NTOKMAX)
                    tmpS, tmpk = FA(4, 64)
                    stRs = [FA(8, 64) for _ in range(2)]
                    for (off, L, smp) in tiles:
                        sp_ = 0 if smp is None else 1 + (smp % 2)
                        Sr_f, Sr_b, SrFk, SrBk = Sr_fs[sp_], Sr_bs[sp_], ('Sr_f', sp_), ('Sr_b', sp_)
                        stR, stRk = stRs[sp_ % 2]
                        if smp is not None:
                            S.dma('sp', stR[0:64, :, :], srw[l, smp].rearrange("h v k -> v h k"), writes=[stRk])
                            for c in range(4):
                                b = nextpp()
                                TR(pp[b][:, 0:64], stR[0:64, 2 * c:2 * c + 2, :].rearrange("p a k -> p (a k)"), identf[0:64, 0:64], [stRk] + CK, [('pp', b)])
                                CP('act', Sr_f[:, c, :], pp[b][:, 0:64], [('pp', b)], [SrFk])
                            CP('dve', Sr_b[:], Sr_f[:], [SrFk], [SrBk])
                        o = 0 if smp is None else smp
                        if smp is None or smp == 0:
                            W_ = L if smp is None else NS
                            if smp is None:
                                for c in range(4):
                                    SCAN(Bc[:, c, 0:L], onesf[:, 0:L], lw[:, c, off:off + L], 0.0, LWK + CK, [Bk])
                            else:
                                CP('dve', Bc[:, :, 0:W_], lw[:, :, off:off + W_], LWK, [Bk])
                            ACT(Ep[:, :, 0:W_], Bc[:, :, 0:W_], AF.Exp, [Bk], [Epk])
                            ACT(Em[:, :, 0:W_], Bc[:, :, 0:W_], AF.Exp, [Bk], [Emk], scale=-1.0)
                            TT('pool', Ex[:, :, 0:W_], Bc[:, :, 0:W_], lw[:, :, off:off + W_], SUB, [Bk] + LWK, [Exk])
                            ACT(Ex[:, :, 0:W_], Ex[:, :, 0:W_], AF.Exp, [Exk], [Exk])
                            TT('dve', rt[:, :, 0:W_], xs[:, 0:4, off:off + W_], Ep[:, :, 0:W_], M, [xsk, Epk], [rtk])
                            TT('pool', kt[:, :, 0:W_], xs[:, 4:8, off:off + W_], Em[:, :, 0:W_], M, XK + [Emk], [ktk])
                            TT('pool', bt[:, :, 0:W_], bb[:, :, off:off + W_], Em[:, :, 0:W_], M, [bbk, Emk], [btk])
                            TT('pool', at[:, :, 0:W_], kk[:, :, off:off + W_], Ex[:, :, 0:W_], M, KKK + [Exk], [atk])
                            CP('act', vb[:, :, 0:W_], xs[:, 8:12, off:off + W_], [xsk], [vbk])
                        for (src, sk_, dst, dk_) in [(kt, ktk, Ktm, Ktk), (bt, btk, Btm, Btk), (vb, vbk, Vtm, Vtk)]:
                            for c in range(4):
                                TR(ptb[0:L, c * 128:(c + 1) * 128], src[:, c, o:o + L], cstb[:], [sk_, 'cstb'], ['ptb'])
                            CP('act', dst[0:L, 0, :], ptb[0:L, 0:512], ['ptb'], [dk_])
                        nlev = {128: 7, 16: 4, 1: 0}[L]
                        for grp in range(8 // NH):
                            HD = []
                            for i in range(NH):
                                h = grp * NH + i
                                c, hp = h // 2, h % 2
                                HD.append((i, h, c, slice(64 * hp, 64 * hp + 64), slice(h * 64, (h + 1) * 64)))
                            if nlev > 0:
                                for (i, h, c, P, hs) in HD:
                                    t_, k_ = npc()
                                    MM(t_[0:L, 0:L], bt[P, c, o:o + L], at[P, c, o:o + L], [btk, atk], [k_])
                                    TT('dve', RR(Qb[i][0][0][0:L, 0, 0:L]), t_[0:L, 0:L], MUn[0:L, 0:L], M, [k_] + CK, [Qb[i][0][1]])
                                for (i, h, c, P, hs) in HD:
                                    t_, k_ = npc()
                                    MM(t_[0:L, 0:L], at[P, c, o:o + L], bt[P, c, o:o + L], [btk, atk], [k_])
                                    TT('dve', RR(Nb[i][0][0][0:L, 0, 0:L]), t_[0:L, 0:L], MLn[0:L, 0:L], M, [k_] + CK, [Nb[i][0][1]])
                                for (i, h, c, P, hs) in HD:
                                    t_, k_ = npc()
                                    MM(t_[0:L, 0:L], kt[P, c, o:o + L], at[P, c, o:o + L], [ktk, atk], [k_])
                                    TT('dve', LkT[i][0][0:L, 0, 0:L], t_[0:L, 0:L], MU[0:L, 0:L], M, [k_] + CK, [LkT[i][1]])
                            for (i, h, c, P, hs) in HD:
                                t_, k_ = npc()
                                MM(t_[0:L, 0:L], kt[P, c, o:o + L], rt[P, c, o:o + L], [ktk, rtk], [k_])
                                TT('dve', ArkT[i][0][0:L, 0, 0:L], t_[0:L, 0:L], MUI[0:L, 0:L], M, [k_] + CK, [ArkT[i][1]])
                            for (i, h, c, P, hs) in HD:
                                t_, k_ = npc()
                                MM(t_[0:L, 0:L], bt[P, c, o:o + L], rt[P, c, o:o + L], [btk, rtk], [k_])
                                TT('dve', ArbT[i][0][0:L, 0, 0:L], t_[0:L, 0:L], MUI[0:L, 0:L], M, [k_] + CK, [ArbT[i][1]])
                            for (i, h, c, P, hs) in HD:
                                t_, k_ = npc()
                                MM(t_[0:L, 0:64], at[P, c, o:o + L], Sr_b[P, c, :], [atk, SrBk], [k_], start=True, stop=(nlev == 0))
                                if nlev > 0:
                                    MM(t_[0:L, 0:64], LkT[i][0][0:L, 0, 0:L], Vtm[0:L, 0, hs], [LkT[i][1], Vtk], [k_], start=False, stop=True)
                                CP('act', RR(Xb[i][0][0][0:L, 0, :]), t_[0:L, 0:64], [k_], [Xb[i][0][1]])
                            for k in range(nlev):
                                a_, b_ = k % 2, (k + 1) % 2
                                merged = (L == 128 and k < nlev - 2)
                                for (i, h, c, P, hs) in HD:
                                    t_, k_ = npc()
                                    if merged:
                                        MM(t_[0:L, 0:192], Qb[i][a_][0][0:L, 0, 0:L], NX[i][a_][0][0:L, 0, 0:192], [Qb[i][a_][1], Nb[i][a_][1], Xb[i][a_][1]], [k_])
                                        TT('dve', Xb[i][b_][0][0:L, 0, :], Xb[i][a_][0][0:L, 0, :], t_[0:L, 128:192], A, [Xb[i][a_][1], k_], [Xb[i][b_][1]])
                                        CP('act', Nb[i][b_][0][0:L, 0, 0:L], t_[0:L, 0:L], [k_], [Nb[i][b_][1]])
                                    else:
                                        MM(t_[0:L, 0:64], Qb[i][a_][0][0:L, 0, 0:L], Xb[i][a_][0][0:L, 0, :], [Qb[i][a_][1], Xb[i][a_][1]], [k_])
                                        TT('dve', Xb[i][b_][0][0:L, 0, :], Xb[i][a_][0][0:L, 0, :], t_[0:L, 0:64], A, [Xb[i][a_][1], k_], [Xb[i][b_][1]])
                                if k < nlev - 1:
                                    for (i, h, c, P, hs) in HD:
                                        t_, k_ = npc()
                                        MM(t_[0:L, 0:L], Nb[i][a_][0][0:L, 0, 0:L], Qb[i][a_][0][0:L, 0, 0:L], [Nb[i][a_][1], Qb[i][a_][1]], [k_])
                                        CP('act' if i % 2 else 'dve', Qb[i][b_][0][0:L, 0, 0:L], t_[0:L, 0:L], [k_], [Qb[i][b_][1]])
                                if k < nlev - 2 and not merged:
                                    for (i, h, c, P, hs) in HD:
                                        t_, k_ = npc()
                                        MM(t_[0:L, 0:L], Qb[i][a_][0][0:L, 0, 0:L], Nb[i][a_][0][0:L, 0, 0:L], [Nb[i][a_][1], Qb[i][a_][1]], [k_])
                                        CP('act' if i % 2 else 'dve', Nb[i][b_][0][0:L, 0, 0:L], t_[0:L, 0:L], [k_], [Nb[i][b_][1]])
                            xf = nlev % 2
                            for (i, h, c, P, hs) in HD:
                                TS('dve', Un[i][0][0:L, 0, :], Xb[i][xf][0][0:L, 0, :], -1.0, M, [Xb[i][xf][1]], [Un[i][1]])
                            for (i, h, c, P, hs) in HD:
                                yo = py[P, c * 128:c * 128 + L]
                                MM(yo, Sr_b[P, c, :], rt[P, c, o:o + L], [SrBk, rtk], ['py'], start=True, stop=False)
                                MM(yo, Vtm[0:L, 0, hs], ArkT[i][0][0:L, 0, 0:L], [Vtk, ArkT[i][1]], ['py'], start=False, stop=False)
                                MM(yo, Un[i][0][0:L, 0, :], ArbT[i][0][0:L, 0, 0:L], [Un[i][1], ArbT[i][1]], ['py'], start=False, stop=True)
                                so = pst[P, c * 64:(c + 1) * 64]
                                MM(so, Ktm[0:L, 0, hs], Vtm[0:L, 0, hs], [Ktk, Vtk], ['pst'], start=True, stop=False)
                                MM(so, Btm[0:L, 0, hs], Un[i][0][0:L, 0, :], [Btk, Un[i][1]], ['pst'], start=False, stop=True)
                        CP('act', yfr[:, :, off:off + L], py[:, :].rearrange("p (c j) -> p c j", c=4)[:, :, 0:L], ['py'], [yfrk])
                        for c in range(4):
                            TT('dve', tmpS[:, c, :], Sr_f[:, c, :], pst[:, c * 64:(c + 1) * 64], A, [SrFk, 'pst'], [tmpk])
                            TS('dve', Sr_f[:, c, :], tmpS[:, c, :], Ep[:, c, o + L - 1:o + L], M, [tmpk, Epk], [SrFk])
                        CP('act', Sr_b[:], Sr_f[:], [SrFk], [SrBk])
                        if smp is not None or (last and off == 256):
                            for c in range(4):
                                b = nextpp()
                                TR(pp[b][0:64, 0:128], Sr_f[:, c, :], identf, [SrFk] + CK, [('pp', b)])
                                CP('act', stR[0:64, 2 * c:2 * c + 2, :].rearrange("p a k -> p (a k)"), pp[b][0:64, 0:128], [('pp', b)], [stRk])
                            dst = rws[l, smp] if smp is not None else rwp[l]
                            S.dma('sp', dst.rearrange("h v k -> v h k"), stR[0:64, :, :], reads=[stRk], is_output=True)
                    for c in C4:
                        t_, k_ = rbn[c]
                        MM(t_[:, 0:n], bonesf, yfr[:, c, 0:n], [yfrk] + CK, [k_])
                        STT(t1[:, c, 0:n], t_[:, 0:n], -1.0 / 64, yfr[:, c, 0:n], M, A, [k_, yfrk], [(t1k, c)])
                    for c in C4:
                        ACT(t2[:, c, 0:n], t1[:, c, 0:n], AF.Square, [(t1k, c)], [(t2k, c)])
                    for c in C4:
                        t_, k_ = rbn[c]
                        MM(t_[:, 0:n], bonesf, t2[:, c, 0:n], [(t2k, c)] + CK, [k_])
                        ACT(t2[:, c, 0:n], t_[:, 0:n], AF.Sqrt, [k_], [(t2k, c)], bias=64e-5, scale=1.0 / 64)
                    for c in C4:
                        RECIP(t2[:, c, 0:n], t2[:, c, 0:n], [(t2k, c)], [(t2k, c)])
                    for c in C4:
                        TT('pool', t1[:, c, 0:n], t1[:, c, 0:n], t2[:, c, 0:n], M, [(t1k, c), (t2k, c)], [(t1k, c)])
                    for c in C4:
                        TS('dve', t1[:, c, 0:n], t1[:, c, 0:n], pcol('rwkv_ln_w', c), M, [(t1k, c), 'pvt'], [(t1k, c)], s2=pcol('rwkv_ln_b', c), op1=A)
                    for c in C4:
                        TT('pool', t1[:, c, 0:n], t1[:, c, 0:n], bon[:, c, 0:n], A, [(t1k, c), bonk], [(t1k, c)])
                    for c in C4:
                        TT('dve', og[:, 1, c, 0:n], t1[:, c, 0:n], gq[:, c, 0:n], M, [(t1k, c), gqk], ['og1'])
                if 'S' in PH:
                    prefetch_w('w_in', l, 3344, 8, 512)
                    phase()
                    uf, ufk = FA(4, NTOKMAX)
                    ub, ubk = BA(4, NTOKMAX)
                    wi = load_w('w_in', l, 3344, 8, 512)
                    for j in range(4):
                        b = nextpp()
                        proj(wi, 8, j * 128, 128, hb, HK, n, b)
                        CP('act', uf[:, j, 0:n], pp[b][:, 0:n], [('pp', b)], [ufk])
                        CP('dve', ub[:, j, 0:n], pp[b][:, 0:n], [('pp', b)], [ubk])
                    tab, tbk = FA(64, 128)
                    S.dma('sp', tab, tabd.rearrange("p a s j -> p (a s) j"), reads=['tabd'], writes=[tbk])
                    tabv = tab.rearrange("p (a s) j -> p a s j", a=4)
                    bpb = []
                    for i, src in enumerate([bpr, bpi, cpr, cpi]):
                        t_, k_ = BA(16, 128)
                        S.dma('pool', t_, src[l], writes=[k_])
                        bpb.append((t_, k_))
                    TS('dve', bpb[3][0], bpb[3][0], -1.0, M, [bpb[3][1]], [bpb[3][1]])
                    ys5, ysk = FA(4, NTOKMAX)
                    NR5 = 8
                    R5 = [[FA(2, 128) for _ in range(5)] + [BA(2, 128)] for _ in range(NR5)]
                    pcs5 = [(pc[0], ('pc', 0)), (pc[1], ('pc', 1)), (pp[0], ('pp', 0)), (pp[1], ('pp', 1)), (pn, 'pn')]
                    stiles = [(0, 128, None), (128, 128, None)]
                    if last:
                        stiles += [(256, 16, None), (272, 16, 0)]
                        xs0, xs0k = FA(16 * 2, 16)
                        xs0v = xs0.rearrange("p (s a) n -> p s a n", a=2)
                        xso, xsok = FA(16 * 2, 16)
                        xsov = xso.rearrange("p (s a) n -> p s a n", a=2)
                        stg, stgk = FA(1, 2048)
                        for a_, src in enumerate([s5r, s5i]):
                            S.dma('sp', stg[0:16, 0, :], src[l], writes=[stgk])
                            for sc in range(16):
                                b = nextpp()
                                TR(pp[b][:, 0:16], stg[0:16, 0, sc * 128:(sc + 1) * 128], identf[0:16, 0:16], [stgk] + CK, [('pp', b)])
                                CP('act', xs0v[:, sc, a_, :], pp[b][:, 0:16], [('pp', b)], [xs0k])
                    def s5_chain(kidx, off, L, smp, uc):
                        SCS = []
                        for q in range(4):
                            sc = uc * 4 + q
                            SCS.append((q, sc, R5[(kidx % 2) * 4 + q], pcs5[sc % 5]))
                        for (q, sc, R_, (pct, pck)) in SCS:
                            for pl, wsel in enumerate([0, 1, 1, 0]):
                                MM(pct[:, pl * 128:pl * 128 + L], bpb[wsel][0][:, sc, :], ub[:, uc, off:off + L], [bpb[wsel][1], ubk], [pck], chain=True)
                        BU = {}
                        for (q, sc, R_, (pct, pck)) in SCS:
                            BU[sc] = (pct[:, 0:256].rearrange("p (a j) -> p a j", a=2)[:, :, 0:L],
                                      pct[:, 256:512].rearrange("p (a j) -> p a j", a=2)[:, :, 0:L])
                        yield
                        if smp is None:
                            for (q, sc, R_, (pct, pck)) in SCS:
                                (m1, m1k), (m2, m2k), (zz, zzk), (ZZ, ZZk), (xx, xxk), (xbf, xbk) = R_
                                Bu, Bus = BU[sc]
                                TT('dve', m1[:, :, 0:L], Bu, tabv[:, 2, sc, 0:L].unsqueeze(1).broadcast_to([128, 2, L]), M, [pck, tbk], [m1k])
                                TT('dve', m2[:, :, 0:L], Bus, tabv[:, 3, sc, 0:L].unsqueeze(1).broadcast_to([128, 2, L]), M, [pck, tbk], [m2k])
                            yield
                            for (q, sc, R_, (pct, pck)) in SCS:
                                (m1, m1k), (m2, m2k), (zz, zzk), (ZZ, ZZk), (xx, xxk), (xbf, xbk) = R_
                                TT('pool', zz[:, 0, 0:L], m1[:, 0, 0:L], m2[:, 0, 0:L], SUB, [m1k, m2k], [zzk])
                                TT('pool', zz[:, 1, 0:L], m1[:, 1, 0:L], m2[:, 1, 0:L], A, [m1k, m2k], [zzk])
                            yield
                            for (q, sc, R_, (pct, pck)) in SCS:
                                (m1, m1k), (m2, m2k), (zz, zzk), (ZZ, ZZk), (xx, xxk), (xbf, xbk) = R_
                                for a_ in range(2):
                                    SCAN(ZZ[:, a_, 0:L], onesf[:, 0:L], zz[:, a_, 0:L], x5[:, sc, a_:a_ + 1], [zzk, ('x5', sc)] + CK, [ZZk])
                            yield
                            for (q, sc, R_, (pct, pck)) in SCS:
                                (m1, m1k), (m2, m2k), (zz, zzk), (ZZ, ZZk), (xx, xxk), (xbf, xbk) = R_
                                TT('dve', m1[:, :, 0:L], ZZ[:, :, 0:L], tabv[:, 0, sc, 0:L].unsqueeze(1).broadcast_to([128, 2, L]), M, [ZZk, tbk], [m1k])
                                TT('pool', m2[:, 0, 0:L], ZZ[:, 1, 0:L], tabv[:, 1, sc, 0:L], M, [ZZk, tbk], [m2k])
                                TT('pool', m2[:, 1, 0:L], ZZ[:, 0, 0:L], tabv[:, 1, sc, 0:L], M, [ZZk, tbk], [m2k])
                            yield
                            for (q, sc, R_, (pct, pck)) in SCS:
                                (m1, m1k), (m2, m2k), (zz, zzk), (ZZ, ZZk), (xx, xxk), (xbf, xbk) = R_
                                TT('dve', xx[:, 0, 0:L], m1[:, 0, 0:L], m2[:, 0, 0:L], SUB, [m1k, m2k], [xxk])
                                TT('dve', xx[:, 1, 0:L], m1[:, 1, 0:L], m2[:, 1, 0:L], A, [m1k, m2k], [xxk])
                            yield
                            for (q, sc, R_, (pct, pck)) in SCS:
                                (m1, m1k), (m2, m2k), (zz, zzk), (ZZ, ZZk), (xx, xxk), (xbf, xbk) = R_
                                CP('pool', x5[:, sc, :], xx[:, :, L - 1], [xxk], [('x5', sc)])
                                CP('act', xbf[:, :, 0:L], xx[:, :, 0:L], [xxk], [xbk])
                            yield
                        else:
                            for (q, sc, R_, (pct, pck)) in SCS:
                                (m1, m1k), (m2, m2k), (zz, zzk), (ZZ, ZZk), (xx, xxk), (xbf, xbk) = R_
                                Bu, Bus = BU[sc]
                                PCK = [pck]
                                cr = s5c[:, 2, sc:sc + 1]
                                cim_ = s5c[:, 3, sc:sc + 1]
                                ar_ = s5c[:, 0, sc:sc + 1]
                                ai_ = s5c[:, 1, sc:sc + 1]
                                TS('dve', m1[:, :, 0:L], Bu, cr, M, PCK + ['s5c'], [m1k])
                                TS('dve', m2[:, :, 0:L], Bus, cim_, M, PCK + ['s5c'], [m2k])
                                TT('pool', zz[:, 0, 0:L], m1[:, 0, 0:L], m2[:, 0, 0:L], SUB, [m1k, m2k], [zzk])
                                TT('pool', zz[:, 1, 0:L], m1[:, 1, 0:L], m2[:, 1, 0:L], A, [m1k, m2k], [zzk])
                                STT(ZZ[:, :, 0:L], xs0v[:, sc, :, :], ar_, zz[:, :, 0:L], M, A, [xs0k, zzk, 's5c'], [ZZk])
                                TS('dve', m2[:, 0, 0:L], xs0v[:, sc, 1, :], ai_, M, [xs0k, 's5c'], [m2k])
                                TS('dve', m2[:, 1, 0:L], xs0v[:, sc, 0, :], ai_, M, [xs0k, 's5c'], [m2k])
                                TT('dve', xx[:, 0, 0:L], ZZ[:, 0, 0:L], m2[:, 0, 0:L], SUB, [ZZk, m2k], [xxk])
                                TT('dve', xx[:, 1, 0:L], ZZ[:, 1, 0:L], m2[:, 1, 0:L], A, [ZZk, m2k], [xxk])
                                CP('pool', xsov[:, sc, :, :], xx[:, :, 0:L], [xxk], [xsok])
                                CP('act', xbf[:, :, 0:L], xx[:, :, 0:L], [xxk], [xbk])
                            yield
                        for (q, sc, R_, (pct, pck)) in SCS:
                            (m1, m1k), (m2, m2k), (zz, zzk), (ZZ, ZZk), (xx, xxk), (xbf, xbk) = R_
                            MM(py[:, 0:L], bpb[2][0][:, sc, :], xbf[:, 0, 0:L], [bpb[2][1], xbk], ['py'], start=(q == 0), stop=False, chain=True)
                            MM(py[:, 0:L], bpb[3][0][:, sc, :], xbf[:, 1, 0:L], [bpb[3][1], xbk], ['py'], start=False, stop=(q == 3), chain=True)
                        CP('act', ys5[:, uc, off:off + L], py[:, 0:L], ['py'], [ysk])

                    def s5_store_prompt():
                        for a_, dst in enumerate([s5rp, s5ip]):
                            b = nextpp()
                            TR(pp[b][0:16, 0:128], x5[:, :, a_], identf, [('x5', s_) for s_ in range(16)] + CK, [('pp', b)])
                            CP('act', stg[0:16, 0, 0:128], pp[b][0:16, 0:128], [('pp', b)], [stgk])
                            S.dma('sp', dst[l], stg[0:16, 0, 0:128], reads=[stgk], is_output=True)
                        yield

                    pend = []
                    kidx = 0
                    for (off, L, smp) in stiles:
                        for uc in range(4):
                            pend.append(s5_chain(kidx, off, L, smp, uc))
                            kidx += 1
                        if last and off == 256:
                            pend.append('fence')
                            pend.append(s5_store_prompt())
                            pend.append('fence')
                    SKEW = 4
                    active = []
                    while pend or active:
                        while pend:
                            if pend[0] == 'fence':
                                if active:
                                    break
                                pend.pop(0)
                                continue
                            if not active or (len(active) < 2 and active[-1][1] >= SKEW):
                                active.append([pend.pop(0), 0])
                            else:
                                break
                        for ent in list(active):
                            try:
                                next(ent[0])
                                ent[1] += 1
                            except StopIteration:
                                active.remove(ent)
                    if last:
                        for a_, dst in enumerate([s5rs, s5is]):
                            for sc in range(16):
                                b = nextpp()
                                TR(pp[b][0:16, 0:128], xsov[:, sc, a_, :], identf, [xsok] + CK, [('pp', b)])
                                CP('act', stg[0:16, 0, sc * 128:(sc + 1) * 128], pp[b][0:16, 0:128], [('pp', b)], [stgk])
                            S.dma('sp', dst[l], stg[0:16, 0, :], reads=[stgk], is_output=True)
                    yv, yvk = FA(4, NTOKMAX)
                    t5, t5k = FA(1, NTOKMAX)
                    ygb, ygbk = BA(4, NTOKMAX)
                    for uc in range(4):
                        STT(yv[:, uc, 0:n], uf[:, uc, 0:n], pvt[:, PVO['s5_d'] + uc:PVO['s5_d'] + uc + 1], ys5[:, uc, 0:n], M, A, [ufk, ysk, 'pvt'], [yvk])
                        ACT(t5[:, 0, 0:n], yv[:, uc, 0:n], AF.Square, [yvk], [t5k])
                        TS('dve', t5[:, 0, 0:n], t5[:, 0, 0:n], 0.044715, M, [t5k], [t5k], s2=1.0, op1=A)
                        TT('dve', t5[:, 0, 0:n], t5[:, 0, 0:n], yv[:, uc, 0:n], M, [t5k, yvk], [t5k])
                        ACT(t5[:, 0, 0:n], t5[:, 0, 0:n], AF.Sigmoid, [t5k], [t5k], scale=1.5957691216057308)
                        TT('dve', yv[:, uc, 0:n], yv[:, uc, 0:n], t5[:, 0, 0:n], M, [t5k, yvk], [yvk])
                        CP('act', ygb[:, uc, 0:n], yv[:, uc, 0:n], [yvk], [ygbk])
                    for oc in range(4):
                        b = nextpp()
                        for kc in range(4):
                            MM(pp[b][:, 0:n], glub[:, kc, oc * 128:(oc + 1) * 128], ygb[:, kc, 0:n], ['glub', ygbk], [('pp', b)], start=(kc == 0), stop=(kc == 3), chain=True)
                        ACT(t5[:, 0, 0:n], pp[b][:, 0:n], AF.Sigmoid, [('pp', b), 'pvt'], [t5k], bias=pvt[:, PVO['s5_glu_b'] + oc:PVO['s5_glu_b'] + oc + 1])
                        TT('dve', og[:, 2, oc, 0:n], yv[:, oc, 0:n], t5[:, 0, 0:n], M, [yvk, t5k], ['og2'])
                if 'C' in PH:
                    prefetch_w('w_in', l, 3856, 8, 512)
                    prefetch_w('w_br_gla', l, 0, 4, 512)
                    phase()
                    mf, mfk = FA(8, NTOKMAX)
                    sg, sgk = FA(2, NTOKMAX)
                    brn = ['w_br_gla', 'w_br_rwkv', 'w_br_s5']
                    for br in range(3):
                        for half in range(2):
                            c0 = 3856 + br * 1024 + half * 512
                            wi = load_w('w_in', l, c0, 8, 512)
                            wj = load_w(brn[br], l, half * 512, 4, 512)
                            for j in range(4):
                                oc = half * 4 + j
                                b = nextpp()
                                proj(wi, 8, j * 128, 128, hb, HK, n, b)
                                ACT(sg[:, j % 2, 0:n], pp[b][:, 0:n], AF.Sigmoid, [('pp', b)], [(sgk, j % 2)])
                                b2 = nextpp()
                                proj(wj, 4, j * 128, 128, og[:, br], ['og%d' % br], n, b2)
                                if br == 0:
                                    TT('dve', mf[:, oc, 0:n], sg[:, j % 2, 0:n], pp[b2][:, 0:n], M, [(sgk, j % 2), ('pp', b2)], [(mfk, oc)])
                                else:
                                    TT('dve', sg[:, j % 2, 0:n], sg[:, j % 2, 0:n], pp[b2][:, 0:n], M, [(sgk, j % 2), ('pp', b2)], [(sgk, j % 2)])
                                    TT('pool', mf[:, oc, 0:n], mf[:, oc, 0:n], sg[:, j % 2, 0:n], A, [(sgk, j % 2), (mfk, oc)], [(mfk, oc)])
                    mb, mbk = BA(8, NTOKMAX)
                    for oc in range(8):
                        CP('act' if oc % 2 else 'dve', mb[:, oc, 0:n], mf[:, oc, 0:n], [(mfk, oc)], [mbk])
                    for half in range(2):
                        wi = load_w('w_out', l, half * 512, 8, 512)
                        for j in range(4):
                            oc = half * 4 + j
                            b = nextpp()
                            proj(wi, 8, j * 128, 128, mb, [mbk], n, b)
                            TT('dve', xb[:, oc, 0:n], xb[:, oc, 0:n], pp[b][:, 0:n], A, ['xb', ('pp', b)], ['xb'])
                    prefetch_w('ffn_w1', l, 0, 8, 512)
                    prefetch_w('ffn_w3', l, 0, 8, 512)
                    phase()
                    rs, rk = rmsnorm(PVO['norm_ffn'], n)
                    for c in range(8):
                        STT(hb[:, c, 0:n], xb[:, c, 0:n], pvt[:, PVO['norm_ffn'] + c:PVO['norm_ffn'] + c + 1], rs[:, 0, 0:n], M, M,
                            ['xb', 'pvt', rk], ['hb'])
                    hid, hidk = BA(22, NTOKMAX)
                    sl, slk = FA(2, NTOKMAX)
                    for g in range(6):
                        c0 = g * 512
                        ncol = min(512, DFF - c0)
                        wi = load_w('ffn_w1', l, c0, 8, ncol)
                        wj = load_w('ffn_w3', l, c0, 8, ncol)
                        for j in range(ncol // 128):
                            hc = g * 4 + j
                            b = nextpp()
                            proj(wi, 8, j * 128, 128, hb, HK, n, b)
                            ACT(sl[:, j % 2, 0:n], pp[b][:, 0:n], AF.Silu, [('pp', b)], [(slk, j % 2)])
                            b2 = nextpp()
                            proj(wj, 8, j * 128, 128, hb, HK, n, b2)
                            TT('dve', hid[:, hc, 0:n], sl[:, j % 2, 0:n], pp[b2][:, 0:n], M, [(slk, j % 2), ('pp', b2)], [hidk])
                    for ocp in range(4):
                        halves = []
                        for kh in range(2):
                            i = wt['i']
                            wt['i'] = (i + 1) % NWB
                            w2v = wbuf[i][:, :, :].rearrange("p a b -> p (a b)")[:, 0:11 * 256].rearrange("p (k m) -> p k m", k=11)
                            S.dma('sp', w2v, wb['ffn_w2'][l, kh * 1408:(kh + 1) * 1408, ocp * 256:(ocp + 1) * 256].rearrange("(kc p) n -> p kc n", p=128),
                                  reads=WK('ffn_w2', l), writes=[('wbuf', i)])
                            halves.append((w2v, i))
                        for j in range(2):
                            oc = ocp * 2 + j
                            b = nextpp()
                            for kc in range(22):
                                w2v, i = halves[kc // 11]
                                MM(pp[b][:, 0:n], w2v[:, kc % 11, j * 128:(j + 1) * 128], hid[:, kc, 0:n], [('wbuf', i), hidk], [('pp', b)],
                                   start=(kc == 0), stop=(kc == 21), chain=True)
                            TT('dve', xb[:, oc, 0:n], xb[:, oc, 0:n], pp[b][:, 0:n], A, ['xb', ('pp', b)], ['xb'])
                    S.dma('sp', xres[:, :, t0:t0 + n], xb[:, :, 0:n], reads=['xb'], writes=['xres'])
                    if l == depth - 1:
                        rs, rk = rmsnorm(PVO['final_norm'], n)
                        yf, yfk = FA(8, NTOKMAX)
                        for c in range(8):
                            STT(yf[:, c, 0:n], xb[:, c, 0:n], pvt[:, PVO['final_norm'] + c:PVO['final_norm'] + c + 1], rs[:, 0, 0:n], M, M,
                                ['xb', 'pvt', rk], [yfk])
                        yt, ytk = FA(2, 1024)
                        otl = [(0, 128), (128, 128)] + ([(256, 32)] if last else [])
                        for ti, (off, L) in enumerate(otl):
                            for c in range(8):
                                b = nextpp()
                                TR(pp[b][0:L, 0:128], yf[:, c, off:off + L], identf, [yfk] + CK, [('pp', b)])
                                CP('act' if c % 2 else 'dve', yt[0:L, ti % 2, c * 128:(c + 1) * 128], pp[b][0:L, 0:128], [('pp', b)], [(ytk, ti % 2)])
                            g0 = t0 + off
                            lo, hi = max(g0, 16), min(g0 + L, NPR)
                            if hi > lo:
                                S.dma('sp', yp[lo - 16:hi - 16, :], yt[lo - g0:hi - g0, ti % 2, :], reads=[(ytk, ti % 2)], is_output=True)
                            lo2 = max(g0, NPR)
                            if g0 + L > lo2:
                                S.dma('sp', ys[lo2 - NPR:g0 + L - NPR, :], yt[lo2 - g0:L, ti % 2, :], reads=[(ytk, ti % 2)], is_output=True)
        S.finish(block)
        print("n_instr", S.n_instr, {e: len(v) for e, v in S.q.items()})
    return nc


def _fm(v, nch):
    return np.ascontiguousarray(np.asarray(v, np.float32).reshape(nch, 128).T)


def _consts():
    p = np.arange(128)[:, None]
    f = np.arange(128)[None, :]
    mu = (p < f).astype(np.float32)
    mui = (p <= f).astype(np.float32)
    mln = -(p > f).astype(np.float32)
    mun = -(p < f).astype(np.float32)
    ident = np.eye(128, dtype=np.float32)
    ones = np.ones((128, 128), np.float32)
    bones = np.zeros((128, 128), np.float32)
    bones[:64, :64] = 1
    bones[64:, 64:] = 1
    return np.ascontiguousarray(np.concatenate([mu, mui, mln, mun, ident, ones, bones], axis=1))


def _prep_shared(inp):
    L = 4
    pvs = np.zeros((L, 128, NV), np.float32)
    for l in range(L):
        def put(name, arr, nch):
            pvs[l, :, PVO[name]:PVO[name] + nch] = _fm(arr, nch)
        put('norm_mix', inp['norm_mix'][l], 8)
        put('norm_ffn', inp['norm_ffn'][l], 8)
        put('gla_gate_b', inp['gla_gate_b'][l], 2)
        put('gla_norm', inp['gla_norm'][l], 4)
        put('rwkv_mu', inp['rwkv_mu'][l], 14)
        put('rwkv_w0', inp['rwkv_w0'][l], 4)
        put('rwkv_a0', inp['rwkv_a0'][l], 4)
        put('rwkv_k_k', inp['rwkv_k_k'][l], 4)
        put('rwkv_k_a', inp['rwkv_k_a'][l], 4)
        put('rwkv_r_k', np.asarray(inp['rwkv_r_k'][l]).reshape(-1), 4)
        put('rwkv_ln_w', inp['rwkv_ln_w'][l], 4)
        put('rwkv_ln_b', inp['rwkv_ln_b'][l], 4)
        put('s5_d', inp['s5_d'][l], 4)
        put('s5_glu_b', inp['s5_glu_b'][l], 4)
        put('s5_are', np.asarray(inp['s5_a_re'][l]).reshape(-1), 16)
        put('s5_aim', np.asarray(inp['s5_a_im'][l]).reshape(-1), 16)
        put('s5_ldt', np.repeat(np.asarray(inp['s5_log_dt'][l]), 64), 16)
        put('final_norm', inp['final_norm'], 8)
    def bpad(B):
        B = np.asarray(B, np.float32)
        out = np.zeros((L, 128, 16, 128), np.float32)
        for sc in range(16):
            uc, q = sc // 4, sc % 4
            for g2 in range(2):
                g = 2 * sc + g2
                gl = 2 * q + g2
                out[:, gl * 16:(gl + 1) * 16, sc, g2 * 64:(g2 + 1) * 64] = np.transpose(B[:, g], (0, 2, 1))
        return out
    def cpad(C):
        C = np.asarray(C, np.float32)
        out = np.zeros((L, 128, 16, 128), np.float32)
        for sc in range(16):
            q = sc % 4
            for g2 in range(2):
                g = 2 * sc + g2
                gl = 2 * q + g2
                out[:, g2 * 64:(g2 + 1) * 64, sc, gl * 16:(gl + 1) * 16] = np.transpose(C[:, g], (0, 2, 1))
        return out
    sh = {
        'pv': pvs, 'consts': _consts(),
        'bpad_re': bpad(inp['s5_b_re']), 'bpad_im': bpad(inp['s5_b_im']),
        'cpad_re': cpad(inp['s5_c_re']), 'cpad_im': cpad(inp['s5_c_im']),
    }
    for k in ['gla_gate_w2', 'rwkv_w2', 'rwkv_a2', 'rwkv_g2', 's5_glu_w'] + [n for n, _, _ in BIGW]:
        sh[k] = np.ascontiguousarray(np.asarray(inp[k], np.float32))
    return sh


def kernel(_depth=4, _nblk=None, _cores=8, **inp):
    sh = _prep_shared(inp)
    xp = np.asarray(inp['x_prompt'], np.float32)
    xs = np.asarray(inp['x_sample'], np.float32)
    meta = np.asarray(inp['meta_tokens'], np.float32)
    in_maps = []
    for c in range(_cores):
        sl = slice(c * NS, (c + 1) * NS)
        m = dict(sh)
        m['xin'] = np.ascontiguousarray(np.concatenate([meta, xp[c], xs[sl, 0]], axis=0))
        m['sgla'] = np.ascontiguousarray(np.asarray(inp['state_gla'], np.float32)[:, sl])
        m['srw'] = np.ascontiguousarray(np.asarray(inp['state_rwkv'], np.float32)[:, sl])
        m['ssh'] = np.ascontiguousarray(np.asarray(inp['state_rwkv_shift'], np.float32)[:, sl])
        m['s5r'] = np.ascontiguousarray(np.asarray(inp['state_s5_re'], np.float32)[:, sl].reshape(4, NS, 2048))
        m['s5i'] = np.ascontiguousarray(np.asarray(inp['state_s5_im'], np.float32)[:, sl].reshape(4, NS, 2048))
        in_maps.append(m)
    nc = build_nc(_depth, _nblk)
    res = run_bass_kernel_spmd(nc, in_maps, core_ids=list(range(_cores)))
    R = res.results
    cat = lambda k, ax: np.concatenate([np.asarray(r[k], np.float32) for r in R], axis=ax)
    stk = lambda k: np.stack([np.asarray(r[k], np.float32) for r in R], axis=1)
    y_prompt = np.stack([np.asarray(r['yp'], np.float32) for r in R], axis=0)
    y_sample = cat('ys', 0).reshape(-1, 1, D)
    gla_p = stk('glap')
    rwkv_p = stk('rwp')
    shift_p = stk('shp').reshape(4, -1, 1792)
    s5re_p = stk('s5rp').reshape(4, -1, 32, 64)
    s5im_p = stk('s5ip').reshape(4, -1, 32, 64)
    gla_s = cat('glas', 1)
    rwkv_s = cat('rws', 1)
    shift_s = cat('shs', 1)
    s5re_s = cat('s5rs', 1).reshape(4, -1, 32, 64)
    s5im_s = cat('s5is', 1).reshape(4, -1, 32, 64)
    return (y_prompt, y_sample, gla_p, rwkv_p, shift_p, s5re_p, s5im_p, gla_s, rwkv_s, shift_s, s5re_s, s5im_s)
```

```python
import contextlib
import numpy as np
import concourse.bass as bass
import concourse.mybir as mybir
from concourse.bass_utils import run_bass_kernel_spmd
PH = set('RSC')
MARK = True
FP32R = False

F32 = mybir.dt.float32
BF16 = mybir.dt.bfloat16
AF = mybir.ActivationFunctionType
ALU = mybir.AluOpType

EPOCH = 30000
DEPOCH = 1500


class Sched:
    CE = ('pe', 'act', 'dve', 'pool')

    def __init__(self, nc, stack, n_dma_slots=12, n_bg=10, n_pf=4):
        self.nc = nc
        self.stack = stack
        self.eng = {'pe': nc.tensor, 'act': nc.scalar, 'dve': nc.vector,
                    'pool': nc.gpsimd, 'sp': nc.sync}
        self.q = {e: [] for e in self.eng}
        self.cnt = {e: 0 for e in self.CE}
        self.sems = {}
        self.seen = {e: {} for e in self.eng}
        self.last_w = {}
        self.readers = {}
        self.nslots = n_dma_slots
        self.slot_sems = {}
        self.slot_cnt = [0] * (n_dma_slots + n_bg)
        self.bgslots = list(range(n_dma_slots, n_dma_slots + n_bg))
        self.bgnext = 0
        self.slot_cnt += [0] * n_pf
        self.pfslots = list(range(n_dma_slots + n_bg, n_dma_slots + n_bg + n_pf))
        self.pfnext = 0
        self.qslots = {'sp': list(range(0, 6)), 'act': list(range(6, 8)), 'pool': list(range(8, n_dma_slots))}
        self.qnext = {'sp': 0, 'act': 0, 'pool': 0}
        self.out_dmas = []
        self.n_instr = 0

    def _sem(self, e, epoch):
        k = (e, epoch)
        if k not in self.sems:
            self.sems[k] = self.stack.enter_context(self.nc.semaphore(f"p_{e}_{epoch}"))
        return self.sems[k]

    def _slot_sem(self, s, ep):
        k = (s, ep)
        if k not in self.slot_sems:
            self.slot_sems[k] = self.stack.enter_context(self.nc.semaphore(f"dslot{s}_{ep}"))
        return self.slot_sems[k]

    def _wait(self, eng, src, c):
        if self.seen[eng].get(src, 0) >= c:
            return
        self.seen[eng][src] = c
        e = self.eng[eng]
        if isinstance(src, tuple):
            sem = self._slot_sem(src[1], (c - 1) // DEPOCH)
            val = ((c - 1) % DEPOCH + 1) * 16
        else:
            idx = c - 1
            sem = self._sem(src, idx // EPOCH)
            val = (idx % EPOCH) + 1
        self.q[eng].append(lambda e=e, sem=sem, val=val: e.wait_ge(sem, val))
        self.n_instr += 1

    def _deps(self, reads, writes):
        deps = set()
        for k in reads:
            w = self.last_w.get(k)
            if w:
                deps.add(w)
        for k in writes:
            w = self.last_w.get(k)
            if w:
                deps.add(w)
            for r in self.readers.get(k, ()):
                deps.add(r)
        return deps

    def _commit(self, me, reads, writes):
        for k in writes:
            self.last_w[k] = me
            self.readers[k] = []
        for k in reads:
            self.readers.setdefault(k, []).append(me)

    @staticmethod
    def _is_psum(k):
        return k in ('pn', 'ptb', 'py', 'pst') or (isinstance(k, tuple) and k[0] in ('pp', 'pc'))

    def op(self, eng, fn, reads=(), writes=(), skip_same=False):
        pr = [k for k in reads if self._is_psum(k)]
        if pr:
            writes = list(writes) + pr
            reads = [k for k in reads if not self._is_psum(k)]
        for (src, c) in sorted(self._deps(reads, writes), key=str):
            if skip_same and src == eng:
                continue
            self._wait(eng, src, c)
        self.cnt[eng] += 1
        c = self.cnt[eng]
        idx = c - 1
        sem = self._sem(eng, idx // EPOCH)
        self.q[eng].append(lambda fn=fn, sem=sem: fn().then_inc(sem, 1))
        self.n_instr += 1
        self._commit((eng, c), reads, writes)

    def dma(self, qeng, out, in_, reads=(), writes=(), is_output=False, bg=False, **kw):
        if bg == 'p':
            s = self.pfslots[self.pfnext % len(self.pfslots)]
            self.pfnext += 1
        elif bg:
            s = self.bgslots[self.bgnext % len(self.bgslots)]
            self.bgnext += 1
        else:
            sl = self.qslots[qeng]
            s = sl[self.qnext[qeng] % len(sl)]
            self.qnext[qeng] += 1
        if self.slot_cnt[s] > 0:
            self._wait(qeng, ('d', s), self.slot_cnt[s])
        for (src, c) in sorted(self._deps(reads, writes), key=str):
            self._wait(qeng, src, c)
        self.slot_cnt[s] += 1
        c = self.slot_cnt[s]
        e = self.eng[qeng]
        sem = self._slot_sem(s, (c - 1) // DEPOCH)
        self.q[qeng].append(lambda e=e, out=out, in_=in_, sem=sem, kw=kw: e.dma_start(out=out, in_=in_, **kw).then_inc(sem, 16))
        self.n_instr += 1
        me = (('d', s), c)
        self._commit(me, reads, writes)
        if is_output:
            self.out_dmas.append(me)

    def finish(self, block):
        for (src, c) in self.out_dmas:
            self.seen['sp'].pop(src, None) if self.seen['sp'].get(src, 0) < c else None
            self._wait('sp', src, c)
        q = self.q

        if q['sp']:
            @block.sync
            def _(e):
                for f in q['sp']:
                    f()
        if q['pe']:
            @block.tensor
            def _(e):
                for f in q['pe']:
                    f()
        if q['act']:
            @block.scalar
            def _(e):
                for f in q['act']:
                    f()
        if q['dve']:
            @block.vector
            def _(e):
                for f in q['dve']:
                    f()
        if q['pool']:
            @block.gpsimd
            def _(e):
                for f in q['pool']:
                    f()


def _barrier(self):
    for e in self.eng:
        for src in self.CE:
            if self.cnt[src] > 0:
                self._wait(e, src, self.cnt[src])
        for qn, slots in self.qslots.items():
            for s in slots:
                if self.slot_cnt[s] > 0:
                    self._wait(e, ('d', s), self.slot_cnt[s])


Sched.barrier = _barrier


NT = 2080
NPR = 2064
NS = 16
BLK = 256
NTOKMAX = 288
D = 1024
NIN = 6928
DFF = 2816

PVO = {}
_o = 0
for _n, _w in [('norm_mix', 8), ('norm_ffn', 8), ('gla_gate_b', 2), ('gla_norm', 4), ('rwkv_mu', 14),
               ('rwkv_w0', 4), ('rwkv_a0', 4), ('rwkv_k_k', 4), ('rwkv_k_a', 4), ('rwkv_r_k', 4),
               ('rwkv_ln_w', 4), ('rwkv_ln_b', 4), ('s5_d', 4), ('s5_glu_b', 4), ('s5_are', 16),
               ('s5_aim', 16), ('s5_ldt', 16), ('final_norm', 8)]:
    PVO[_n] = _o
    _o += _w
NV = _o
CO = {n: i * 128 for i, n in enumerate(['MU', 'MUI', 'MLn', 'MUn', 'ident', 'ones', 'bones'])}
NCONST = 7 * 128

BIGW = [('w_in', D, NIN), ('w_br_gla', 512, D), ('w_br_rwkv', 512, D), ('w_br_s5', 512, D), ('w_out', D, D),
        ('ffn_w1', D, DFF), ('ffn_w3', D, DFF), ('ffn_w2', DFF, D)]


def build_nc(depth=4, nblk=None):
    nc = bass.Bass("TRN2", target_bir_lowering=False)
    din = {}

    def I(name, shape, dt=F32):
        din[name] = nc.dram_tensor(name, shape, dt, kind="ExternalInput").ap()
        return din[name]

    def O(name, shape):
        return nc.dram_tensor(name, shape, F32, kind="ExternalOutput").ap()

    def SC(name, shape, dt):
        return nc.dram_tensor(name, shape, dt, kind="Internal").ap()

    xin = I("xin", [NT, D])
    sgla = I("sgla", [4, NS, 4, 64, 128])
    srw = I("srw", [4, NS, 8, 64, 64])
    ssh = I("ssh", [4, NS, 1792])
    s5r = I("s5r", [4, NS, 2048])
    s5i = I("s5i", [4, NS, 2048])
    pv = I("pv", [4, 128, NV])
    consts = I("consts", [128, NCONST])
    gw2 = I("gla_gate_w2", [4, 16, 256])
    rw2 = I("rwkv_w2", [4, 64, 512])
    ra2 = I("rwkv_a2", [4, 64, 512])
    rg2 = I("rwkv_g2", [4, 128, 512])
    gluw = I("s5_glu_w", [4, 512, 512])
    bpr = I("bpad_re", [4, 128, 16, 128])
    bpi = I("bpad_im", [4, 128, 16, 128])
    cpr = I("cpad_re", [4, 128, 16, 128])
    cpi = I("cpad_im", [4, 128, 16, 128])
    wf = {n: I(n, [4, r, c]) for n, r, c in BIGW}
    wb = {n: SC(n + "_b", [4, r, c], BF16) for n, r, c in BIGW}
    xres = SC("xres", [128, 8, NT], F32)
    tabd = SC("tabd", [128, 4, 16, 128], F32)

    yp = O("yp", [2048, D])
    ys = O("ys", [NS, D])
    glap = O("glap", [4, 4, 64, 128])
    rwp = O("rwp", [4, 8, 64, 64])
    shp = O("shp", [4, 14, 128])
    s5rp = O("s5rp", [4, 16, 128])
    s5ip = O("s5ip", [4, 16, 128])
    glas = O("glas", [4, NS, 4, 64, 128])
    rws = O("rws", [4, NS, 8, 64, 64])
    shs = O("shs", [4, NS, 1792])
    s5rs = O("s5rs", [4, NS, 2048])
    s5is = O("s5is", [4, NS, 2048])

    st = contextlib.ExitStack()
    with st:
        S = Sched(nc, st)
        T = lambda name, shape, dt=F32: st.enter_context(nc.sbuf_tensor(name, shape, dt))
        PS = lambda name, shape, dt=F32: st.enter_context(nc.psum_tensor(name, shape, dt))

        def ACT(out, in_, func, R, W, bias=None, scale=None):
            kw = {}
            if bias is not None:
                kw['bias'] = bias
            if scale is not None:
                kw['scale'] = scale
            S.op('act', lambda: nc.scalar.activation(out=out, in_=in_, func=func, **kw), R, W)

        def EN(eng):
            return nc.vector if eng == 'dve' else nc.gpsimd

        def TT(eng, out, a, b, op, R, W):
            S.op(eng, lambda: EN(eng).tensor_tensor(out=out, in0=a, in1=b, op=op), R, W)

        def TS(eng, out, a, s1, op0, R, W, s2=None, op1=None):
            if op1 is None:
                S.op(eng, lambda: EN(eng).tensor_scalar(out=out, in0=a, scalar1=s1, scalar2=None, op0=op0), R, W)
            else:
                S.op(eng, lambda: EN(eng).tensor_scalar(out=out, in0=a, scalar1=s1, scalar2=s2, op0=op0, op1=op1), R, W)

        def STT(out, a, s, b, op0, op1, R, W):
            S.op('dve', lambda: nc.vector.scalar_tensor_tensor(out=out, in0=a, scalar=s, in1=b, op0=op0, op1=op1), R, W)

        def CP(eng, out, in_, R, W):
            if eng == 'act':
                S.op('act', lambda: nc.scalar.copy(out=out, in_=in_), R, W)
            else:
                S.op(eng, lambda: EN(eng).tensor_copy(out=out, in_=in_), R, W)

        def MM(out, lhsT, rhs, R, W, start=True, stop=True, chain=False):
            S.op('pe', lambda: nc.tensor.matmul(out, lhsT=lhsT, rhs=rhs, start=start, stop=stop), R, W, skip_same=chain)

        def TR(out, in_, ident, R, W):
            S.op('pe', lambda: nc.tensor.transpose(out=out, in_=in_, identity=ident), R, W)

        def SCAN(out, d0, d1, init, R, W):
            S.op('dve', lambda: nc.vector.tensor_tensor_scan(out=out, data0=d0, data1=d1, initial=init, op0=ALU.mult, op1=ALU.add), R, W)

        def RECIP(out, in_, R, W):
            S.op('dve', lambda: nc.vector.reciprocal(out=out, in_=in_), R, W)

        def MSET(eng, ap, v, W):
            S.op(eng, lambda: EN(eng).memset(ap, v), (), W)

        M, A, SUB = ALU.mult, ALU.add, ALU.subtract

        def RR(ap):
            return ap.bitcast(mybir.dt.float32r) if FP32R else ap

        cst = T("cst", [128, NCONST])
        cstb = T("cstb", [128, 128], BF16)
        pvt = T("pvt", [128, NV])
        pvd = T("pvd", [128, 8])
        gw2b = T("gw2b", [128, 256], BF16)
        w2b = T("w2b", [128, 512], BF16)
        a2b = T("a2b", [128, 512], BF16)
        g2b = T("g2b", [128, 512], BF16)
        glub = T("glub", [128, 4, 512], BF16)
        xb = T("xb", [128, 8, NTOKMAX])
        hb = T("hb", [128, 8, NTOKMAX], BF16)
        og = T("og", [128, 3, 4, NTOKMAX], BF16)
        Sg_fs = [T(f"Sg_f{i}", [128, 2, 128]) for i in range(3)]
        Sg_bs = [T(f"Sg_b{i}", [128, 2, 128], BF16) for i in range(3)]
        Sr_fs = [T(f"Sr_f{i}", [128, 4, 64]) for i in range(3)]
        Sr_bs = [T(f"Sr_b{i}", [128, 4, 64], BF16) for i in range(3)]
        x5 = T("x5", [128, 16, 2])
        shprev = T("shprev", [128, 14, 1])
        NWB = 4
        wbuf = [T(f"wbuf{i}", [128, 8, 512], BF16) for i in range(NWB)]
        RF = T("RF", [128, 26000])
        RB = T("RB", [128, 14200], BF16)
        pp = [PS(f"pp{i}", [128, 512]) for i in range(2)]
        pn = PS("pn", [128, 512])
        ptb = PS("ptb", [128, 1024], BF16)
        pc = [PS(f"pc{i}", [128, 512]) for i in range(2)]
        py = PS("py", [128, 512])
        pst = PS("pst", [128, 512])
        block = st.enter_context(nc.Block())

        MU = cst[:, CO['MU']:CO['MU'] + 128]
        MUI = cst[:, CO['MUI']:CO['MUI'] + 128]
        MLn = cst[:, CO['MLn']:CO['MLn'] + 128]
        MUn = cst[:, CO['MUn']:CO['MUn'] + 128]
        identf = cst[:, CO['ident']:CO['ident'] + 128]
        onesf = cst[:, CO['ones']:CO['ones'] + 128]
        bonesf = cst[:, CO['bones']:CO['bones'] + 128]
        CK = ['cst']

        al = {'f': 0, 'b': 0, 'n': 0}

        mark = T("mark", [128, 2])

        def phase(soft=False):
            if not soft:
                S.barrier()
                al['f'] = 0
                al['b'] = 0
            if MARK:
                S.op('pool', lambda: nc.gpsimd.memset(mark[:, 0:1], 1.0), (), ['mark'])

        def FA(c, t):
            o = al['f']
            al['f'] += c * t
            assert al['f'] <= 25900, al['f']
            al['n'] += 1
            return RF[:, o:o + c * t].rearrange("p (c t) -> p c t", c=c), f"F{al['n']}"

        def BA(c, t):
            o = al['b']
            al['b'] += c * t
            assert al['b'] <= 14200, al['b']
            al['n'] += 1
            return RB[:, o:o + c * t].rearrange("p (c t) -> p c t", c=c), f"B{al['n']}"

        wt = {'i': 0, 'j': 0, 'p': 0, 'c': 0}

        WROWS = {n: r for n, r, c in BIGW}

        def WK(name, l):
            return [('wb', name, l, r0) for r0 in range(0, WROWS[name], 128)]

        pref = {}

        def prefetch_w(name, l, c0, nk, ncols):
            i = wt['i']
            wt['i'] = (i + 1) % NWB
            src = wb[name][l, :, c0:c0 + ncols]
            S.dma('sp', wbuf[i][:, 0:nk, 0:ncols], src.rearrange("(kc p) n -> p kc n", p=128), reads=WK(name, l), writes=[('wbuf', i)], bg='p')
            pref[(name, l, c0, ncols)] = i

        def load_w(name, l, c0, nk, ncols):
            if (name, l, c0, ncols) in pref:
                return pref.pop((name, l, c0, ncols))
            i = wt['i']
            wt['i'] = (i + 1) % NWB
            src = wb[name][l, :, c0:c0 + ncols]
            S.dma('sp', wbuf[i][:, 0:nk, 0:ncols], src.rearrange("(kc p) n -> p kc n", p=128), reads=WK(name, l), writes=[('wbuf', i)])
            return i

        def convert_list(l):
            return [(n, l, r0) for n, r, c in BIGW for r0 in range(0, r, 128)]

        def issue_convert(items):
            for (n, l, r0) in items:
                S.dma('pool', wb[n][l, r0:r0 + 128, :], wf[n][l, r0:r0 + 128, :], writes=[('wb', n, l, r0)], bg=True)

        def nextpp():
            wt['p'] ^= 1
            return wt['p']

        def nextpc():
            wt['c'] ^= 1
            return wt['c']

        def proj(wi, nk, c0, m, rhs, rk, n, psb, prow=None):
            for kc in range(nk):
                MM(pp[psb][0:m, 0:n], wbuf[wi][:, kc, c0:c0 + m], rhs[:, kc, 0:n], [('wbuf', wi)] + rk, [('pp', psb)],
                   start=(kc == 0), stop=(kc == nk - 1), chain=True)

        S.dma('sp', cst[:], consts, writes=CK)
        CP('dve', cstb[:], identf, CK, ['cstb'])
        issue_convert(convert_list(0))
        tiles_all = [(t0, 128) for t0 in range(0, 2048, 128)] + [(2048, 32)]
        for (t0, L) in tiles_all:
            xt, xk = RF[:, 0:1024], 'xt0'
            S.dma('sp', xt[0:L, :], xin[t0:t0 + L, :], writes=[xk])
            for c in range(8):
                b = nextpp()
                TR(pp[b][:, 0:L], xt[0:L, c * 128:(c + 1) * 128], identf[0:L, 0:L], [xk] + CK, [('pp', b)])
                CP('act' if c % 2 else 'dve', xb[:, c, 0:L], pp[b][:, 0:L], [('pp', b)], ['xb'])
            S.dma('sp', xres[:, :, t0:t0 + L], xb[:, :, 0:L], reads=['xb'], writes=['xres'])
        S.barrier()

        def rmsnorm(goff, n):
            sq, sk = FA(2, NTOKMAX)
            rs, rk = FA(1, NTOKMAX)
            for c in range(8):
                ACT(sq[:, c % 2, 0:n], xb[:, c, 0:n], AF.Square, ['xb'], [(sk, c % 2)])
                MM(pn[:, 0:n], onesf, sq[:, c % 2, 0:n], [(sk, c % 2)] + CK, ['pn'], start=(c == 0), stop=(c == 7), chain=True)
            ACT(rs[:, 0, 0:n], pn[:, 0:n], AF.Sqrt, ['pn'], [rk], bias=1e-6, scale=1.0 / 1024)
            RECIP(rs[:, 0, 0:n], rs[:, 0, 0:n], [rk], [rk])
            return rs, rk

        nb_all = 8 if nblk is None else nblk
        for l in range(depth):
            phase()
            S.dma('sp', pvt[:], pv[l], writes=['pvt'])
            S.dma('pool', gw2b[0:16, :], gw2[l], writes=['gw2b'])
            S.dma('pool', w2b[0:64, :], rw2[l], writes=['w2b'])
            S.dma('pool', a2b[64:128, :], ra2[l], writes=['a2b'])
            S.dma('pool', g2b[:], rg2[l], writes=['g2b'])
            S.dma('pool', glub[:], gluw[l].rearrange("(kc p) n -> p kc n", p=128), writes=['glub'])
            PK = ['pvt', 'pvd']
            TS('dve', pvd[:, 0:2], pvt[:, PVO['gla_gate_b']:PVO['gla_gate_b'] + 2], -1.0, M, ['pvt'], ['pvd'])
            TS('dve', pvd[:, 2:6], pvt[:, PVO['rwkv_k_a']:PVO['rwkv_k_a'] + 4], -1.0, M, ['pvt'], ['pvd'], s2=1.0, op1=A)
            c5, c5k = FA(16, 16)
            K5 = [c5k]
            are = pvt[:, PVO['s5_are']:PVO['s5_are'] + 16]
            aim = pvt[:, PVO['s5_aim']:PVO['s5_aim'] + 16]
            ldt = pvt[:, PVO['s5_ldt']:PVO['s5_ldt'] + 16]
            dt_, mag, th, cs, sn, abr, abi, den, nr, cre, cim, t1, t2, iar, iai = [c5[:, i, :] for i in range(15)]
            ACT(dt_, ldt, AF.Exp, ['pvt'], K5)
            TT('dve', mag, are, dt_, M, ['pvt'] + K5, K5)
            ACT(mag, mag, AF.Exp, K5, K5)
            TT('dve', th, aim, dt_, M, ['pvt'] + K5, K5)
            pi = float(np.pi)
            ACT(sn, th, AF.Sin, K5, K5, scale=1.0 / 32)
            ACT(cs, th, AF.Sin, K5, K5, scale=1.0 / 32, bias=pi / 2)
            for _ in range(5):
                TT('dve', t1, sn, cs, M, K5, K5)
                TT('dve', t2, sn, sn, M, K5, K5)
                TS('dve', sn, t1, 2.0, M, K5, K5)
                TS('dve', cs, t2, -2.0, M, K5, K5, s2=1.0, op1=A)
            TT('dve', abr, mag, cs, M, K5, K5)
            TT('dve', abi, mag, sn, M, K5, K5)
            TT('dve', t1, are, are, M, ['pvt'] + K5, K5)
            TT('dve', t2, aim, aim, M, ['pvt'] + K5, K5)
            TT('dve', den, t1, t2, A, K5, K5)
            RECIP(den, den, K5, K5)
            TS('dve', nr, abr, -1.0, A, K5, K5)
            TT('dve', t1, nr, are, M, ['pvt'] + K5, K5)
            TT('dve', t2, abi, aim, M, ['pvt'] + K5, K5)
            TT('dve', t1, t1, t2, A, K5, K5)
            TT('dve', cre, t1, den, M, K5, K5)
            TT('dve', t1, abi, are, M, ['pvt'] + K5, K5)
            TT('dve', t2, nr, aim, M, ['pvt'] + K5, K5)
            TT('dve', t1, t1, t2, SUB, K5, K5)
            TT('dve', cim, t1, den, M, K5, K5)
            TT('dve', t1, abr, abr, M, K5, K5)
            TT('dve', t2, abi, abi, M, K5, K5)
            TT('dve', t1, t1, t2, A, K5, K5)
            RECIP(t1, t1, K5, K5)
            TT('dve', iar, abr, t1, M, K5, K5)
            TT('dve', iai, abi, t1, M, K5, K5)
            TS('dve', iai, iai, -1.0, M, K5, K5)
            s5c = T(f"s5c{l}", [128, 4, 16])
            for i, src in enumerate([abr, abi, cre, cim]):
                CP('dve', s5c[:, i, :], src, K5, ['s5c'])
            tab, tbk = FA(4 * 16, 128)
            tabv = tab.rearrange("p (a s) j -> p a s j", a=4)

            tA, tAk = FA(16, 128)
            tB, tBk = FA(16, 128)
            tC, tCk = FA(16, 128)
            for a_, src in enumerate([abr, abi, iar, iai]):
                CP('dve', tabv[:, a_, :, 0], src, K5, [tbk])

            def cmul_all(ore, oim, ire, iim, sre, sim, m):
                TT('dve', tA[:, :, 0:m], iim, sim, M, [tbk] + K5, [tAk])
                TT('pool', tB[:, :, 0:m], ire, sre, M, [tbk] + K5, [tBk])
                TT('dve', tC[:, :, 0:m], tB[:, :, 0:m], tA[:, :, 0:m], SUB, [tAk, tBk], [tCk])
                TT('dve', tA[:, :, 0:m], ire, sim, M, [tbk] + K5, [tAk])
                TT('pool', tB[:, :, 0:m], iim, sre, M, [tbk] + K5, [tBk])
                TT('dve', oim, tB[:, :, 0:m], tA[:, :, 0:m], A, [tAk, tBk], [tbk])
                CP('pool', ore, tC[:, :, 0:m], [tCk], [tbk])

            m = 1
            while m < 128:
                for a_ in (0, 2):
                    cmul_all(tabv[:, a_, :, m:2 * m], tabv[:, a_ + 1, :, m:2 * m], tabv[:, a_, :, 0:m], tabv[:, a_ + 1, :, 0:m],
                             tabv[:, a_, :, m - 1:m].broadcast_to([128, 16, m]), tabv[:, a_ + 1, :, m - 1:m].broadcast_to([128, 16, m]), m)
                m *= 2
            cmul_all(tabv[:, 2, :, :], tabv[:, 3, :, :], tabv[:, 2, :, :], tabv[:, 3, :, :],
                     cre.unsqueeze(2).broadcast_to([128, 16, 128]), cim.unsqueeze(2).broadcast_to([128, 16, 128]), 128)
            S.dma('sp', tabd.rearrange("p a s j -> p (a s) j"), tab, reads=[tbk], writes=['tabd'])
            MSET('dve', Sg_fs[0][:], 0.0, [('Sg_f', 0)])
            MSET('dve', Sg_bs[0][:], 0.0, [('Sg_b', 0)])
            MSET('dve', Sr_fs[0][:], 0.0, [('Sr_f', 0)])
            MSET('dve', Sr_bs[0][:], 0.0, [('Sr_b', 0)])
            MSET('dve', x5[:], 0.0, [('x5', s_) for s_ in range(16)])
            MSET('dve', shprev[:], 0.0, ['shprev'])
            cvt_next = convert_list(l + 1) if l + 1 < depth else []

            for bi in range(nb_all):
                t0 = bi * BLK
                last = (bi == 7)
                n = 288 if last else 256
                tiles = [(0, 128, None), (128, 128, None)]
                if last:
                    tiles += [(256, 16, None)] + [(272 + s, 1, s) for s in range(NS)]
                npr = 272 if last else 256

                phase()
                ncv = (len(cvt_next) + nb_all - 1) // nb_all
                issue_convert(cvt_next[bi * ncv:(bi + 1) * ncv])
                S.dma('sp', xb[:, :, 0:n], xres[:, :, t0:t0 + n], reads=['xres'], writes=['xb'])
                rs, rk = rmsnorm(PVO['norm_mix'], n)
                for c in range(8):
                    STT(hb[:, c, 0:n], xb[:, c, 0:n], pvt[:, PVO['norm_mix'] + c:PVO['norm_mix'] + c + 1], rs[:, 0, 0:n], M, M,
                        ['xb', 'pvt', rk], ['hb'])
                HK = ['hb']

                prefetch_w('w_in', l, 0, 8, 512)
                prefetch_w('w_in', l, 512, 8, 512)
                phase()
                qk, qkk = FA(4, NTOKMAX)
                vbf, vk = BA(4, NTOKMAX)
                gg, ggk = FA(4, NTOKMAX)
                zg, zgk = BA(1, NTOKMAX)
                la, lak = FA(2, NTOKMAX)
                of, ofk = FA(4, NTOKMAX)
                wi = load_w('w_in', l, 0, 8, 512)
                for j in range(4):
                    b = nextpp()
                    proj(wi, 8, j * 128, 128, hb, HK, n, b)
                    CP('act', qk[:, j, 0:n], pp[b][:, 0:n], [('pp', b)], [qkk])
                wi = load_w('w_in', l, 512, 8, 512)
                for j in range(4):
                    b = nextpp()
                    proj(wi, 8, j * 128, 128, hb, HK, n, b)
                    CP('act', vbf[:, j, 0:n], pp[b][:, 0:n], [('pp', b)], [vk])
                wi = load_w('w_in', l, 1024, 8, 512)
                for j in range(4):
                    b = nextpp()
                    proj(wi, 8, j * 128, 128, hb, HK, n, b)
                    ACT(gg[:, j, 0:n], pp[b][:, 0:n], AF.Silu, [('pp', b)], [ggk])
                wi = load_w('w_in', l, 1536, 8, 16)
                b = nextpp()
                proj(wi, 8, 0, 16, hb, HK, n, b)
                CP('act', zg[0:16, 0, 0:n], pp[b][0:16, 0:n], [('pp', b)], [zgk])
                for c in range(2):
                    b = nextpp()
                    MM(pp[b][:, 0:n], gw2b[0:16, c * 128:(c + 1) * 128], zg[0:16, 0, 0:n], ['gw2b', zgk], [('pp', b)])
                    ACT(la[:, c, 0:n], pp[b][:, 0:n], AF.Exp, [('pp', b)] + PK, [lak], bias=pvd[:, c:c + 1], scale=-1.0)
                    ACT(la[:, c, 0:n], la[:, c, 0:n], AF.Ln, [lak], [lak], bias=1.0)
                    TS('dve', la[:, c, 0:n], la[:, c, 0:n], -1.0 / 16, M, [lak], [lak])
                Bc, Bk = FA(2, 128)
                Ep, Epk = FA(2, 128)
                Em, Emk = FA(2, 128)
                tmpS, tmpk = FA(2, 128)
                qt, qtk = BA(2, 128)
                kt, ktk = BA(2, 128)
                Ktm, Ktk = BA(1, 256)
                Vtm, Vtk = BA(1, 512)
                STb, STk = BA(4, 128)
                for (off, L, smp) in tiles:
                    sp_ = 0 if smp is None else 1 + (smp % 2)
                    Sg_f, Sg_b, SgFk, SgBk = Sg_fs[sp_], Sg_bs[sp_], ('Sg_f', sp_), ('Sg_b', sp_)
                    if smp is not None:
                        S.dma('sp', Sg_f[:], sgla[l, smp].rearrange("(c hp) d e -> (hp d) c e", hp=2), writes=[SgFk])
                        CP('dve', Sg_b[:], Sg_f[:], [SgFk], [SgBk])
                    o = 0 if smp is None else smp
                    if smp is None or smp == 0:
                        W_ = L if smp is None else NS
                        if smp is None:
                            for c in range(2):
                                SCAN(Bc[:, c, 0:L], onesf[:, 0:L], la[:, c, off:off + L], 0.0, [lak] + CK, [Bk])
                        else:
                            CP('dve', Bc[:, :, 0:W_], la[:, :, off:off + W_], [lak], [Bk])
                        ACT(Ep[:, :, 0:W_], Bc[:, :, 0:W_], AF.Exp, [Bk], [Epk])
                        ACT(Em[:, :, 0:W_], Bc[:, :, 0:W_], AF.Exp, [Bk], [Emk], scale=-1.0)
                        STT(qt[:, :, 0:W_], qk[:, 0:2, off:off + W_], 0.125, Ep[:, :, 0:W_], M, M, [qkk, Epk], [qtk])
                        TT('dve', kt[:, :, 0:W_], qk[:, 2:4, off:off + W_], Em[:, :, 0:W_], M, [qkk, Emk], [ktk])
                    for c in range(2):
                        TR(ptb[0:L, c * 128:(c + 1) * 128], kt[:, c, o:o + L], cstb[:], [ktk, 'cstb'], ['ptb'])
                    CP('act', Ktm[0:L, 0, :], ptb[0:L, 0:256], ['ptb'], [Ktk])
                    for hh in range(4):
                        TR(ptb[0:L, 256 + hh * 128:256 + (hh + 1) * 128], vbf[:, hh, off:off + L], cstb[:], [vk, 'cstb'], ['ptb'])
                    CP('dve', Vtm[0:L, 0, :], ptb[0:L, 256:768], ['ptb'], [Vtk])
                    gbanks = [(pc[0], ('pc', 0)), (pc[1], ('pc', 1)), (pp[0], ('pp', 0)), (pp[1], ('pp', 1))]
                    obanks = [(py, 'py'), (pn, 'pn')]
                    for h in range(4):
                        c, hp = h // 2, h % 2
                        P = slice(64 * hp, 64 * hp + 64)
                        t_, k_ = gbanks[h]
                        MM(t_[0:L, 0:L], kt[P, c, o:o + L], qt[P, c, o:o + L], [ktk, qtk], [k_])
                        TT('dve', STb[0:L, h, 0:L], t_[0:L, 0:L], MUI[0:L, 0:L], M, [k_] + CK, [(STk, h)])
                    for h in range(4):
                        c, hp = h // 2, h % 2
                        P = slice(64 * hp, 64 * hp + 64)
                        ob, obk = obanks[h % 2]
                        MM(ob[:, 0:L], Sg_b[P, c, :], qt[P, c, o:o + L], [SgBk, qtk], [obk], start=True, stop=False)
                        MM(ob[:, 0:L], Vtm[0:L, 0, h * 128:(h + 1) * 128], STb[0:L, h, 0:L], [Vtk, (STk, h)], [obk], start=False, stop=True)
                        CP('act' if h % 2 else 'dve', of[:, h, off:off + L], ob[:, 0:L], [obk], [ofk])
                    for h in range(4):
                        c, hp = h // 2, h % 2
                        P = slice(64 * hp, 64 * hp + 64)
                        MM(pst[P, c * 128:(c + 1) * 128], Ktm[0:L, 0, c * 128 + 64 * hp:c * 128 + 64 * hp + 64], Vtm[0:L, 0, h * 128:(h + 1) * 128],
                           [Ktk, Vtk], ['pst'])
                    for c in range(2):
                        TT('dve', tmpS[:, c, :], Sg_f[:, c, :], pst[:, c * 128:(c + 1) * 128], A, [SgFk, 'pst'], [tmpk])
                        TS('dve', Sg_f[:, c, :], tmpS[:, c, :], Ep[:, c, o + L - 1:o + L], M, [tmpk, Epk], [SgFk])
                    CP('act', Sg_b[:], Sg_f[:], [SgFk], [SgBk])
                    if smp is not None:
                        S.dma('sp', glas[l, smp].rearrange("(c hp) d e -> (hp d) c e", hp=2), Sg_f[:], reads=[SgFk], is_output=True)
                    elif last and off == 256:
                        S.dma('sp', glap[l].rearrange("(c hp) d e -> (hp d) c e", hp=2), Sg_f[:], reads=[SgFk], is_output=True)
                sq, sk = FA(1, NTOKMAX)
                r2, r2k = FA(1, NTOKMAX)
                for h in range(4):
                    ACT(sq[:, 0, 0:n], of[:, h, 0:n], AF.Square, [ofk], [sk])
                    MM(pn[:, 0:n], onesf, sq[:, 0, 0:n], [sk] + CK, ['pn'])
                    ACT(r2[:, 0, 0:n], pn[:, 0:n], AF.Sqrt, ['pn'], [r2k], bias=1e-6, scale=1.0 / 128)
                    RECIP(r2[:, 0, 0:n], r2[:, 0, 0:n], [r2k], [r2k])
                    STT(sq[:, 0, 0:n], of[:, h, 0:n], pvt[:, PVO['gla_norm'] + h:PVO['gla_norm'] + h + 1], r2[:, 0, 0:n], M, M, [ofk, r2k, 'pvt'], [sk])
                    TT('dve', og[:, 0, h, 0:n], sq[:, 0, 0:n], gg[:, h, 0:n], M, [sk, ggk], ['og0'])
                if 'R' in PH:
                    prefetch_w('w_in', l, 1552, 8, 512)
                    prefetch_w('w_in', l, 2064, 8, 512)
                    phase()
                    xs, xsk = FA(14, NTOKMAX)
                    pbufs = [FA(1, NTOKMAX + 2) for _ in range(2)]
                    dds = [FA(1, NTOKMAX) for _ in range(2)]
                    stg, stgk = FA(1, 1792)
                    if last:
                        shin, shink = FA(14, 16)
                        pso, psok = FA(14, 16)
                        S.dma('sp', stg[0:16, 0, :], ssh[l], writes=[stgk])
                        for cc in range(14):
                            b = nextpp()
                            TR(pp[b][:, 0:16], stg[0:16, 0, cc * 128:(cc + 1) * 128], identf[0:16, 0:16], [stgk] + CK, [('pp', b)])
                            CP('act', shin[:, cc, :], pp[b][:, 0:16], [('pp', b)], [shink])
                    mu0 = PVO['rwkv_mu']
                    for gi, (c0, ncol) in enumerate([(1552, 512), (2064, 512), (2576, 512), (3088, 256)]):
                        wi = load_w('w_in', l, c0, 8, ncol)
                        for j in range(ncol // 128):
                            cc = gi * 4 + j
                            pbuf, pbk = pbufs[cc % 2]
                            dd, ddk = dds[cc % 2]
                            b = nextpp()
                            proj(wi, 8, j * 128, 128, hb, HK, n, b)
                            CP('act', pbuf[:, 0, 1:n + 1], pp[b][:, 0:n], [('pp', b)], [pbk])
                            CP('dve', pbuf[:, 0, 0:1], shprev[:, cc, :], ['shprev'], [pbk])
                            TT('dve', dd[:, 0, 0:npr], pbuf[:, 0, 0:npr], pbuf[:, 0, 1:npr + 1], SUB, [pbk], [ddk])
                            if last:
                                TT('dve', dd[:, 0, 272:288], shin[:, cc, :], pbuf[:, 0, 273:289], SUB, [pbk, shink], [ddk])
                                CP('pool', pso[:, cc, :], pbuf[:, 0, 273:289], [pbk], [psok])
                            STT(xs[:, cc, 0:n], dd[:, 0, 0:n], pvt[:, mu0 + cc:mu0 + cc + 1], pbuf[:, 0, 1:n + 1], M, A, [ddk, pbk, 'pvt'], [xsk])
                            CP('pool', shprev[:, cc, :], pbuf[:, 0, npr:npr + 1], [pbk], ['shprev'])
                    if last:
                        b = nextpp()
                        TR(pp[b][0:14, 0:128], shprev[:, :, 0], identf, ['shprev'] + CK, [('pp', b)])
                        CP('act', stg[0:14, 0, 0:128], pp[b][0:14, 0:128], [('pp', b)], [stgk])
                        S.dma('sp', shp[l], stg[0:14, 0, 0:128], reads=[stgk], is_output=True)
                        for cc in range(14):
                            b = nextpp()
                            TR(pp[b][0:16, 0:128], pso[:, cc, :], identf, [psok] + CK, [('pp', b)])
                            CP('act', stg[0:16, 0, cc * 128:(cc + 1) * 128], pp[b][0:16, 0:128], [('pp', b)], [stgk])
                        S.dma('sp', shs[l], stg[0:16, 0, :], reads=[stgk], is_output=True)
                    tzw, tzk = BA(1, NTOKMAX)
                    zab, zak = BA(1, NTOKMAX)
                    szg, szk = BA(1, NTOKMAX)
                    ACT(tzw[0:64, 0, 0:n], xs[0:64, 12, 0:n], AF.Tanh, [xsk], [tzk])
                    CP('dve', zab[64:128, 0, 0:n], xs[64:128, 12, 0:n], [xsk], [zak])
                    ACT(szg[:, 0, 0:n], xs[:, 13, 0:n], AF.Sigmoid, [xsk], [szk])
                    lw, lwk = FA(4, NTOKMAX)
                    kk, kkk = FA(4, NTOKMAX)
                    bb, bbk = FA(4, NTOKMAX)
                    gq, gqk = FA(4, NTOKMAX)
                    bon, bonk = FA(4, NTOKMAX)
                    aa, aak = FA(4, NTOKMAX)
                    t1, t1k = FA(4, NTOKMAX)
                    t2, t2k = FA(4, NTOKMAX)
                    rbs = [(pp[0], ('pp', 0)), (pp[1], ('pp', 1)), (pc[0], ('pc', 0)), (pc[1], ('pc', 1))]
                    rbn = [(pn, 'pn'), (py, 'py'), (pst, 'pst'), (pc[0], ('pc', 0))]
                    C4 = range(4)
                    csl = lambda c: slice(c * 128, (c + 1) * 128)
                    pcol = lambda nm, c: pvt[:, PVO[nm] + c:PVO[nm] + c + 1]
                    for c in C4:
                        t_, k_ = rbs[c]
                        MM(t_[:, 0:n], w2b[0:64, csl(c)], tzw[0:64, 0, 0:n], ['w2b', tzk], [k_])
                        ACT(lw[:, c, 0:n], t_[:, 0:n], AF.Sigmoid, [k_, 'pvt'], [(lwk, c)], bias=pcol('rwkv_w0', c))
                    for c in C4:
                        t_, k_ = rbs[c]
                        MM(t_[:, 0:n], a2b[64:128, csl(c)], zab[64:128, 0, 0:n], ['a2b', zak], [k_])
                        ACT(aa[:, c, 0:n], t_[:, 0:n], AF.Sigmoid, [k_, 'pvt'], [(aak, c)], bias=pcol('rwkv_a0', c))
                    for c in C4:
                        t_, k_ = rbs[c]
                        MM(t_[:, 0:n], g2b[:, csl(c)], szg[:, 0, 0:n], ['g2b', szk], [k_])
                        CP('act', gq[:, c, 0:n], t_[:, 0:n], [k_], [gqk])
                    for c in C4:
                        TS('pool', lw[:, c, 0:n], lw[:, c, 0:n], -0.6065306597126334, M, [(lwk, c)], [(lwk, c)])
                        TS('dve', kk[:, c, 0:n], xs[:, 4 + c, 0:n], pcol('rwkv_k_k', c), M, [xsk, 'pvt'], [(kkk, c)])
                    for c in C4:
                        ACT(t1[:, c, 0:n], kk[:, c, 0:n], AF.Square, [(kkk, c)], [(t1k, c)])
                    for c in C4:
                        t_, k_ = rbn[c]
                        MM(t_[:, 0:n], bonesf, t1[:, c, 0:n], [(t1k, c)] + CK, [k_])
                        ACT(t1[:, c, 0:n], t_[:, 0:n], AF.Sqrt, [k_], [(t1k, c)], bias=1e-12)
                    for c in C4:
                        RECIP(t1[:, c, 0:n], t1[:, c, 0:n], [(t1k, c)], [(t1k, c)])
                        TS('dve', t2[:, c, 0:n], aa[:, c, 0:n], pcol('rwkv_k_a', c), M, [(aak, c)] + PK, [(t2k, c)], s2=pvd[:, 2 + c:3 + c], op1=A)
                    for c in C4:
                        TT('pool', kk[:, c, 0:n], kk[:, c, 0:n], t1[:, c, 0:n], M, [(kkk, c), (t1k, c)], [(kkk, c)])
                        TT('pool', xs[:, 4 + c, 0:n], xs[:, 4 + c, 0:n], t2[:, c, 0:n], M, [xsk, (t2k, c)], [xsk])
                    for c in C4:
                        TT('pool', bb[:, c, 0:n], kk[:, c, 0:n], aa[:, c, 0:n], M, [(kkk, c), (aak, c)], [bbk])
                        STT(t1[:, c, 0:n], xs[:, c, 0:n], pcol('rwkv_r_k', c), xs[:, 4 + c, 0:n], M, M, [xsk, (kkk, c), 'pvt'], [(t1k, c)])
                    for c in C4:
                        t_, k_ = rbn[c]
                        MM(t_[:, 0:n], bonesf, t1[:, c, 0:n], [(t1k, c)] + CK, [k_])
                        TT('dve', bon[:, c, 0:n], t_[:, 0:n], xs[:, 8 + c, 0:n], M, [k_, xsk], [bonk])
                    XK = [xsk]
                    LWK = [(lwk, c) for c in C4]
                    KKK = [(kkk, c) for c in C4]
                    Bc, Bk = FA(4, 128)
                    Ep, Epk = FA(4, 128)
                    Em, Emk = FA(4, 128)
                    Ex, Exk = FA(4, 128)
                    rt, rtk = BA(4, 128)
                    kt, ktk = BA(4, 128)
                    bt, btk = BA(4, 128)
                    at, atk = BA(4, 128)
                    vb, vbk = BA(4, 128)
                    Ktm, Ktk = BA(1, 512)
                    Btm, Btk = BA(1, 512)
                    Vtm, Vtk = BA(1, 512)
                    NH = 8
                    Qb = [[BA(1, 128) for _ in range(2)] for _ in range(NH)]
                    NX = [[BA(1, 192) for _ in range(2)] for _ in range(NH)]
                    Nb = [[(NX[i][a][0][:, :, 0:128], (NX[i][a][1], 'n')) for a in range(2)] for i in range(NH)]
                    Xb = [[(NX[i][a][0][:, :, 128:192], (NX[i][a][1], 'x')) for a in range(2)] for i in range(NH)]
                    LkT = [BA(1, 128) for _ in range(NH)]
                    ArkT = [BA(1, 128) for _ in range(NH)]
                    ArbT = [BA(1, 128) for _ in range(NH)]
                    Un = [BA(1, 64) for _ in range(NH)]
                    pcs = [(pc[0], ('pc', 0)), (pc[1], ('pc', 1)), (pp[0], ('pp', 0)), (pp[1], ('pp', 1)), (pn, 'pn')]
                    pci = [0]

                    def npc():
                        pci[0] = (pci[0] + 1) % len(pcs)
                        return pcs[pci[0]]
                    yfr, yfrk = FA(4, NTOKMAX)
                    tmpS, tmpk = FA(4, 64)
                    stRs = [FA(8, 64) for _ in range(2)]
                    for (off, L, smp) in tiles:
                        sp_ = 0 if smp is None else 1 + (smp % 2)
                        Sr_f, Sr_b, SrFk, SrBk = Sr_fs[sp_], Sr_bs[sp_], ('Sr_f', sp_), ('Sr_b', sp_)
                        stR, stRk = stRs[sp_ % 2]
                        if smp is not None:
                            S.dma('sp', stR[0:64, :, :], srw[l, smp].rearrange("h v k -> v h k"), writes=[stRk])
                            for c in range(4):
                                b = nextpp()
                                TR(pp[b][:, 0:64], stR[0:64, 2 * c:2 * c + 2, :].rearrange("p a k -> p (a k)"), identf[0:64, 0:64], [stRk] + CK, [('pp', b)])
                                CP('act', Sr_f[:, c, :], pp[b][:, 0:64], [('pp', b)], [SrFk])
                            CP('dve', Sr_b[:], Sr_f[:], [SrFk], [SrBk])
                        o = 0 if smp is None else smp
                        if smp is None or smp == 0:
                            W_ = L if smp is None else NS
                            if smp is None:
                                for c in range(4):
                                    SCAN(Bc[:, c, 0:L], onesf[:, 0:L], lw[:, c, off:off + L], 0.0, LWK + CK, [Bk])
                            else:
                                CP('dve', Bc[:, :, 0:W_], lw[:, :, off:off + W_], LWK, [Bk])
                            ACT(Ep[:, :, 0:W_], Bc[:, :, 0:W_], AF.Exp, [Bk], [Epk])
                            ACT(Em[:, :, 0:W_], Bc[:, :, 0:W_], AF.Exp, [Bk], [Emk], scale=-1.0)
                            TT('pool', Ex[:, :, 0:W_], Bc[:, :, 0:W_], lw[:, :, off:off + W_], SUB, [Bk] + LWK, [Exk])
                            ACT(Ex[:, :, 0:W_], Ex[:, :, 0:W_], AF.Exp, [Exk], [Exk])
                            TT('dve', rt[:, :, 0:W_], xs[:, 0:4, off:off + W_], Ep[:, :, 0:W_], M, [xsk, Epk], [rtk])
                            TT('pool', kt[:, :, 0:W_], xs[:, 4:8, off:off + W_], Em[:, :, 0:W_], M, XK + [Emk], [ktk])
                            TT('pool', bt[:, :, 0:W_], bb[:, :, off:off + W_], Em[:, :, 0:W_], M, [bbk, Emk], [btk])
                            TT('pool', at[:, :, 0:W_], kk[:, :, off:off + W_], Ex[:, :, 0:W_], M, KKK + [Exk], [atk])
                            CP('act', vb[:, :, 0:W_], xs[:, 8:12, off:off + W_], [xsk], [vbk])
                        for (src, sk_, dst, dk_) in [(kt, ktk, Ktm, Ktk), (bt, btk, Btm, Btk), (vb, vbk, Vtm, Vtk)]:
                            for c in range(4):
                                TR(ptb[0:L, c * 128:(c + 1) * 128], src[:, c, o:o + L], cstb[:], [sk_, 'cstb'], ['ptb'])
                            CP('act', dst[0:L, 0, :], ptb[0:L, 0:512], ['ptb'], [dk_])
                        nlev = {128: 7, 16: 4, 1: 0}[L]
                        for grp in range(8 // NH):
                            HD = []
                            for i in range(NH):
                                h = grp * NH + i
                                c, hp = h // 2, h % 2
                                HD.append((i, h, c, slice(64 * hp, 64 * hp + 64), slice(h * 64, (h + 1) * 64)))
                            if nlev > 0:
                                for (i, h, c, P, hs) in HD:
                                    t_, k_ = npc()
                                    MM(t_[0:L, 0:L], bt[P, c, o:o + L], at[P, c, o:o + L], [btk, atk], [k_])
                                    TT('dve', RR(Qb[i][0][0][0:L, 0, 0:L]), t_[0:L, 0:L], MUn[0:L, 0:L], M, [k_] + CK, [Qb[i][0][1]])
                                for (i, h, c, P, hs) in HD:
                                    t_, k_ = npc()
                                    MM(t_[0:L, 0:L], at[P, c, o:o + L], bt[P, c, o:o + L], [btk, atk], [k_])
                                    TT('dve', RR(Nb[i][0][0][0:L, 0, 0:L]), t_[0:L, 0:L], MLn[0:L, 0:L], M, [k_] + CK, [Nb[i][0][1]])
                                for (i, h, c, P, hs) in HD:
                                    t_, k_ = npc()
                                    MM(t_[0:L, 0:L], kt[P, c, o:o + L], at[P, c, o:o + L], [ktk, atk], [k_])
                                    TT('dve', LkT[i][0][0:L, 0, 0:L], t_[0:L, 0:L], MU[0:L, 0:L], M, [k_] + CK, [LkT[i][1]])
                            for (i, h, c, P, hs) in HD:
                                t_, k_ = npc()
                                MM(t_[0:L, 0:L], kt[P, c, o:o + L], rt[P, c, o:o + L], [ktk, rtk], [k_])
                                TT('dve', ArkT[i][0][0:L, 0, 0:L], t_[0:L, 0:L], MUI[0:L, 0:L], M, [k_] + CK, [ArkT[i][1]])
                            for (i, h, c, P, hs) in HD:
                                t_, k_ = npc()
                                MM(t_[0:L, 0:L], bt[P, c, o:o + L], rt[P, c, o:o + L], [btk, rtk], [k_])
                                TT('dve', ArbT[i][0][0:L, 0, 0:L], t_[0:L, 0:L], MUI[0:L, 0:L], M, [k_] + CK, [ArbT[i][1]])
                            for (i, h, c, P, hs) in HD:
                                t_, k_ = npc()
                                MM(t_[0:L, 0:64], at[P, c, o:o + L], Sr_b[P, c, :], [atk, SrBk], [k_], start=True, stop=(nlev == 0))
                                if nlev > 0:
                                    MM(t_[0:L, 0:64], LkT[i][0][0:L, 0, 0:L], Vtm[0:L, 0, hs], [LkT[i][1], Vtk], [k_], start=False, stop=True)
                                CP('act', RR(Xb[i][0][0][0:L, 0, :]), t_[0:L, 0:64], [k_], [Xb[i][0][1]])
                            for k in range(nlev):
                                a_, b_ = k % 2, (k + 1) % 2
                                merged = (L == 128 and k < nlev - 2)
                                for (i, h, c, P, hs) in HD:
                                    t_, k_ = npc()
                                    if merged:
                                        MM(t_[0:L, 0:192], Qb[i][a_][0][0:L, 0, 0:L], NX[i][a_][0][0:L, 0, 0:192], [Qb[i][a_][1], Nb[i][a_][1], Xb[i][a_][1]], [k_])
                                        TT('dve', Xb[i][b_][0][0:L, 0, :], Xb[i][a_][0][0:L, 0, :], t_[0:L, 128:192], A, [Xb[i][a_][1], k_], [Xb[i][b_][1]])
                                        CP('act', Nb[i][b_][0][0:L, 0, 0:L], t_[0:L, 0:L], [k_], [Nb[i][b_][1]])
                                    else:
                                        MM(t_[0:L, 0:64], Qb[i][a_][0][0:L, 0, 0:L], Xb[i][a_][0][0:L, 0, :], [Qb[i][a_][1], Xb[i][a_][1]], [k_])
                                        TT('dve', Xb[i][b_][0][0:L, 0, :], Xb[i][a_][0][0:L, 0, :], t_[0:L, 0:64], A, [Xb[i][a_][1], k_], [Xb[i][b_][1]])
                                if k < nlev - 1:
                                    for (i, h, c, P, hs) in HD:
                                        t_, k_ = npc()
                                        MM(t_[0:L, 0:L], Nb[i][a_][0][0:L, 0, 0:L], Qb[i][a_][0][0:L, 0, 0:L], [Nb[i][a_][1], Qb[i][a_][1]], [k_])
                                        CP('act' if i % 2 else 'dve', Qb[i][b_][0][0:L, 0, 0:L], t_[0:L, 0:L], [k_], [Qb[i][b_][1]])
                                if k < nlev - 2 and not merged:
                                    for (i, h, c, P, hs) in HD:
                                        t_, k_ = npc()
                                        MM(t_[0:L, 0:L], Qb[i][a_][0][0:L, 0, 0:L], Nb[i][a_][0][0:L, 0, 0:L], [Nb[i][a_][1], Qb[i][a_][1]], [k_])
                                        CP('act' if i % 2 else 'dve', Nb[i][b_][0][0:L, 0, 0:L], t_[0:L, 0:L], [k_], [Nb[i][b_][1]])
                            xf = nlev % 2
                            for (i, h, c, P, hs) in HD:
                                TS('dve', Un[i][0][0:L, 0, :], Xb[i][xf][0][0:L, 0, :], -1.0, M, [Xb[i][xf][1]], [Un[i][1]])
                            for (i, h, c, P, hs) in HD:
                                yo = py[P, c * 128:c * 128 + L]
                                MM(yo, Sr_b[P, c, :], rt[P, c, o:o + L], [SrBk, rtk], ['py'], start=True, stop=False)
                                MM(yo, Vtm[0:L, 0, hs], ArkT[i][0][0:L, 0, 0:L], [Vtk, ArkT[i][1]], ['py'], start=False, stop=False)
                                MM(yo, Un[i][0][0:L, 0, :], ArbT[i][0][0:L, 0, 0:L], [Un[i][1], ArbT[i][1]], ['py'], start=False, stop=True)
                                so = pst[P, c * 64:(c + 1) * 64]
                                MM(so, Ktm[0:L, 0, hs], Vtm[0:L, 0, hs], [Ktk, Vtk], ['pst'], start=True, stop=False)
                                MM(so, Btm[0:L, 0, hs], Un[i][0][0:L, 0, :], [Btk, Un[i][1]], ['pst'], start=False, stop=True)
                        CP('act', yfr[:, :, off:off + L], py[:, :].rearrange("p (c j) -> p c j", c=4)[:, :, 0:L], ['py'], [yfrk])
                        for c in range(4):
                            TT('dve', tmpS[:, c, :], Sr_f[:, c, :], pst[:, c * 64:(c + 1) * 64], A, [SrFk, 'pst'], [tmpk])
                            TS('dve', Sr_f[:, c, :], tmpS[:, c, :], Ep[:, c, o + L - 1:o + L], M, [tmpk, Epk], [SrFk])
                        CP('act', Sr_b[:], Sr_f[:], [SrFk], [SrBk])
                        if smp is not None or (last and off == 256):
                            for c in range(4):
                                b = nextpp()
                                TR(pp[b][0:64, 0:128], Sr_f[:, c, :], identf, [SrFk] + CK, [('pp', b)])
                                CP('act', stR[0:64, 2 * c:2 * c + 2, :].rearrange("p a k -> p (a k)"), pp[b][0:64, 0:128], [('pp', b)], [stRk])
                            dst = rws[l, smp] if smp is not None else rwp[l]
                            S.dma('sp', dst.rearrange("h v k -> v h k"), stR[0:64, :, :], reads=[stRk], is_output=True)
                    for c in C4:
                        t_, k_ = rbn[c]
                        MM(t_[:, 0:n], bonesf, yfr[:, c, 0:n], [yfrk] + CK, [k_])
                        STT(t1[:, c, 0:n], t_[:, 0:n], -1.0 / 64, yfr[:, c, 0:n], M, A, [k_, yfrk], [(t1k, c)])
                    for c in C4:
                        ACT(t2[:, c, 0:n], t1[:, c, 0:n], AF.Square, [(t1k, c)], [(t2k, c)])
                    for c in C4:
                        t_, k_ = rbn[c]
                        MM(t_[:, 0:n], bonesf, t2[:, c, 0:n], [(t2k, c)] + CK, [k_])
                        ACT(t2[:, c, 0:n], t_[:, 0:n], AF.Sqrt, [k_], [(t2k, c)], bias=64e-5, scale=1.0 / 64)
                    for c in C4:
                        RECIP(t2[:, c, 0:n], t2[:, c, 0:n], [(t2k, c)], [(t2k, c)])
                    for c in C4:
                        TT('pool', t1[:, c, 0:n], t1[:, c, 0:n], t2[:, c, 0:n], M, [(t1k, c), (t2k, c)], [(t1k, c)])
                    for c in C4:
                        TS('dve', t1[:, c, 0:n], t1[:, c, 0:n], pcol('rwkv_ln_w', c), M, [(t1k, c), 'pvt'], [(t1k, c)], s2=pcol('rwkv_ln_b', c), op1=A)
                    for c in C4:
                        TT('pool', t1[:, c, 0:n], t1[:, c, 0:n], bon[:, c, 0:n], A, [(t1k, c), bonk], [(t1k, c)])
                    for c in C4:
                        TT('dve', og[:, 1, c, 0:n], t1[:, c, 0:n], gq[:, c, 0:n], M, [(t1k, c), gqk], ['og1'])
                if 'S' in PH:
                    prefetch_w('w_in', l, 3344, 8, 512)
                    phase()
                    uf, ufk = FA(4, NTOKMAX)
                    ub, ubk = BA(4, NTOKMAX)
                    wi = load_w('w_in', l, 3344, 8, 512)
                    for j in range(4):
                        b = nextpp()
                        proj(wi, 8, j * 128, 128, hb, HK, n, b)
                        CP('act', uf[:, j, 0:n], pp[b][:, 0:n], [('pp', b)], [ufk])
                        CP('dve', ub[:, j, 0:n], pp[b][:, 0:n], [('pp', b)], [ubk])
                    tab, tbk = FA(64, 128)
                    S.dma('sp', tab, tabd.rearrange("p a s j -> p (a s) j"), reads=['tabd'], writes=[tbk])
                    tabv = tab.rearrange("p (a s) j -> p a s j", a=4)
                    bpb = []
                    for i, src in enumerate([bpr, bpi, cpr, cpi]):
                        t_, k_ = BA(16, 128)
                        S.dma('pool', t_, src[l], writes=[k_])
                        bpb.append((t_, k_))
                    TS('dve', bpb[3][0], bpb[3][0], -1.0, M, [bpb[3][1]], [bpb[3][1]])
                    ys5, ysk = FA(4, NTOKMAX)
                    NR5 = 8
                    R5 = [[FA(2, 128) for _ in range(5)] + [BA(2, 128)] for _ in range(NR5)]
                    pcs5 = [(pc[0], ('pc', 0)), (pc[1], ('pc', 1)), (pp[0], ('pp', 0)), (pp[1], ('pp', 1)), (pn, 'pn')]
                    stiles = [(0, 128, None), (128, 128, None)]
                    if last:
                        stiles += [(256, 16, None), (272, 16, 0)]
                        xs0, xs0k = FA(16 * 2, 16)
                        xs0v = xs0.rearrange("p (s a) n -> p s a n", a=2)
                        xso, xsok = FA(16 * 2, 16)
                        xsov = xso.rearrange("p (s a) n -> p s a n", a=2)
                        stg, stgk = FA(1, 2048)
                        for a_, src in enumerate([s5r, s5i]):
                            S.dma('sp', stg[0:16, 0, :], src[l], writes=[stgk])
                            for sc in range(16):
                                b = nextpp()
                                TR(pp[b][:, 0:16], stg[0:16, 0, sc * 128:(sc + 1) * 128], identf[0:16, 0:16], [stgk] + CK, [('pp', b)])
                                CP('act', xs0v[:, sc, a_, :], pp[b][:, 0:16], [('pp', b)], [xs0k])
                    def s5_chain(kidx, off, L, smp, uc):
                        SCS = []
                        for q in range(4):
                            sc = uc * 4 + q
                            SCS.append((q, sc, R5[(kidx % 2) * 4 + q], pcs5[sc % 5]))
                        for (q, sc, R_, (pct, pck)) in SCS:
                            for pl, wsel in enumerate([0, 1, 1, 0]):
                                MM(pct[:, pl * 128:pl * 128 + L], bpb[wsel][0][:, sc, :], ub[:, uc, off:off + L], [bpb[wsel][1], ubk], [pck], chain=True)
                        BU = {}
                        for (q, sc, R_, (pct, pck)) in SCS:
                            BU[sc] = (pct[:, 0:256].rearrange("p (a j) -> p a j", a=2)[:, :, 0:L],
                                      pct[:, 256:512].rearrange("p (a j) -> p a j", a=2)[:, :, 0:L])
                        yield
                        if smp is None:
                            for (q, sc, R_, (pct, pck)) in SCS:
                                (m1, m1k), (m2, m2k), (zz, zzk), (ZZ, ZZk), (xx, xxk), (xbf, xbk) = R_
                                Bu, Bus = BU[sc]
                                TT('dve', m1[:, :, 0:L], Bu, tabv[:, 2, sc, 0:L].unsqueeze(1).broadcast_to([128, 2, L]), M, [pck, tbk], [m1k])
                                TT('dve', m2[:, :, 0:L], Bus, tabv[:, 3, sc, 0:L].unsqueeze(1).broadcast_to([128, 2, L]), M, [pck, tbk], [m2k])
                            yield
                            for (q, sc, R_, (pct, pck)) in SCS:
                                (m1, m1k), (m2, m2k), (zz, zzk), (ZZ, ZZk), (xx, xxk), (xbf, xbk) = R_
                                TT('pool', zz[:, 0, 0:L], m1[:, 0, 0:L], m2[:, 0, 0:L], SUB, [m1k, m2k], [zzk])
                                TT('pool', zz[:, 1, 0:L], m1[:, 1, 0:L], m2[:, 1, 0:L], A, [m1k, m2k], [zzk])
                            yield
                            for (q, sc, R_, (pct, pck)) in SCS:
                                (m1, m1k), (m2, m2k), (zz, zzk), (ZZ, ZZk), (xx, xxk), (xbf, xbk) = R_
                                for a_ in range(2):
                                    SCAN(ZZ[:, a_, 0:L], onesf[:, 0:L], zz[:, a_, 0:L], x5[:, sc, a_:a_ + 1], [zzk, ('x5', sc)] + CK, [ZZk])
                            yield
                            for (q, sc, R_, (pct, pck)) in SCS:
                                (m1, m1k), (m2, m2k), (zz, zzk), (ZZ, ZZk), (xx, xxk), (xbf, xbk) = R_
                                TT('dve', m1[:, :, 0:L], ZZ[:, :, 0:L], tabv[:, 0, sc, 0:L].unsqueeze(1).broadcast_to([128, 2, L]), M, [ZZk, tbk], [m1k])
                                TT('pool', m2[:, 0, 0:L], ZZ[:, 1, 0:L], tabv[:, 1, sc, 0:L], M, [ZZk, tbk], [m2k])
                                TT('pool', m2[:, 1, 0:L], ZZ[:, 0, 0:L], tabv[:, 1, sc, 0:L], M, [ZZk, tbk], [m2k])
                            yield
                            for (q, sc, R_, (pct, pck)) in SCS:
                                (m1, m1k), (m2, m2k), (zz, zzk), (ZZ, ZZk), (xx, xxk), (xbf, xbk) = R_
                                TT('dve', xx[:, 0, 0:L], m1[:, 0, 0:L], m2[:, 0, 0:L], SUB, [m1k, m2k], [xxk])
                                TT('dve', xx[:, 1, 0:L], m1[:, 1, 0:L], m2[:, 1, 0:L], A, [m1k, m2k], [xxk])
                            yield
                            for (q, sc, R_, (pct, pck)) in SCS:
                                (m1, m1k), (m2, m2k), (zz, zzk), (ZZ, ZZk), (xx, xxk), (xbf, xbk) = R_
                                CP('pool', x5[:, sc, :], xx[:, :, L - 1], [xxk], [('x5', sc)])
                                CP('act', xbf[:, :, 0:L], xx[:, :, 0:L], [xxk], [xbk])
                            yield
                        else:
                            for (q, sc, R_, (pct, pck)) in SCS:
                                (m1, m1k), (m2, m2k), (zz, zzk), (ZZ, ZZk), (xx, xxk), (xbf, xbk) = R_
                                Bu, Bus = BU[sc]
                                PCK = [pck]
                                cr = s5c[:, 2, sc:sc + 1]
                                cim_ = s5c[:, 3, sc:sc + 1]
                                ar_ = s5c[:, 0, sc:sc + 1]
                                ai_ = s5c[:, 1, sc:sc + 1]
                                TS('dve', m1[:, :, 0:L], Bu, cr, M, PCK + ['s5c'], [m1k])
                                TS('dve', m2[:, :, 0:L], Bus, cim_, M, PCK + ['s5c'], [m2k])
                                TT('pool', zz[:, 0, 0:L], m1[:, 0, 0:L], m2[:, 0, 0:L], SUB, [m1k, m2k], [zzk])
                                TT('pool', zz[:, 1, 0:L], m1[:, 1, 0:L], m2[:, 1, 0:L], A, [m1k, m2k], [zzk])
                                STT(ZZ[:, :, 0:L], xs0v[:, sc, :, :], ar_, zz[:, :, 0:L], M, A, [xs0k, zzk, 's5c'], [ZZk])
                                TS('dve', m2[:, 0, 0:L], xs0v[:, sc, 1, :], ai_, M, [xs0k, 's5c'], [m2k])
                                TS('dve', m2[:, 1, 0:L], xs0v[:, sc, 0, :], ai_, M, [xs0k, 's5c'], [m2k])
                                TT('dve', xx[:, 0, 0:L], ZZ[:, 0, 0:L], m2[:, 0, 0:L], SUB, [ZZk, m2k], [xxk])
                                TT('dve', xx[:, 1, 0:L], ZZ[:, 1, 0:L], m2[:, 1, 0:L], A, [ZZk, m2k], [xxk])
                                CP('pool', xsov[:, sc, :, :], xx[:, :, 0:L], [xxk], [xsok])
                                CP('act', xbf[:, :, 0:L], xx[:, :, 0:L], [xxk], [xbk])
                            yield
                        for (q, sc, R_, (pct, pck)) in SCS:
                            (m1, m1k), (m2, m2k), (zz, zzk), (ZZ, ZZk), (xx, xxk), (xbf, xbk) = R_
                            MM(py[:, 0:L], bpb[2][0][:, sc, :], xbf[:, 0, 0:L], [bpb[2][1], xbk], ['py'], start=(q == 0), stop=False, chain=True)
                            MM(py[:, 0:L], bpb[3][0][:, sc, :], xbf[:, 1, 0:L], [bpb[3][1], xbk], ['py'], start=False, stop=(q == 3), chain=True)
                        CP('act', ys5[:, uc, off:off + L], py[:, 0:L], ['py'], [ysk])

                    def s5_store_prompt():
                        for a_, dst in enumerate([s5rp, s5ip]):
                            b = nextpp()
                            TR(pp[b][0:16, 0:128], x5[:, :, a_], identf, [('x5', s_) for s_ in range(16)] + CK, [('pp', b)])
                            CP('act', stg[0:16, 0, 0:128], pp[b][0:16, 0:128], [('pp', b)], [stgk])
                            S.dma('sp', dst[l], stg[0:16, 0, 0:128], reads=[stgk], is_output=True)
                        yield

                    pend = []
                    kidx = 0
                    for (off, L, smp) in stiles:
                        for uc in range(4):
                            pend.append(s5_chain(kidx, off, L, smp, uc))
                            kidx += 1
                        if last and off == 256:
                            pend.append('fence')
                            pend.append(s5_store_prompt())
                            pend.append('fence')
                    SKEW = 4
                    active = []
                    while pend or active:
                        while pend:
                            if pend[0] == 'fence':
                                if active:
                                    break
                                pend.pop(0)
                                continue
                            if not active or (len(active) < 2 and active[-1][1] >= SKEW):
                                active.append([pend.pop(0), 0])
                            else:
                                break
                        for ent in list(active):
                            try:
                                next(ent[0])
                                ent[1] += 1
                            except StopIteration:
                                active.remove(ent)
                    if last:
                        for a_, dst in enumerate([s5rs, s5is]):
                            for sc in range(16):
                                b = nextpp()
                                TR(pp[b][0:16, 0:128], xsov[:, sc, a_, :], identf, [xsok] + CK, [('pp', b)])
                                CP('act', stg[0:16, 0, sc * 128:(sc + 1) * 128], pp[b][0:16, 0:128], [('pp', b)], [stgk])
                            S.dma('sp', dst[l], stg[0:16, 0, :], reads=[stgk], is_output=True)
                    yv, yvk = FA(4, NTOKMAX)
                    t5, t5k = FA(1, NTOKMAX)
                    ygb, ygbk = BA(4, NTOKMAX)
                    for uc in range(4):
                        STT(yv[:, uc, 0:n], uf[:, uc, 0:n], pvt[:, PVO['s5_d'] + uc:PVO['s5_d'] + uc + 1], ys5[:, uc, 0:n], M, A, [ufk, ysk, 'pvt'], [yvk])
                        ACT(t5[:, 0, 0:n], yv[:, uc, 0:n], AF.Square, [yvk], [t5k])
                        TS('dve', t5[:, 0, 0:n], t5[:, 0, 0:n], 0.044715, M, [t5k], [t5k], s2=1.0, op1=A)
                        TT('dve', t5[:, 0, 0:n], t5[:, 0, 0:n], yv[:, uc, 0:n], M, [t5k, yvk], [t5k])
                        ACT(t5[:, 0, 0:n], t5[:, 0, 0:n], AF.Sigmoid, [t5k], [t5k], scale=1.5957691216057308)
                        TT('dve', yv[:, uc, 0:n], yv[:, uc, 0:n], t5[:, 0, 0:n], M, [t5k, yvk], [yvk])
                        CP('act', ygb[:, uc, 0:n], yv[:, uc, 0:n], [yvk], [ygbk])
                    for oc in range(4):
                        b = nextpp()
                        for kc in range(4):
                            MM(pp[b][:, 0:n], glub[:, kc, oc * 128:(oc + 1) * 128], ygb[:, kc, 0:n], ['glub', ygbk], [('pp', b)], start=(kc == 0), stop=(kc == 3), chain=True)
                        ACT(t5[:, 0, 0:n], pp[b][:, 0:n], AF.Sigmoid, [('pp', b), 'pvt'], [t5k], bias=pvt[:, PVO['s5_glu_b'] + oc:PVO['s5_glu_b'] + oc + 1])
                        TT('dve', og[:, 2, oc, 0:n], yv[:, oc, 0:n], t5[:, 0, 0:n], M, [yvk, t5k], ['og2'])
                if 'C' in PH:
                    prefetch_w('w_in', l, 3856, 8, 512)
                    prefetch_w('w_br_gla', l, 0, 4, 512)
                    phase()
                    mf, mfk = FA(8, NTOKMAX)
                    sg, sgk = FA(2, NTOKMAX)
                    brn = ['w_br_gla', 'w_br_rwkv', 'w_br_s5']
                    for br in range(3):
                        for half in range(2):
                            c0 = 3856 + br * 1024 + half * 512
                            wi = load_w('w_in', l, c0, 8, 512)
                            wj = load_w(brn[br], l, half * 512, 4, 512)
                            for j in range(4):
                                oc = half * 4 + j
                                b = nextpp()
                                proj(wi, 8, j * 128, 128, hb, HK, n, b)
                                ACT(sg[:, j % 2, 0:n], pp[b][:, 0:n], AF.Sigmoid, [('pp', b)], [(sgk, j % 2)])
                                b2 = nextpp()
                                proj(wj, 4, j * 128, 128, og[:, br], ['og%d' % br], n, b2)
                                if br == 0:
                                    TT('dve', mf[:, oc, 0:n], sg[:, j % 2, 0:n], pp[b2][:, 0:n], M, [(sgk, j % 2), ('pp', b2)], [(mfk, oc)])
                                else:
                                    TT('dve', sg[:, j % 2, 0:n], sg[:, j % 2, 0:n], pp[b2][:, 0:n], M, [(sgk, j % 2), ('pp', b2)], [(sgk, j % 2)])
                                    TT('pool', mf[:, oc, 0:n], mf[:, oc, 0:n], sg[:, j % 2, 0:n], A, [(sgk, j % 2), (mfk, oc)], [(mfk, oc)])
                    mb, mbk = BA(8, NTOKMAX)
                    for oc in range(8):
                        CP('act' if oc % 2 else 'dve', mb[:, oc, 0:n], mf[:, oc, 0:n], [(mfk, oc)], [mbk])
                    for half in range(2):
                        wi = load_w('w_out', l, half * 512, 8, 512)
                        for j in range(4):
                            oc = half * 4 + j
                            b = nextpp()
                            proj(wi, 8, j * 128, 128, mb, [mbk], n, b)
                            TT('dve', xb[:, oc, 0:n], xb[:, oc, 0:n], pp[b][:, 0:n], A, ['xb', ('pp', b)], ['xb'])
                    prefetch_w('ffn_w1', l, 0, 8, 512)
                    prefetch_w('ffn_w3', l, 0, 8, 512)
                    phase()
                    rs, rk = rmsnorm(PVO['norm_ffn'], n)
                    for c in range(8):
                        STT(hb[:, c, 0:n], xb[:, c, 0:n], pvt[:, PVO['norm_ffn'] + c:PVO['norm_ffn'] + c + 1], rs[:, 0, 0:n], M, M,
                            ['xb', 'pvt', rk], ['hb'])
                    hid, hidk = BA(22, NTOKMAX)
                    sl, slk = FA(2, NTOKMAX)
                    for g in range(6):
                        c0 = g * 512
                        ncol = min(512, DFF - c0)
                        wi = load_w('ffn_w1', l, c0, 8, ncol)
                        wj = load_w('ffn_w3', l, c0, 8, ncol)
                        for j in range(ncol // 128):
                            hc = g * 4 + j
                            b = nextpp()
                            proj(wi, 8, j * 128, 128, hb, HK, n, b)
                            ACT(sl[:, j % 2, 0:n], pp[b][:, 0:n], AF.Silu, [('pp', b)], [(slk, j % 2)])
                            b2 = nextpp()
                            proj(wj, 8, j * 128, 128, hb, HK, n, b2)
                            TT('dve', hid[:, hc, 0:n], sl[:, j % 2, 0:n], pp[b2][:, 0:n], M, [(slk, j % 2), ('pp', b2)], [hidk])
                    for ocp in range(4):
                        halves = []
                        for kh in range(2):
                            i = wt['i']
                            wt['i'] = (i + 1) % NWB
                            w2v = wbuf[i][:, :, :].rearrange("p a b -> p (a b)")[:, 0:11 * 256].rearrange("p (k m) -> p k m", k=11)
                            S.dma('sp', w2v, wb['ffn_w2'][l, kh * 1408:(kh + 1) * 1408, ocp * 256:(ocp + 1) * 256].rearrange("(kc p) n -> p kc n", p=128),
                                  reads=WK('ffn_w2', l), writes=[('wbuf', i)])
                            halves.append((w2v, i))
                        for j in range(2):
                            oc = ocp * 2 + j
                            b = nextpp()
                            for kc in range(22):
                                w2v, i = halves[kc // 11]
                                MM(pp[b][:, 0:n], w2v[:, kc % 11, j * 128:(j + 1) * 128], hid[:, kc, 0:n], [('wbuf', i), hidk], [('pp', b)],
                                   start=(kc == 0), stop=(kc == 21), chain=True)
                            TT('dve', xb[:, oc, 0:n], xb[:, oc, 0:n], pp[b][:, 0:n], A, ['xb', ('pp', b)], ['xb'])
                    S.dma('sp', xres[:, :, t0:t0 + n], xb[:, :, 0:n], reads=['xb'], writes=['xres'])
                    if l == depth - 1:
                        rs, rk = rmsnorm(PVO['final_norm'], n)
                        yf, yfk = FA(8, NTOKMAX)
                        for c in range(8):
                            STT(yf[:, c, 0:n], xb[:, c, 0:n], pvt[:, PVO['final_norm'] + c:PVO['final_norm'] + c + 1], rs[:, 0, 0:n], M, M,
                                ['xb', 'pvt', rk], [yfk])
                        yt, ytk = FA(2, 1024)
                        otl = [(0, 128), (128, 128)] + ([(256, 32)] if last else [])
                        for ti, (off, L) in enumerate(otl):
                            for c in range(8):
                                b = nextpp()
                                TR(pp[b][0:L, 0:128], yf[:, c, off:off + L], identf, [yfk] + CK, [('pp', b)])
                                CP('act' if c % 2 else 'dve', yt[0:L, ti % 2, c * 128:(c + 1) * 128], pp[b][0:L, 0:128], [('pp', b)], [(ytk, ti % 2)])
                            g0 = t0 + off
                            lo, hi = max(g0, 16), min(g0 + L, NPR)
                            if hi > lo:
                                S.dma('sp', yp[lo - 16:hi - 16, :], yt[lo - g0:hi - g0, ti % 2, :], reads=[(ytk, ti % 2)], is_output=True)
                            lo2 = max(g0, NPR)
                            if g0 + L > lo2:
                                S.dma('sp', ys[lo2 - NPR:g0 + L - NPR, :], yt[lo2 - g0:L, ti % 2, :], reads=[(ytk, ti % 2)], is_output=True)
        S.finish(block)
        print("n_instr", S.n_instr, {e: len(v) for e, v in S.q.items()})
    return nc


def _fm(v, nch):
    return np.ascontiguousarray(np.asarray(v, np.float32).reshape(nch, 128).T)


def _consts():
    p = np.arange(128)[:, None]
    f = np.arange(128)[None, :]
    mu = (p < f).astype(np.float32)
    mui = (p <= f).astype(np.float32)
    mln = -(p > f).astype(np.float32)
    mun = -(p < f).astype(np.float32)
    ident = np.eye(128, dtype=np.float32)
    ones = np.ones((128, 128), np.float32)
    bones = np.zeros((128, 128), np.float32)
    bones[:64, :64] = 1
    bones[64:, 64:] = 1
    return np.ascontiguousarray(np.concatenate([mu, mui, mln, mun, ident, ones, bones], axis=1))


def _prep_shared(inp):
    L = 4
    pvs = np.zeros((L, 128, NV), np.float32)
    for l in range(L):
        def put(name, arr, nch):
            pvs[l, :, PVO[name]:PVO[name] + nch] = _fm(arr, nch)
        put('norm_mix', inp['norm_mix'][l], 8)
        put('norm_ffn', inp['norm_ffn'][l], 8)
        put('gla_gate_b', inp['gla_gate_b'][l], 2)
        put('gla_norm', inp['gla_norm'][l], 4)
        put('rwkv_mu', inp['rwkv_mu'][l], 14)
        put('rwkv_w0', inp['rwkv_w0'][l], 4)
        put('rwkv_a0', inp['rwkv_a0'][l], 4)
        put('rwkv_k_k', inp['rwkv_k_k'][l], 4)
        put('rwkv_k_a', inp['rwkv_k_a'][l], 4)
        put('rwkv_r_k', np.asarray(inp['rwkv_r_k'][l]).reshape(-1), 4)
        put('rwkv_ln_w', inp['rwkv_ln_w'][l], 4)
        put('rwkv_ln_b', inp['rwkv_ln_b'][l], 4)
        put('s5_d', inp['s5_d'][l], 4)
        put('s5_glu_b', inp['s5_glu_b'][l], 4)
        put('s5_are', np.asarray(inp['s5_a_re'][l]).reshape(-1), 16)
        put('s5_aim', np.asarray(inp['s5_a_im'][l]).reshape(-1), 16)
        put('s5_ldt', np.repeat(np.asarray(inp['s5_log_dt'][l]), 64), 16)
        put('final_norm', inp['final_norm'], 8)
    def bpad(B):
        B = np.asarray(B, np.float32)
        out = np.zeros((L, 128, 16, 128), np.float32)
        for sc in range(16):
            uc, q = sc // 4, sc % 4
            for g2 in range(2):
                g = 2 * sc + g2
                gl = 2 * q + g2
                out[:, gl * 16:(gl + 1) * 16, sc, g2 * 64:(g2 + 1) * 64] = np.transpose(B[:, g], (0, 2, 1))
        return out
    def cpad(C):
        C = np.asarray(C, np.float32)
        out = np.zeros((L, 128, 16, 128), np.float32)
        for sc in range(16):
            q = sc % 4
            for g2 in range(2):
                g = 2 * sc + g2
                gl = 2 * q + g2
                out[:, g2 * 64:(g2 + 1) * 64, sc, gl * 16:(gl + 1) * 16] = np.transpose(C[:, g], (0, 2, 1))
        return out
    sh = {
        'pv': pvs, 'consts': _consts(),
        'bpad_re': bpad(inp['s5_b_re']), 'bpad_im': bpad(inp['s5_b_im']),
        'cpad_re': cpad(inp['s5_c_re']), 'cpad_im': cpad(inp['s5_c_im']),
    }
    for k in ['gla_gate_w2', 'rwkv_w2', 'rwkv_a2', 'rwkv_g2', 's5_glu_w'] + [n for n, _, _ in BIGW]:
        sh[k] = np.ascontiguousarray(np.asarray(inp[k], np.float32))
    return sh


def kernel(_depth=4, _nblk=None, _cores=8, **inp):
    sh = _prep_shared(inp)
    xp = np.asarray(inp['x_prompt'], np.float32)
    xs = np.asarray(inp['x_sample'], np.float32)
    meta = np.asarray(inp['meta_tokens'], np.float32)
    in_maps = []
    for c in range(_cores):
        sl = slice(c * NS, (c + 1) * NS)
        m = dict(sh)
        m['xin'] = np.ascontiguousarray(np.concatenate([meta, xp[c], xs[sl, 0]], axis=0))
        m['sgla'] = np.ascontiguousarray(np.asarray(inp['state_gla'], np.float32)[:, sl])
        m['srw'] = np.ascontiguousarray(np.asarray(inp['state_rwkv'], np.float32)[:, sl])
        m['ssh'] = np.ascontiguousarray(np.asarray(inp['state_rwkv_shift'], np.float32)[:, sl])
        m['s5r'] = np.ascontiguousarray(np.asarray(inp['state_s5_re'], np.float32)[:, sl].reshape(4, NS, 2048))
        m['s5i'] = np.ascontiguousarray(np.asarray(inp['state_s5_im'], np.float32)[:, sl].reshape(4, NS, 2048))
        in_maps.append(m)
    nc = build_nc(_depth, _nblk)
    res = run_bass_kernel_spmd(nc, in_maps, core_ids=list(range(_cores)))
    R = res.results
    cat = lambda k, ax: np.concatenate([np.asarray(r[k], np.float32) for r in R], axis=ax)
    stk = lambda k: np.stack([np.asarray(r[k], np.float32) for r in R], axis=1)
    y_prompt = np.stack([np.asarray(r['yp'], np.float32) for r in R], axis=0)
    y_sample = cat('ys', 0).reshape(-1, 1, D)
    gla_p = stk('glap')
    rwkv_p = stk('rwp')
    shift_p = stk('shp').reshape(4, -1, 1792)
    s5re_p = stk('s5rp').reshape(4, -1, 32, 64)
    s5im_p = stk('s5ip').reshape(4, -1, 32, 64)
    gla_s = cat('glas', 1)
    rwkv_s = cat('rws', 1)
    shift_s = cat('shs', 1)
    s5re_s = cat('s5rs', 1).reshape(4, -1, 32, 64)
    s5im_s = cat('s5is', 1).reshape(4, -1, 32, 64)
    return (y_prompt, y_sample, gla_p, rwkv_p, shift_p, s5re_p, s5im_p, gla_s, rwkv_s, shift_s, s5re_s, s5im_s)
```
